# Optimizing a Trainium2 kernel written in Bass

```python
import math
import jax, jax.numpy as jnp
from jax import lax
import numpy as np

D_MODEL = 1024
BATCH = 4
SEQ = 8192
DEPTH = 1

CHUNK = 64
MIX_WIDTH = D_MODEL
LRU_WIDTH = MIX_WIDTH // 2
ATT_WIDTH = MIX_WIDTH - LRU_WIDTH
LRU_BLOCKS = 8
LRU_BLOCK_W = LRU_WIDTH // LRU_BLOCKS
CONV_W = 4
LRU_C = 8.0
ATT_HEADS = 8
ATT_HEAD_DIM = ATT_WIDTH // ATT_HEADS
Q_BLOCK = 128
D_FF = -(-8 * D_MODEL // (3 * 256)) * 256
IN_COLS = 2 * LRU_WIDTH + 3 * ATT_WIDTH
EPS = 1e-6

kernel_name = "hymba_rglru_stickbreaking_block"


def rms_norm(x, g):
    xf = x.astype(jnp.float32)
    xf = xf * lax.rsqrt(jnp.mean(xf * xf, axis=-1, keepdims=True) + EPS)
    return xf.astype(x.dtype) * g


def causal_depthwise_conv(x, w, b):
    s = x.shape[1]
    xp = jnp.pad(x, ((0, 0), (CONV_W - 1, 0), (0, 0)))
    y = b
    for i in range(CONV_W):
        y = y + xp[:, i:i + s, :] * w[i]
    return y


def block_diag_linear(x, w, b):
    bsz, s, _ = x.shape
    xb = x.reshape(bsz, s, LRU_BLOCKS, LRU_BLOCK_W)
    y = jnp.einsum('bsnc,ncd->bsnd', xb, w) + b
    return y.reshape(bsz, s, LRU_WIDTH)


def rg_lru(x, w_rg, b_rg, w_ig, b_ig, lam):
    r = jax.nn.sigmoid(block_diag_linear(x, w_rg, b_rg).astype(jnp.float32))
    i = jax.nn.sigmoid(block_diag_linear(x, w_ig, b_ig).astype(jnp.float32))
    log_a = -LRU_C * r * jax.nn.softplus(-lam.astype(jnp.float32))
    a = jnp.exp(log_a)
    mult = jnp.sqrt(-jnp.expm1(2.0 * log_a))
    bterm = mult * (i * x.astype(jnp.float32))

    def combine(e1, e2):
        a1, b1 = e1
        a2, b2 = e2
        return a1 * a2, a2 * b1 + b2

    _, h = lax.associative_scan(combine, (a, bterm), axis=1)
    return h.astype(x.dtype)


def stick_breaking_attention(q, k, v):
    bsz, s, h, dh = q.shape
    nb = s // Q_BLOCK
    scale = 1.0 / math.sqrt(dh)
    qb = q.reshape(bsz, nb, Q_BLOCK, h, dh).transpose(1, 0, 2, 3, 4)
    key_pos = jnp.arange(s)

    def one_block(args):
        q_blk, blk = args
        z = jnp.einsum('bqhd,bkhd->bhqk', q_blk, k).astype(jnp.float32) * scale
        t = blk * Q_BLOCK + jnp.arange(Q_BLOCK)
        mask = key_pos[None, :] < t[:, None]
        log_beta = jax.nn.log_sigmoid(z)
        log_1mb = jnp.where(mask, jax.nn.log_sigmoid(-z), 0.0)
        suffix = lax.cumsum(log_1mb, axis=3, reverse=True) - log_1mb
        att = jnp.where(mask, jnp.exp(log_beta + suffix), 0.0)
        return jnp.einsum('bhqk,bkhd->bqhd', att.astype(v.dtype), v)

    out = lax.map(one_block, (qb, jnp.arange(nb)))
    return out.transpose(1, 0, 2, 3, 4).reshape(bsz, s, h, dh)


def setup_inputs(seed: int = 0) -> dict:
    key = jax.random.key(seed)
    ks = jax.random.split(key, 20)
    f32 = jnp.float32
    nrm = lambda k, shp, sc: jax.random.normal(k, shp, f32) * sc
    gain = lambda k, shp: 1.0 + 0.05 * jax.random.normal(k, shp, f32)
    x = jax.random.normal(ks[0], (BATCH, SEQ, D_MODEL), f32)
    u = jax.random.uniform(ks[9], (DEPTH, LRU_WIDTH), f32, 0.9, 0.999)
    a0 = u ** (1.0 / LRU_C)
    lru_lambda = jnp.log(a0) - jnp.log1p(-a0)
    return {
        "x": x,
        "norm_mix": gain(ks[1], (DEPTH, D_MODEL)),
        "w_in": nrm(ks[2], (DEPTH, D_MODEL, IN_COLS), D_MODEL ** -0.5),
        "conv_w": nrm(ks[3], (DEPTH, CONV_W, LRU_WIDTH), CONV_W ** -0.5),
        "conv_b": nrm(ks[4], (DEPTH, LRU_WIDTH), 0.01),
        "w_rg": nrm(ks[5], (DEPTH, LRU_BLOCKS, LRU_BLOCK_W, LRU_BLOCK_W), LRU_BLOCK_W ** -0.5),
        "b_rg": nrm(ks[6], (DEPTH, LRU_BLOCKS, LRU_BLOCK_W), 0.01),
        "w_ig": nrm(ks[7], (DEPTH, LRU_BLOCKS, LRU_BLOCK_W, LRU_BLOCK_W), LRU_BLOCK_W ** -0.5),
        "b_ig": nrm(ks[8], (DEPTH, LRU_BLOCKS, LRU_BLOCK_W), 0.01),
        "lru_lambda": lru_lambda,
        "norm_lru_out": gain(ks[10], (DEPTH, LRU_WIDTH)),
        "norm_att_out": gain(ks[11], (DEPTH, ATT_WIDTH)),
        "w_out": nrm(ks[12], (DEPTH, MIX_WIDTH, D_MODEL), MIX_WIDTH ** -0.5),
        "norm_ffn": gain(ks[13], (DEPTH, D_MODEL)),
        "w_ffn_in": nrm(ks[14], (DEPTH, D_MODEL, 2 * D_FF), D_MODEL ** -0.5),
        "w_ffn_out": nrm(ks[15], (DEPTH, D_FF, D_MODEL), D_FF ** -0.5),
        "norm_final": gain(ks[16], (D_MODEL,)),
    }


def reference(x, norm_mix, w_in, conv_w, conv_b, w_rg, b_rg, w_ig, b_ig, lru_lambda,
              norm_lru_out, norm_att_out, w_out, norm_ffn, w_ffn_in, w_ffn_out, norm_final):
    bsz, s, _ = x.shape
    for l in range(DEPTH):
        h = rms_norm(x, norm_mix[l])
        u = jnp.einsum('bsd,de->bse', h, w_in[l])
        o = 0
        lru_x = u[..., o:o + LRU_WIDTH]; o += LRU_WIDTH
        lru_g = u[..., o:o + LRU_WIDTH]; o += LRU_WIDTH
        q = u[..., o:o + ATT_WIDTH].reshape(bsz, s, ATT_HEADS, ATT_HEAD_DIM); o += ATT_WIDTH
        k = u[..., o:o + ATT_WIDTH].reshape(bsz, s, ATT_HEADS, ATT_HEAD_DIM); o += ATT_WIDTH
        v = u[..., o:o + ATT_WIDTH].reshape(bsz, s, ATT_HEADS, ATT_HEAD_DIM)

        xc = causal_depthwise_conv(lru_x, conv_w[l], conv_b[l])
        hl = rg_lru(xc, w_rg[l], b_rg[l], w_ig[l], b_ig[l], lru_lambda[l])
        y_lru = jax.nn.gelu(lru_g) * hl

        y_att = stick_breaking_attention(q, k, v).reshape(bsz, s, ATT_WIDTH)

        y = jnp.concatenate([rms_norm(y_lru, norm_lru_out[l]),
                             rms_norm(y_att, norm_att_out[l])], axis=-1)
        x = x + jnp.einsum('bse,ed->bsd', y, w_out[l])

        h2 = rms_norm(x, norm_ffn[l])
        gu = jnp.einsum('bsd,df->bsf', h2, w_ffn_in[l])
        x = x + jnp.einsum('bsf,fd->bsd', jax.nn.silu(gu[..., :D_FF]) * gu[..., D_FF:], w_ffn_out[l])
    return rms_norm(x, norm_final)
```

```python
import numpy as np
import ml_dtypes
from contextlib import ExitStack
import concourse.bass as bass
import concourse.mybir as mybir
from concourse.bass_utils import run_bass_kernel_spmd

F32 = mybir.dt.float32
BF16 = mybir.dt.bfloat16
AF = mybir.ActivationFunctionType
ALU = mybir.AluOpType

D = 1024
SEQ = 8192
NG = 16
NSLOT = 8
DFF = 2816
NFC = 22
EPS = 1e-6
NEG = -30000.0
NV = 64
N_CORES = 8
TG = 256
NT3 = TG // 128


class _Op:
    __slots__ = ("eng", "fn", "deps", "key", "sig", "signal")


class Prog:
    def __init__(self):
        self.ops = []
        self.lastw = {}
        self.rd = {}
        self.last_eng = {}
        self.last_dma = {}

    def add(self, eng, fn, r=(), w=(), dma=None):
        if getattr(self, "cap", None) is not None:
            self.cap.append((eng, fn, tuple(r), tuple(w), dma))
            return -1
        i = len(self.ops)
        xr = [k for k in r if isinstance(k, tuple) and k[0] == "pb"]
        if xr:
            w = list(w) + xr
        deps = set()
        for k in r:
            j = self.lastw.get(k)
            if j is not None:
                deps.add(j)
        for k in w:
            j = self.lastw.get(k)
            if j is not None:
                deps.add(j)
            rk = self.rd.get(k)
            if rk:
                deps.update(rk.values())
        for k in r:
            rk = self.rd.setdefault(k, {})
            rk[("dma", i) if dma is not None else eng] = i
        for k in w:
            self.lastw[k] = i
            self.rd[k] = {}
        op = _Op()
        op.eng = eng
        op.fn = fn
        op.deps = deps
        op.key = dma
        op.sig = None
        op.signal = dma is not None
        self.ops.append(op)
        if dma is not None:
            self.last_dma[dma] = i
        else:
            self.last_eng[eng] = i
        return i

    def barrier(self, engines=("pe", "act", "dve", "pool", "sp")):
        deps = set(self.last_eng.values()) | set(self.last_dma.values())
        for e in engines:
            op = _Op()
            op.eng = e
            op.fn = None
            op.deps = set(deps)
            op.key = None
            op.sig = None
            op.signal = False
            self.ops.append(op)

    def emit(self, nc, es, pre_sp=None):
        ops = self.ops
        for op in ops:
            for d in op.deps:
                D_ = ops[d]
                if D_.key is None and D_.fn is not None:
                    if D_.eng == "pe" and op.eng == "pe" and op.key is None:
                        continue
                    D_.signal = True
        cnt = {}
        for op in ops:
            if op.fn is None:
                continue
            if op.key is not None:
                k = ("dma", op.key)
                cnt[k] = cnt.get(k, 0) + 16
                op.sig = cnt[k]
            elif op.signal:
                cnt[op.eng] = cnt.get(op.eng, 0) + 1
                op.sig = cnt[op.eng]
        self.counts = cnt
        sems = {}
        for n_, k in enumerate(cnt.keys()):
            sems[k] = es.enter_context(nc.semaphore("s%d" % n_))
        block = es.enter_context(nc.Block())
        stats = {}

        def run(engname, e):
            waited = {}
            nw = 0
            ni = 0
            for op in ops:
                if op.eng != engname:
                    continue
                need = {}
                for d in op.deps:
                    D_ = ops[d]
                    if D_.fn is None:
                        continue
                    if D_.key is None:
                        if D_.eng == "pe" and engname == "pe" and op.key is None:
                            continue
                        k = D_.eng
                    else:
                        k = ("dma", D_.key)
                    if D_.sig > need.get(k, 0):
                        need[k] = D_.sig
                for k, v in need.items():
                    if waited.get(k, 0) >= v:
                        continue
                    e.wait_ge(sems[k], v)
                    waited[k] = v
                    nw += 1
                if op.fn is not None:
                    ins = op.fn(e)
                    ni += 1
                    if op.key is not None:
                        ins.then_inc(sems[("dma", op.key)], 16)
                    elif op.signal:
                        ins.then_inc(sems[op.eng], 1)
            stats[engname] = (ni, nw)

        @block.tensor
        def _(t):
            run("pe", t)

        @block.scalar
        def _(a):
            run("act", a)

        @block.vector
        def _(v):
            run("dve", v)

        @block.gpsimd
        def _(g):
            run("pool", g)

        @block.sync
        def _(s):
            if pre_sp is not None:
                pre_sp(s)
            run("sp", s)

        self.stats = stats


class Arena:
    def __init__(self, h32):
        self.h32 = h32
        self.h16 = h32.bitcast(BF16)
        self.top = 0
        self.cap = h32.shape[1] * 4

    def alloc(self, free, dtype):
        n = 1
        for f in free:
            n *= f
        sz = 4 if dtype == F32 else 2
        off = self.top
        nb = (n * sz + 63) // 64 * 64
        self.top += nb
        assert self.top <= self.cap, ("SBUF arena overflow", self.top, self.cap)
        if dtype == F32:
            ap = self.h32[:, off // 4: off // 4 + n]
        else:
            ap = self.h16[:, off // 2: off // 2 + n]
        if len(free) == 2:
            ap = ap.rearrange("p (a b) -> p a b", a=free[0])
        elif len(free) == 3:
            ap = ap.rearrange("p (a b c) -> p a b c", a=free[0], b=free[1])
        return ap


def ACT(out, in_, func, **kw):
    return lambda e: e.activation(out=out, in_=in_, func=func, **kw)


def MM(out, lhsT, rhs, start, stop):
    return lambda e: e.matmul(out, lhsT=lhsT, rhs=rhs, start=start, stop=stop)


def TR(out, in_, ident):
    return lambda e: e.transpose(out=out, in_=in_, identity=ident)


def TT(out, in0, in1, op):
    return lambda e: e.tensor_tensor(out=out, in0=in0, in1=in1, op=op)


def TS(out, in0, s1, s2, op0, op1=None):
    if op1 is None:
        return lambda e: e.tensor_scalar(out=out, in0=in0, scalar1=s1, scalar2=None, op0=op0)
    return lambda e: e.tensor_scalar(out=out, in0=in0, scalar1=s1, scalar2=s2, op0=op0, op1=op1)


def STT(out, in0, scalar, in1, op0, op1):
    return lambda e: e.scalar_tensor_tensor(out=out, in0=in0, scalar=scalar, in1=in1, op0=op0, op1=op1)


def CP(out, in_):
    return lambda e: e.tensor_copy(out=out, in_=in_)


def MS(ap, v):
    return lambda e: e.memset(ap, v)


def SCAN(out, d0, d1, init):
    return lambda e: e.tensor_tensor_scan(out=out, data0=d0, data1=d1, initial=init, op0=ALU.mult, op1=ALU.add)


def DMA(out, in_, **kw):
    return lambda e: e.dma_start(out=out, in_=in_, **kw)


def build_program(stop_after=None, debug=False, n_groups=NG):
    nc = bass.Bass("TRN2", target_bir_lowering=False)
    kind_s = "ExternalOutput" if debug else "Internal"
    dt = nc.dram_tensor
    xs = dt("xs", [SEQ, D], F32, kind="ExternalInput").ap()
    xq = dt("xq", [NSLOT * 512, D], F32, kind="ExternalInput").ap()
    w_in = dt("w_in", [D, 2560], F32, kind="ExternalInput").ap()
    w_out = dt("w_out", [D, D], F32, kind="ExternalInput").ap()
    w_f1 = dt("w_f1", [D, 2 * DFF], F32, kind="ExternalInput").ap()
    w_f2 = dt("w_f2", [DFF, D], F32, kind="ExternalInput").ap()
    wbd_d = dt("wbd", [128, 8, 128], F32, kind="ExternalInput").ap()
    pv_d = dt("pv", [128, NV], F32, kind="ExternalInput").ap()
    nfin_d = dt("nfin", [128, D], F32, kind="ExternalInput").ap()
    cb_d = dt("cb", [128, 3, 128], BF16, kind="ExternalInput").ap()
    mk_d = dt("mk", [128, 2, 8, 512], BF16, kind="ExternalInput").ap()
    out_d = dt("out", [NSLOT * 512, D], F32, kind="ExternalOutput").ap()
    KT = dt("KT", [512, SEQ], BF16, kind=kind_s).ap()
    Vd = dt("Vd", [SEQ, 512], BF16, kind=kind_s).ap()
    QT = dt("QT", [NSLOT, 2, 512, 512], BF16, kind=kind_s).ap()
    YL = dt("YL", [NSLOT, 2, 512, 512], BF16, kind=kind_s).ap()
    YA = dt("YA", [512, NSLOT * 512], BF16, kind=kind_s).ap()
    X1 = dt("X1", [NSLOT * 512, D], F32, kind=kind_s).ap()
    W1s = dt("W1s", [128, 8, 2 * DFF], BF16, kind="Internal").ap()
    W2s = dt("W2s", [128, NFC, D], BF16, kind="Internal").ap()

    P = Prog()
    def finish(es):
        P.emit(nc, es)
        nc._prog_stats = (P.stats, P.counts)
        return nc

    with ExitStack() as es:
        arena_h = es.enter_context(nc.sbuf_tensor("arena", [128, 51968], F32))
        AR = Arena(arena_h)
        psum_h = es.enter_context(nc.psum_tensor("ps", [128, 4096], F32))
        psum16_h = psum_h.bitcast(BF16)
        banks = [psum_h[:, i * 512:(i + 1) * 512] for i in range(8)]
        banks16 = [psum16_h[:, i * 1024:(i + 1) * 1024] for i in range(8)]

        pv = AR.alloc([NV], F32)
        cb = AR.alloc([3, 128], BF16)
        ident, negtri, ones = cb[:, 0, :], cb[:, 1, :], cb[:, 2, :]
        cneg = AR.alloc([512], F32)
        cpos = AR.alloc([512], F32)
        cs = AR.alloc([4], F32)
        epsb = AR.alloc([4], F32)
        tmp4 = AR.alloc([4], F32)
        junks = [AR.alloc([1024], BF16) for _ in range(2)]
        jcnt = [0]

        def sq_accum(in_ap, acc_ap, rkeys, wkeys):
            jb = jcnt[0] % 2
            jcnt[0] += 1
            P.add("act", ACT(junks[jb], in_ap, AF.Square, accum_out=acc_ap), r=rkeys, w=list(wkeys) + [("junk", jb)])
        P.add("sp", DMA(pv, pv_d[:, :]), w=["pv"], dma="pv")
        P.add("sp", DMA(cb, cb_d[:, :, :]), w=["cb"], dma="cb")
        P.add("pool", MS(cneg, -0.5), w=["cneg"])
        P.add("pool", MS(epsb, EPS), w=["epsb"])
        P.add("pool", MS(cpos, 0.5), w=["cpos"])
        P.add("act", ACT(tmp4, pv[:, 44:48], AF.Exp, scale=-1.0), r=["pv"], w=["tmp4"])
        P.add("act", ACT(tmp4, tmp4, AF.Ln, bias=1.0), r=["tmp4"], w=["tmp4"])
        P.add("dve", TS(cs, tmp4, -8.0, None, ALU.mult), r=["tmp4"], w=["cs"])
        base_top = AR.top

        def norm_tile(xt_ap, xkey, ss_ap, sskey, rs_ap, rskey, hb_ap, hbkey):
            sq_accum(xt_ap, ss_ap, [xkey], [sskey])
            P.add("pool", TS(rs_ap, ss_ap, 1.0 / D, EPS, ALU.mult, ALU.add), r=[sskey], w=[rskey])
            P.add("pool", TT(rs_ap, rs_ap, cneg[:, 0:1], ALU.pow), r=[rskey, "cneg"], w=[rskey])
            P.add("dve", TS(hb_ap, xt_ap, rs_ap, None, ALU.mult), r=[xkey, rskey], w=[hbkey])

        def transpose_group(hb, hbkeys, hT, hTkey, gcol0, ntt):
            w_ = ntt * 128
            for cp in range(4):
                bk = cp % 2
                for ci in range(2):
                    c = cp * 2 + ci
                    for tt in range(ntt):
                        o0 = ci * w_ + tt * 128
                        P.add("pe", TR(banks16[bk][:, o0:o0 + 128], hb[:, tt, c * 128:(c + 1) * 128], ident), r=[hbkeys[tt], "cb"], w=[("pb", bk)])
                for ci in range(2):
                    c = cp * 2 + ci
                    src = banks16[bk][:, ci * w_:(ci + 1) * w_]
                    g = pv[:, gcol0 + c:gcol0 + c + 1]
                    P.add("act", ACT(hT[:, c, :], src, AF.Identity, scale=g), r=[("pb", bk), "pv"], w=[(hTkey, c)])

        Win = AR.alloc([8, 2560], BF16)
        Wbd = AR.alloc([8, 128], BF16)
        for c in range(8):
            P.add("pool", DMA(Win[:, c, :], w_in[c * 128:(c + 1) * 128, :], max_dma_last_dim=4096), w=["win"], dma="win")
        P.add("pool", DMA(Wbd, wbd_d[:, :, :]), w=["wbd"], dma="wbd")
        NXT = 4
        xt = [AR.alloc([1024], F32) for _ in range(NXT)]
        ssb = AR.alloc([2, 4], F32)
        rsb = AR.alloc([2, 4], F32)
        hb = [AR.alloc([4, 1024], BF16) for _ in range(2)]
        hT = [AR.alloc([8, 512], BF16) for _ in range(2)]
        kt_o = [AR.alloc([4, 512], BF16) for _ in range(2)]
        qt_o = [AR.alloc([4, 512], BF16) for _ in range(2)]
        v_o = [AR.alloc([4, 512], BF16) for _ in range(2)]
        yn_o = [AR.alloc([4, 512], BF16) for _ in range(2)]
        xl = [AR.alloc([4, 516], F32) for _ in range(2)]
        gl = [AR.alloc([4, 512], BF16) for _ in range(2)]
        yl = AR.alloc([4, 512], F32)
        hprev = AR.alloc([4], F32)
        rl = AR.alloc([512], F32)
        L_xc = [AR.alloc([512], F32) for _ in range(2)]
        L_xcb = [AR.alloc([512], BF16) for _ in range(2)]
        L_r = [AR.alloc([512], F32) for _ in range(2)]
        L_i = [AR.alloc([512], F32) for _ in range(2)]
        L_a = [AR.alloc([512], F32) for _ in range(2)]
        L_t = [AR.alloc([512], F32) for _ in range(2)]
        L_bt = [AR.alloc([512], F32) for _ in range(2)]
        L_hl = [AR.alloc([512], F32) for _ in range(2)]
        L_sq = [AR.alloc([512], BF16) for _ in range(4)]
        for g2_ in range(2):
            P.add("pool", MS(xl[g2_], 0.0), w=[("xlm", g2_, cc) for cc in range(4)] + [("xlh", g2_, cc) for cc in range(4)])
        P.add("pool", MS(hprev, 0.0), w=[("hprev", cc) for cc in range(4)])

        st1 = {"xcnt": 0, "pjc": 0}

        def hTkeys(g2):
            return [(("hT", g2), c) for c in range(8)]

        def proj_fm(G, col):
            g2 = G % 2
            hTk = hTkeys(g2)
            bk = 2 + st1["pjc"] % 2
            st1["pjc"] += 1
            for c in range(8):
                P.add("pe", MM(banks[bk], Win[:, c, col:col + 128], hT[g2][:, c, :], c == 0, c == 7), r=["win", hTk[c]], w=[("pb", bk)])
            return bk

        def front_norm(G):
            g2 = G % 2
            hbk = [("hb", g2, tt) for tt in range(4)]
            for tt in range(4):
                s = st1["xcnt"] % NXT
                st1["xcnt"] += 1
                row0 = (G * 4 + tt) * 128
                P.add("sp", DMA(xt[s], xs[row0:row0 + 128, :]), w=[("xt", s)], dma=("xt", s))
                norm_tile(xt[s], ("xt", s), ssb[:, g2, tt:tt + 1], ("ss", g2, tt), rsb[:, g2, tt:tt + 1], ("rs", g2, tt), hb[g2][:, tt, :], hbk[tt])
            transpose_group(hb[g2], hbk, hT[g2], ("hT", g2), 0, 4)

        def proj_K(G):
            g2 = G % 2
            for i in range(4):
                bk = proj_fm(G, 1536 + i * 128)
                P.add("act", ACT(kt_o[g2][:, i, :], banks[bk], AF.Copy), r=[("pb", bk)], w=[("kt_o", g2, i)])
            P.add("sp", DMA(KT[:, G * 512:(G + 1) * 512].rearrange("(i p) t -> p i t", p=128), kt_o[g2]),
                  r=[("kt_o", g2, i) for i in range(4)], w=[("KT", G)], dma=("st_kt", g2))

        def proj_Q(G):
            g2 = G % 2
            for i in range(4):
                bk = proj_fm(G, 1024 + i * 128)
                P.add("act", ACT(qt_o[g2][:, i, :], banks[bk], AF.Copy, scale=0.125), r=[("pb", bk)], w=[("qt_o", g2, i)])
            P.add("sp", DMA(QT[G // 2, G % 2].rearrange("(i p) t -> p i t", p=128), qt_o[g2]),
                  r=[("qt_o", g2, i) for i in range(4)], w=[("QT", G)], dma=("st_qt", g2))

        def proj_V(G):
            g2 = G % 2
            hTk = hTkeys(g2)
            for tt in range(4):
                bk = 2 + st1["pjc"] % 2
                st1["pjc"] += 1
                for c in range(8):
                    P.add("pe", MM(banks[bk], hT[g2][:, c, tt * 128:(tt + 1) * 128], Win[:, c, 2048:2560], c == 0, c == 7), r=["win", hTk[c]], w=[("pb", bk)])
                P.add("dve", CP(v_o[g2][:, tt, :], banks[bk]), r=[("pb", bk)], w=[("v_o", g2, tt)])
            P.add("sp", DMA(Vd[G * 512:(G + 1) * 512, :].rearrange("(tt p) n -> p tt n", p=128), v_o[g2]),
                  r=[("v_o", g2, tt) for tt in range(4)], w=[("Vd", G)], dma=("st_v", g2))

        def proj_gate(G):
            g2 = G % 2
            for cc in range(4):
                bk = proj_fm(G, 512 + cc * 128)
                P.add("act", ACT(gl[g2][:, cc, :], banks[bk], AF.Gelu_apprx_tanh), r=[("pb", bk)], w=[("gl", g2, cc)])

        def proj_x(G):
            g2 = G % 2
            for cc in range(4):
                if G > 0:
                    P.add("pool", CP(xl[g2][:, cc, 0:3], xl[1 - g2][:, cc, 512:515]), r=[("xlm", 1 - g2, cc)], w=[("xlh", g2, cc)])
                bk = proj_fm(G, cc * 128)
                P.add("dve", CP(xl[g2][:, cc, 3:515], banks[bk]), r=[("pb", bk)], w=[("xlm", g2, cc)])

        def lru_s1(G, cc):
            g2 = G % 2
            lb = cc % 2
            xlg = xl[g2]
            xc, xcb = L_xc[lb], L_xcb[lb]
            kx = [("xlm", g2, cc), ("xlh", g2, cc), "pv"]
            cw = lambda tap: pv[:, 20 + cc * 4 + tap:20 + cc * 4 + tap + 1]
            P.add("pool", TS(xc, xlg[:, cc, 3:515], cw(3), pv[:, 16 + cc:17 + cc], ALU.mult, ALU.add), r=kx, w=[("xc", lb)])
            for tap in (2, 1, 0):
                P.add("dve", STT(xc, xlg[:, cc, tap:tap + 512], cw(tap), xc, ALU.mult, ALU.add), r=kx + [("xc", lb)], w=[("xc", lb)])
            P.add("act", ACT(xcb, xc, AF.Copy), r=[("xc", lb)], w=[("xcb", lb)])
            gr, gi = (5, 6) if lb == 0 else (4, 7)
            P.add("pe", MM(banks[gr], Wbd[:, cc, :], xcb, True, True), r=["wbd", ("xcb", lb)], w=[("pb", gr)])
            P.add("pe", MM(banks[gi], Wbd[:, 4 + cc, :], xcb, True, True), r=["wbd", ("xcb", lb)], w=[("pb", gi)])
            r_, i_ = L_r[lb], L_i[lb]
            P.add("act", ACT(r_, banks[gr], AF.Sigmoid, bias=pv[:, 36 + cc:37 + cc]), r=[("pb", gr), "pv"], w=[("r", lb)])
            P.add("act", ACT(i_, banks[gi], AF.Sigmoid, bias=pv[:, 40 + cc:41 + cc]), r=[("pb", gi), "pv"], w=[("i", lb)])

        def lru_s23(G, cc):
            g2 = G % 2
            lb = cc % 2
            xc, r_, i_, a_, t_, bt, hl, sq = L_xc[lb], L_r[lb], L_i[lb], L_a[lb], L_t[lb], L_bt[lb], L_hl[lb], L_sq[cc]
            P.add("act", ACT(a_, r_, AF.Exp, scale=cs[:, cc:cc + 1]), r=[("r", lb), "cs"], w=[("a", lb)])
            P.add("pool", TT(t_, a_, a_, ALU.mult), r=[("a", lb)], w=[("t", lb)])
            P.add("act", ACT(t_, t_, AF.Ln, scale=-1.0, bias=1.0), r=[("t", lb)], w=[("t", lb)])
            P.add("act", ACT(t_, t_, AF.Exp, scale=0.5), r=[("t", lb)], w=[("t", lb)])
            P.add("pool", TT(bt, i_, xc, ALU.mult), r=[("i", lb), ("xc", lb)], w=[("bt", lb)])
            P.add("pool", TT(bt, bt, t_, ALU.mult), r=[("bt", lb), ("t", lb)], w=[("bt", lb)])
            P.add("dve", SCAN(hl, a_, bt, hprev[:, cc:cc + 1]), r=[("a", lb), ("bt", lb), ("hprev", cc)], w=[("hl", lb)])
            P.add("dve", CP(hprev[:, cc:cc + 1], hl[:, 511:512]), r=[("hl", lb)], w=[("hprev", cc)])
            P.add("pool", TT(yl[:, cc, :], gl[g2][:, cc, :], hl, ALU.mult), r=[("gl", g2, cc), ("hl", lb)], w=[("yl", cc)])
            P.add("pool", TT(sq, yl[:, cc, :], yl[:, cc, :], ALU.mult), r=[("yl", cc)], w=[("sq", cc)])

        def lru_final(G):
            g2 = G % 2
            for cc in range(4):
                P.add("pe", MM(banks[7], ones, L_sq[cc], cc == 0, cc == 3), r=["cb", ("sq", cc)], w=[("pb", 7)])
            P.add("act", ACT(rl, banks[7], AF.Ln, scale=1.0 / 512, bias=epsb[:, 0:1]), r=[("pb", 7), "epsb"], w=["rl"])
            P.add("act", ACT(rl, rl, AF.Exp, scale=-0.5), r=["rl"], w=["rl"])
            for cc in range(4):
                P.add("dve", STT(yn_o[g2][:, cc, :], yl[:, cc, :], pv[:, 48 + cc:49 + cc], rl, ALU.mult, ALU.mult),
                      r=[("yl", cc), "rl", "pv"], w=[("yn_o", g2, cc)])
            P.add("sp", DMA(YL[G // 2, G % 2].rearrange("(i p) t -> p i t", p=128), yn_o[g2]),
                  r=[("yn_o", g2, cc) for cc in range(4)], w=[("YL", G)], dma=("st_yl", g2))

        def capture(fn_):
            P.cap = []
            fn_()
            lst = P.cap
            P.cap = None
            return lst

        def front_all(G):
            proj_K(G)
            proj_Q(G)
            proj_V(G)
            proj_gate(G)
            proj_x(G)

        def merge(lists):
            idx = [0] * len(lists)
            while True:
                best = None
                for k, l in enumerate(lists):
                    if idx[k] < len(l):
                        fr = idx[k] / len(l)
                        if best is None or fr < best[0]:
                            best = (fr, k)
                if best is None:
                    break
                k = best[1]
                e_ = lists[k][idx[k]]
                P.add(e_[0], e_[1], r=e_[2], w=e_[3], dma=e_[4])
                idx[k] += 1

        def lru_stream(G, ccs, fin):
            for cc in ccs:
                lru_s1(G, cc)
                lru_s23(G, cc)
            if fin:
                lru_final(G)

        front_norm(0)
        for G in range(n_groups + 1):
            lists = []
            if G < n_groups:
                lists.append(capture(lambda: front_all(G)))
            if G + 1 < n_groups:
                lists.append(capture(lambda: front_norm(G + 1)))
            if G >= 1:
                lists.append(capture(lambda: lru_stream(G - 1, (0, 2), False)))
                lists.append(capture(lambda: lru_stream(G - 1, (1, 3), False)))
            merge(lists)
            if G >= 1:
                lru_final(G - 1)

        P.barrier()
        if stop_after == "p1":
            return finish(es)

        AR.top = base_top
        Wout = AR.alloc([8, 1024], BF16)
        p2_top = AR.top
        MK = AR.alloc([2, 8, 512], BF16)
        KA = [[AR.alloc([SEQ], BF16) for _ in range(2)] for _ in range(2)]
        VP = [AR.alloc([64, 128], BF16) for _ in range(2)]
        QS = [[AR.alloc([2, 512], BF16) for _ in range(4)] for _ in range(2)]
        QC = [[AR.alloc([2, 512], BF16) for _ in range(2)] for _ in range(2)]
        EB = [AR.alloc([1024], F32) for _ in range(2)]
        SPB = [AR.alloc([1024], BF16) for _ in range(3)]
        AB = [AR.alloc([1024], BF16) for _ in range(3)]
        CF = [AR.alloc([1024], F32) for _ in range(2)]
        YO = [AR.alloc([512], BF16) for _ in range(2)]
        P.add("sp", DMA(MK, mk_d[:, :, :, :]), w=["mk"], dma="mk")
        stg = []
        for c in range(8):
            stg.append((DMA(Wout[:, c, :], w_out[c * 128:(c + 1) * 128, :], max_dma_last_dim=4096), "wout", "wout"))
        for c in range(8):
            for hf in range(2):
                stg.append((DMA(W1s[:, c, hf * DFF:(hf + 1) * DFF], w_f1[c * 128:(c + 1) * 128, hf * DFF:(hf + 1) * DFF], max_dma_last_dim=4096), "Wstg", "stg"))
        for fc in range(NFC):
            stg.append((DMA(W2s[:, fc, :], w_f2[fc * 128:(fc + 1) * 128, :], max_dma_last_dim=4096), "Wstg", "stg"))
        for pb in range(2):
            for s_ in range(2):
                P.add("pool", MS(KA[pb][s_][0:32, :], 0.0), w=[("ka", pb, s_)])
                P.add("pool", MS(KA[pb][s_][0:1, :], -1.0), w=[("ka", pb, s_)])
        for qs in range(2):
            for k in range(4):
                P.add("pool", MS(QS[qs][k][0:32, :, :], 0.0), w=[("q", qs, k)])
            for k in range(2):
                P.add("pool", MS(QC[qs][k][0:32, :, :], 0.0), w=[("qc", qs, k)])

        pjs = [(p, j) for p in range(4) for j in range(NSLOT)]
        if stop_after == "p2small":
            pjs = [(0, 0), (0, 1), (1, 0)]
        blocks = []
        for pj, (p, j) in enumerate(pjs):
            nblk = 8 * j + 8
            for n in range(nblk):
                kb = 8 * j + 7 - n
                blocks.append(dict(p=p, j=j, n=n, kb=kb, first=(n == 0), last=(n == nblk - 1), band=(kb >= 8 * j), bi=kb - 8 * j, pj=pj))
        L = len(blocks)
        allKT = [("KT", G) for G in range(NG)]
        allV = [("Vd", G) for G in range(NG)]
        allQT = [("QT", G) for G in range(NG)]
        loaded_pairs = set()

        def load_pair(p):
            if p in loaded_pairs or p >= 4:
                return
            loaded_pairs.add(p)
            pb = p % 2
            for s_ in range(2):
                h = 2 * p + s_
                P.add("sp", DMA(KA[pb][s_][32:96, :], KT[h * 64:(h + 1) * 64, :]), r=allKT, w=[("ka", pb, s_)], dma=("ld_ka", pb, s_))
            for q8 in range(8):
                P.add("sp", DMA(VP[pb][:, q8 * 8:(q8 + 1) * 8, :], Vd[q8 * 1024:(q8 + 1) * 1024, p * 128:(p + 1) * 128].rearrange("(blk k) d -> k blk d", k=128)),
                      r=allV, w=[("vp", pb)], dma=("ld_vp", pb))

        def load_q(pj_):
            p, j = pjs[pj_]
            qs = pj_ % 2
            for k in range(2):
                P.add("sp", DMA(QC[qs][k][32:96, :, :], QT[j, k, p * 128:(p + 1) * 128, :].rearrange("(s d) t -> d s t", s=2)),
                      r=allQT, w=[("qc", qs, k)], dma=("ld_q", qs, k))
            sc = 56 + 2 * (j % 2)
            qz = QS[qs][0][0:96, :, :]
            P.add("dve", TS(qz, QC[qs][0][0:96, :, :], pv[0:96, sc:sc + 1], None, ALU.mult), r=[("qc", qs, 0), "pv"], w=[("q", qs, 0)])
            P.add("dve", STT(qz, QC[qs][1][0:96, :, :], pv[0:96, sc + 1:sc + 2], qz, ALU.mult, ALU.add), r=[("qc", qs, 1), ("q", qs, 0), "pv"], w=[("q", qs, 0)])
            for k in (1, 2, 3):
                P.add("pool", CP(QS[qs][k][0:96, :, :], qz), r=[("q", qs, 0)], w=[("q", qs, k)])
            if pj_ >= 1:
                for _ in range(2):
                    if stg:
                        f_, wk_, dk_ = stg.pop(0)
                        P.add("pool", f_, w=[wk_], dma=dk_)

        def stageA_pe(i, b):
            p, j, pj = b["p"], b["j"], b["pj"]
            if i == 0:
                load_pair(p)
                load_q(0)
            if b["n"] == 2:
                if pj + 1 < len(pjs):
                    load_q(pj + 1)
                    if j >= 1 or pjs[pj + 1][0] != p:
                        load_pair(pjs[pj + 1][0] if pjs[pj + 1][0] != p else p + 1)
            pb = p % 2
            qs = pj % 2
            kb = b["kb"]
            for s_ in range(2):
                P.add("pe", MM(banks[s_], KA[pb][s_][0:96, kb * 128:(kb + 1) * 128], QS[qs][0][0:96, s_, :], True, not b["band"]),
                      r=[("ka", pb, s_), ("q", qs, 0)], w=[("pb", s_)])
                if b["band"]:
                    P.add("pe", MM(banks[s_], ident, MK[:, j % 2, b["bi"], :], False, True), r=["cb", "mk"], w=[("pb", s_)])

        def stageA_act(i, b):
            P.add("act", ACT(psum_h[:, 0:1024], psum_h[:, 0:1024], AF.Exp), r=[("pb", 0), ("pb", 1)], w=[("pb", 0), ("pb", 1)])
            P.add("act", ACT(SPB[i % 3], psum_h[:, 0:1024], AF.Ln, bias=1.0), r=[("pb", 0), ("pb", 1)], w=[("spb", i % 3)])

        def stageB(i, b):
            p, j, n, pj = b["p"], b["j"], b["n"], b["pj"]
            pb = p % 2
            qs = pj % 2
            kb = b["kb"]
            qa = 1 + n % 3
            qn = 1 + (n + 1) % 3
            cf = CF[pj % 2]
            if not b["last"]:
                for s_ in range(2):
                    gb = 2 + s_
                    P.add("pe", MM(banks[gb], ones, SPB[i % 3][:, s_ * 512:(s_ + 1) * 512], True, True), r=["cb", ("spb", i % 3)], w=[("pb", gb)])
                for s_ in range(2):
                    gb = 2 + s_
                    cfs = cf[0:1, s_ * 512:(s_ + 1) * 512]
                    if n == 0:
                        P.add("dve", CP(cfs, banks[gb][0:1, :]), r=[("pb", gb)], w=[("cf", pj % 2, s_)])
                    else:
                        P.add("dve", TT(cfs, cfs, banks[gb][0:1, :], ALU.add), r=[("pb", gb), ("cf", pj % 2, s_)], w=[("cf", pj % 2, s_)])
                    P.add("dve", CP(QS[qs][qn][0:1, s_, :], cfs), r=[("cf", pj % 2, s_)], w=[("q", qs, qn)])
            for s_ in range(2):
                bb = 4 + s_
                P.add("pe", MM(banks[bb], KA[pb][s_][0:96, kb * 128:(kb + 1) * 128], QS[qs][qa][0:96, s_, :], True, False),
                      r=[("ka", pb, s_), ("q", qs, qa)], w=[("pb", bb)])
                P.add("pe", MM(banks[bb], negtri, SPB[i % 3][:, s_ * 512:(s_ + 1) * 512], False, not b["band"]), r=["cb", ("spb", i % 3)], w=[("pb", bb)])
                if b["band"]:
                    P.add("pe", MM(banks[bb], ident, MK[:, j % 2, b["bi"], :], False, True), r=["cb", "mk"], w=[("pb", bb)])
            P.add("act", ACT(AB[i % 3], psum_h[:, 4 * 512:6 * 512], AF.Exp), r=[("pb", 4), ("pb", 5)], w=[("ab", i % 3)])

        def stageC(i, b):
            p, j, pj = b["p"], b["j"], b["pj"]
            pb = p % 2
            kb = b["kb"]
            for s_ in range(2):
                P.add("pe", MM(banks[6 + s_], VP[pb][:, kb, :], AB[i % 3][:, s_ * 512:(s_ + 1) * 512], b["first"], b["last"]),
                      r=[("vp", pb), ("ab", i % 3)], w=[("pb", 6 + s_)])
            if b["last"]:
                yo = pj % 2
                for s_ in range(2):
                    P.add("dve", CP(YO[yo][s_ * 64:(s_ + 1) * 64, :], banks[6 + s_][s_ * 64:(s_ + 1) * 64, :]), r=[("pb", 6 + s_)], w=[("yo", yo)])
                P.add("sp", DMA(YA[p * 128:(p + 1) * 128, j * 512:(j + 1) * 512], YO[yo]),
                      r=[("yo", yo)], w=[("YA", 2 * p, j), ("YA", 2 * p + 1, j)], dma=("st_ya", yo))

        stageA_pe(0, blocks[0])
        for it in range(L + 2):
            if it == L:
                while stg:
                    f_, wk_, dk_ = stg.pop(0)
                    P.add("pool", f_, w=[wk_], dma=dk_)
            if it < L:
                stageA_act(it, blocks[it])
            if 1 <= it <= L:
                stageB(it - 1, blocks[it - 1])
            if it >= 2:
                stageC(it - 2, blocks[it - 2])
            if it + 1 < L:
                stageA_pe(it + 1, blocks[it + 1])

        P.barrier()
        if stop_after in ("p2", "p2small"):
            return finish(es)

        AR.top = p2_top
        W1 = AR.alloc([8, 2 * DFF], BF16)
        W2 = AR.alloc([NFC, 1024], BF16)
        wloads = []
        for c in range(8):
            wloads.append((DMA(W1[:, c, :], W1s[:, c, :]), "w1", ["Wstg"]))
        for f4 in range(0, NFC, 4):
            f5 = min(NFC, f4 + 4)
            wloads.append((DMA(W2[:, f4:f5, :], W2s[:, f4:f5, :]), "w2", ["Wstg"]))
        p3_top = AR.top
        ylb = AR.alloc([4, 512], BF16)
        ylc = [AR.alloc([4, 512], BF16) for _ in range(2)]
        yab = AR.alloc([4, 512], BF16)
        yan = AR.alloc([4, 512], BF16)
        sq2 = [AR.alloc([512], BF16) for _ in range(2)]
        rl2 = AR.alloc([512], F32)
        xt2 = [AR.alloc([1024], F32) for _ in range(3)]
        xc2 = 0
        allYL = [("YL", G) for G in range(NG)]
        for j in range(NSLOT):
            for k in range(2):
                P.add("sp", DMA(ylc[k], YL[j, k].rearrange("(i p) t -> p i t", p=128)), r=allYL, w=[("ylc", k)], dma=("ld_ylc", k))
            sc = 56 + 2 * (j % 2)
            P.add("dve", TS(ylb, ylc[0], pv[:, sc:sc + 1], None, ALU.mult), r=[("ylc", 0), "pv"], w=["ylb"])
            P.add("dve", STT(ylb, ylc[1], pv[:, sc + 1:sc + 2], ylb, ALU.mult, ALU.add), r=[("ylc", 1), "ylb", "pv"], w=["ylb"])
            P.add("sp", DMA(yab, YA[:, j * 512:(j + 1) * 512].rearrange("(i p) t -> p i t", p=128)),
                  r=[("YA", h, j) for h in range(8)], w=["yab", ("p25slot", j)], dma="ld_yab")
            nw = 3 if j < 4 else len(wloads)
            for fn_, key_, rk_ in wloads[:nw]:
                P.add("sp", fn_, r=[("p25slot", j)] + rk_, w=[key_], dma=key_)
            wloads = wloads[nw:]
            for p in range(4):
                P.add("pool", TT(sq2[p % 2], yab[:, p, :], yab[:, p, :], ALU.mult), r=["yab"], w=[("sq2", p % 2)])
                P.add("pe", MM(banks[7], ones, sq2[p % 2], p == 0, p == 3), r=["cb", ("sq2", p % 2)], w=[("pb", 7)])
            P.add("act", ACT(rl2, banks[7], AF.Ln, scale=1.0 / 512, bias=epsb[:, 0:1]), r=[("pb", 7), "epsb"], w=["rl2"])
            P.add("act", ACT(rl2, rl2, AF.Exp, scale=-0.5), r=["rl2"], w=["rl2"])
            for p in range(4):
                P.add("dve", STT(yan[:, p, :], yab[:, p, :], pv[:, 52 + p:53 + p], rl2, ALU.mult, ALU.mult), r=["yab", "rl2", "pv"], w=[("yan", p)])
            for tt in range(4):
                s = xc2 % 3
                xc2 += 1
                row0 = j * 512 + tt * 128
                P.add("sp", DMA(xt2[s], xq[row0:row0 + 128, :]), w=[("xt2", s)], dma=("xt2", s))
                for hf in range(2):
                    bk = (tt * 2 + hf) % 4
                    for k in range(8):
                        src = ylb[:, k, tt * 128:(tt + 1) * 128] if k < 4 else yan[:, k - 4, tt * 128:(tt + 1) * 128]
                        rk = "ylb" if k < 4 else ("yan", k - 4)
                        P.add("pe", MM(banks[bk], src, Wout[:, k, hf * 512:(hf + 1) * 512], k == 0, k == 7), r=["wout", rk], w=[("pb", bk)])
                    P.add("dve", TT(xt2[s][:, hf * 512:(hf + 1) * 512], xt2[s][:, hf * 512:(hf + 1) * 512], banks[bk], ALU.add),
                          r=[("pb", bk), ("xt2", s)], w=[("xt2", s)])
                P.add("sp", DMA(X1[row0:row0 + 128, :], xt2[s]), r=[("xt2", s)], w=[("X1", j * 4 + tt)], dma=("st_x1", s))

        P.barrier()
        if stop_after == "p25":
            return finish(es)

        AR.top = base_top
        x1t_b = AR.alloc([NT3, 1024], F32)
        hb3_b = AR.alloc([NT3, 1024], BF16)
        hT3_b = AR.alloc([8, TG], BF16)
        assert AR.top <= p2_top
        AR.top = p3_top
        nfin = AR.alloc([1024], F32)
        x1t = [AR.alloc([NT3, 1024], F32), x1t_b]
        hb3 = [AR.alloc([NT3, 1024], BF16), hb3_b]
        hT3 = [AR.alloc([8, TG], BF16), hT3_b]
        aT = AR.alloc([NFC, TG], BF16)
        sgb = [AR.alloc([TG], F32) for _ in range(2)]
        ot = [AR.alloc([1024], F32) for _ in range(2)]
        ss3 = AR.alloc([2, NT3], F32)
        rs3 = AR.alloc([2, NT3], F32)
        ss4 = AR.alloc([2, NT3], F32)
        rs4 = AR.alloc([2, NT3], F32)
        P.add("sp", DMA(nfin, nfin_d[:, :]), w=["nfin"], dma="nfin")
        st3 = {"oc": 0}
        n_it = NSLOT * 512 // TG

        def p3_pre(it3):
            j2 = it3 % 2
            hbk = [("hb3", j2, tt) for tt in range(NT3)]
            for tt in range(NT3):
                tile_id = it3 * NT3 + tt
                row0 = tile_id * 128
                P.add("sp", DMA(x1t[j2][:, tt, :], X1[row0:row0 + 128, :]), r=[("X1", tile_id)], w=[("x1t", j2, tt)], dma=("ld_x1", j2, tt))
                norm_tile(x1t[j2][:, tt, :], ("x1t", j2, tt), ss3[:, j2, tt:tt + 1], ("ss3", j2, tt), rs3[:, j2, tt:tt + 1], ("rs3", j2, tt), hb3[j2][:, tt, :], hbk[tt])
            transpose_group(hb3[j2], hbk, hT3[j2], ("hT3", j2), 8, NT3)

        def p3_main(it3):
            j2 = it3 % 2
            hTk = [(("hT3", j2), c) for c in range(8)]
            for fc in range(NFC):
                gb = 2 + fc % 2
                ub = 4 + fc % 2
                for c in range(8):
                    P.add("pe", MM(banks[gb][:, 0:TG], W1[:, c, fc * 128:(fc + 1) * 128], hT3[j2][:, c, :], c == 0, c == 7), r=["w1", hTk[c]], w=[("pb", gb)])
                for c in range(8):
                    P.add("pe", MM(banks[ub][:, 0:TG], W1[:, c, DFF + fc * 128:DFF + (fc + 1) * 128], hT3[j2][:, c, :], c == 0, c == 7), r=["w1", hTk[c]], w=[("pb", ub)])
                P.add("act", ACT(sgb[fc % 2], banks[gb][:, 0:TG], AF.Silu), r=[("pb", gb)], w=[("sgb", fc % 2)])
                P.add("dve", TT(aT[:, fc, :], sgb[fc % 2], banks[ub][:, 0:TG], ALU.mult), r=[("sgb", fc % 2), ("pb", ub)], w=[("aT", fc)])
            for tt in range(NT3):
                xk = ("x1t", j2, tt)
                for hf in range(2):
                    ob = 6 + (tt * 2 + hf) % 2
                    for fc in range(NFC):
                        P.add("pe", MM(banks[ob], aT[:, fc, tt * 128:(tt + 1) * 128], W2[:, fc, hf * 512:(hf + 1) * 512], fc == 0, fc == NFC - 1), r=["w2", ("aT", fc)], w=[("pb", ob)])
                    P.add("dve", TT(x1t[j2][:, tt, hf * 512:(hf + 1) * 512], x1t[j2][:, tt, hf * 512:(hf + 1) * 512], banks[ob], ALU.add),
                          r=[("pb", ob), xk], w=[xk])
                o = st3["oc"] % 2
                st3["oc"] += 1
                sq_accum(x1t[j2][:, tt, :], ss4[:, j2, tt:tt + 1], [xk], [("ss4", j2, tt)])
                P.add("pool", TS(rs4[:, j2, tt:tt + 1], ss4[:, j2, tt:tt + 1], 1.0 / D, EPS, ALU.mult, ALU.add), r=[("ss4", j2, tt)], w=[("rs4", j2, tt)])
                P.add("pool", TT(rs4[:, j2, tt:tt + 1], rs4[:, j2, tt:tt + 1], cneg[:, 0:1], ALU.pow), r=[("rs4", j2, tt), "cneg"], w=[("rs4", j2, tt)])
                P.add("dve", STT(ot[o], x1t[j2][:, tt, :], rs4[:, j2, tt:tt + 1], nfin, ALU.mult, ALU.mult), r=[xk, ("rs4", j2, tt), "nfin"], w=[("ot", o)])
                row0 = (it3 * NT3 + tt) * 128
                P.add("sp", DMA(out_d[row0:row0 + 128, :], ot[o]), r=[("ot", o)], w=[("out", it3, tt)], dma=("st_out", o))

        p3_pre(0)
        for it3 in range(n_it):
            lists = [capture(lambda: p3_main(it3))]
            if it3 + 1 < n_it:
                lists.append(capture(lambda: p3_pre(it3 + 1)))
            merge(lists)

        P.barrier()
        return finish(es)


def _slot_groups(par):
    return [2 * j + (par ^ (j & 1)) for j in range(NSLOT)]


def _consts():
    bf = ml_dtypes.bfloat16
    cb = np.zeros((128, 3, 128), np.float32)
    cb[:, 0, :] = np.eye(128)
    jj = np.arange(128)[:, None]
    s_ = np.arange(128)[None, :]
    cb[:, 1, :] = np.where(jj >= s_, -1.0, 0.0)
    cb[:, 2, :] = 1.0
    s = np.arange(128)[:, None, None]
    i = np.arange(8)[None, :, None]
    t = np.arange(512)[None, None, :]
    kpos = 128 * i + s
    m_min = np.where(kpos < t, 0.0, NEG)
    m_max = np.where(kpos < 512 + t, 0.0, NEG)
    return cb.astype(bf), m_min.astype(bf), m_max.astype(bf)


def _make_in_maps(inputs):
    f = lambda a: np.ascontiguousarray(np.asarray(a, dtype=np.float32))
    x = f(inputs["x"])
    w_in = f(inputs["w_in"][0])
    w_out = f(inputs["w_out"][0])
    w_f1 = f(inputs["w_ffn_in"][0])
    w_f2 = f(inputs["w_ffn_out"][0])
    pv = np.zeros((128, NV), np.float32)
    pv[:, 0:8] = f(inputs["norm_mix"][0]).reshape(8, 128).T
    pv[:, 8:16] = f(inputs["norm_ffn"][0]).reshape(8, 128).T
    pv[:, 16:20] = f(inputs["conv_b"][0]).reshape(4, 128).T
    cw = f(inputs["conv_w"][0])
    for cc in range(4):
        pv[:, 20 + cc * 4:24 + cc * 4] = cw[:, cc * 128:(cc + 1) * 128].T
    pv[:, 36:40] = f(inputs["b_rg"][0]).reshape(4, 128).T
    pv[:, 40:44] = f(inputs["b_ig"][0]).reshape(4, 128).T
    pv[:, 44:48] = f(inputs["lru_lambda"][0]).reshape(4, 128).T
    pv[:, 48:52] = f(inputs["norm_lru_out"][0]).reshape(4, 128).T
    pv[:, 52:56] = f(inputs["norm_att_out"][0]).reshape(4, 128).T
    wbd = np.zeros((128, 8, 128), np.float32)
    wrg = f(inputs["w_rg"][0])
    wig = f(inputs["w_ig"][0])
    for cc in range(4):
        for hb_ in range(2):
            blk = 2 * cc + hb_
            wbd[hb_ * 64:(hb_ + 1) * 64, cc, hb_ * 64:(hb_ + 1) * 64] = wrg[blk]
            wbd[hb_ * 64:(hb_ + 1) * 64, 4 + cc, hb_ * 64:(hb_ + 1) * 64] = wig[blk]
    nfin = np.ascontiguousarray(np.broadcast_to(f(inputs["norm_final"])[None, :], (128, D)))
    cb, m_min, m_max = _consts()
    maps = []
    for c in range(N_CORES):
        b, par = c // 2, c % 2
        groups = _slot_groups(par)
        xq = np.concatenate([x[b, g * 512:(g + 1) * 512] for g in groups], axis=0)
        mk = np.stack([m_min, m_max] if par == 0 else [m_max, m_min], axis=1)
        pvc = pv.copy()
        pvc[:, 56:60] = np.array([1 - par, par, par, 1 - par], np.float32)[None, :]
        maps.append({"xs": np.ascontiguousarray(x[b]), "xq": np.ascontiguousarray(xq), "w_in": w_in, "w_out": w_out,
                     "w_f1": w_f1, "w_f2": w_f2, "wbd": wbd, "pv": pvc, "nfin": nfin, "cb": cb,
                     "mk": np.ascontiguousarray(mk)})
    return maps


_NC_CACHE = {}


def kernel(**inputs):
    maps = _make_in_maps(inputs)
    if "nc" not in _NC_CACHE:
        _NC_CACHE["nc"] = build_program()
    nc = _NC_CACHE["nc"]
    res = run_bass_kernel_spmd(nc, maps, core_ids=list(range(N_CORES)))
    out = np.zeros((4, SEQ, D), np.float32)
    for c in range(N_CORES):
        b, par = c // 2, c % 2
        o = np.asarray(res.results[c]["out"], dtype=np.float32)
        for j, g in enumerate(_slot_groups(par)):
            out[b, g * 512:(g + 1) * 512] = o[j * 512:(j + 1) * 512]
    return out
```

```python
import numpy as np
import ml_dtypes
from contextlib import ExitStack
import concourse.bass as bass
import concourse.mybir as mybir
from concourse.bass_utils import run_bass_kernel_spmd

F32 = mybir.dt.float32
BF16 = mybir.dt.bfloat16
AF = mybir.ActivationFunctionType
ALU = mybir.AluOpType

D = 1024
SEQ = 8192
NG = 16
NSLOT = 8
DFF = 2816
NFC = 22
EPS = 1e-6
NEG = -30000.0
NV = 64
N_CORES = 8
TG = 256
NT3 = TG // 128


class _Op:
    __slots__ = ("eng", "fn", "deps", "key", "sig", "signal")


class Prog:
    def __init__(self):
        self.ops = []
        self.lastw = {}
        self.rd = {}
        self.last_eng = {}
        self.last_dma = {}

    def add(self, eng, fn, r=(), w=(), dma=None):
        if getattr(self, "cap", None) is not None:
            self.cap.append((eng, fn, tuple(r), tuple(w), dma))
            return -1
        i = len(self.ops)
        xr = [k for k in r if isinstance(k, tuple) and k[0] == "pb"]
        if xr:
            w = list(w) + xr
        deps = set()
        for k in r:
            j = self.lastw.get(k)
            if j is not None:
                deps.add(j)
        for k in w:
            j = self.lastw.get(k)
            if j is not None:
                deps.add(j)
            rk = self.rd.get(k)
            if rk:
                deps.update(rk.values())
        for k in r:
            rk = self.rd.setdefault(k, {})
            rk[("dma", i) if dma is not None else eng] = i
        for k in w:
            self.lastw[k] = i
            self.rd[k] = {}
        op = _Op()
        op.eng = eng
        op.fn = fn
        op.deps = deps
        op.key = dma
        op.sig = None
        op.signal = dma is not None
        self.ops.append(op)
        if dma is not None:
            self.last_dma[dma] = i
        else:
            self.last_eng[eng] = i
        return i

    def barrier(self, engines=("pe", "act", "dve", "pool", "sp")):
        deps = set(self.last_eng.values()) | set(self.last_dma.values())
        for e in engines:
            op = _Op()
            op.eng = e
            op.fn = None
            op.deps = set(deps)
            op.key = None
            op.sig = None
            op.signal = False
            self.ops.append(op)

    def emit(self, nc, es, pre_sp=None):
        ops = self.ops
        for op in ops:
            for d in op.deps:
                D_ = ops[d]
                if D_.key is None and D_.fn is not None:
                    if D_.eng == "pe" and op.eng == "pe" and op.key is None:
                        continue
                    D_.signal = True
        cnt = {}
        for op in ops:
            if op.fn is None:
                continue
            if op.key is not None:
                k = ("dma", op.key)
                cnt[k] = cnt.get(k, 0) + 16
                op.sig = cnt[k]
            elif op.signal:
                cnt[op.eng] = cnt.get(op.eng, 0) + 1
                op.sig = cnt[op.eng]
        self.counts = cnt
        sems = {}
        for n_, k in enumerate(cnt.keys()):
            sems[k] = es.enter_context(nc.semaphore("s%d" % n_))
        block = es.enter_context(nc.Block())
        stats = {}

        def run(engname, e):
            waited = {}
            nw = 0
            ni = 0
            for op in ops:
                if op.eng != engname:
                    continue
                need = {}
                for d in op.deps:
                    D_ = ops[d]
                    if D_.fn is None:
                        continue
                    if D_.key is None:
                        if D_.eng == "pe" and engname == "pe" and op.key is None:
                            continue
                        k = D_.eng
                    else:
                        k = ("dma", D_.key)
                    if D_.sig > need.get(k, 0):
                        need[k] = D_.sig
                for k, v in need.items():
                    if waited.get(k, 0) >= v:
                        continue
                    e.wait_ge(sems[k], v)
                    waited[k] = v
                    nw += 1
                if op.fn is not None:
                    ins = op.fn(e)
                    ni += 1
                    if op.key is not None:
                        ins.then_inc(sems[("dma", op.key)], 16)
                    elif op.signal:
                        ins.then_inc(sems[op.eng], 1)
            stats[engname] = (ni, nw)

        @block.tensor
        def _(t):
            run("pe", t)

        @block.scalar
        def _(a):
            run("act", a)

        @block.vector
        def _(v):
            run("dve", v)

        @block.gpsimd
        def _(g):
            run("pool", g)

        @block.sync
        def _(s):
            if pre_sp is not None:
                pre_sp(s)
            run("sp", s)

        self.stats = stats


class Arena:
    def __init__(self, h32):
        self.h32 = h32
        self.h16 = h32.bitcast(BF16)
        self.top = 0
        self.cap = h32.shape[1] * 4

    def alloc(self, free, dtype):
        n = 1
        for f in free:
            n *= f
        sz = 4 if dtype == F32 else 2
        off = self.top
        nb = (n * sz + 63) // 64 * 64
        self.top += nb
        assert self.top <= self.cap, ("SBUF arena overflow", self.top, self.cap)
        if dtype == F32:
            ap = self.h32[:, off // 4: off // 4 + n]
        else:
            ap = self.h16[:, off // 2: off // 2 + n]
        if len(free) == 2:
            ap = ap.rearrange("p (a b) -> p a b", a=free[0])
        elif len(free) == 3:
            ap = ap.rearrange("p (a b c) -> p a b c", a=free[0], b=free[1])
        return ap


def ACT(out, in_, func, **kw):
    return lambda e: e.activation(out=out, in_=in_, func=func, **kw)


def MM(out, lhsT, rhs, start, stop):
    return lambda e: e.matmul(out, lhsT=lhsT, rhs=rhs, start=start, stop=stop)


def TR(out, in_, ident):
    return lambda e: e.transpose(out=out, in_=in_, identity=ident)


def TT(out, in0, in1, op):
    return lambda e: e.tensor_tensor(out=out, in0=in0, in1=in1, op=op)


def TS(out, in0, s1, s2, op0, op1=None):
    if op1 is None:
        return lambda e: e.tensor_scalar(out=out, in0=in0, scalar1=s1, scalar2=None, op0=op0)
    return lambda e: e.tensor_scalar(out=out, in0=in0, scalar1=s1, scalar2=s2, op0=op0, op1=op1)


def STT(out, in0, scalar, in1, op0, op1):
    return lambda e: e.scalar_tensor_tensor(out=out, in0=in0, scalar=scalar, in1=in1, op0=op0, op1=op1)


def CP(out, in_):
    return lambda e: e.tensor_copy(out=out, in_=in_)


def MS(ap, v):
    return lambda e: e.memset(ap, v)


def SCAN(out, d0, d1, init):
    return lambda e: e.tensor_tensor_scan(out=out, data0=d0, data1=d1, initial=init, op0=ALU.mult, op1=ALU.add)


def DMA(out, in_, **kw):
    return lambda e: e.dma_start(out=out, in_=in_, **kw)


def build_program(stop_after=None, debug=False, n_groups=NG):
    nc = bass.Bass("TRN2", target_bir_lowering=False)
    kind_s = "ExternalOutput" if debug else "Internal"
    dt = nc.dram_tensor
    xs = dt("xs", [SEQ, D], F32, kind="ExternalInput").ap()
    xq = dt("xq", [NSLOT * 512, D], F32, kind="ExternalInput").ap()
    w_in = dt("w_in", [D, 2560], F32, kind="ExternalInput").ap()
    w_out = dt("w_out", [D, D], F32, kind="ExternalInput").ap()
    w_f1 = dt("w_f1", [D, 2 * DFF], F32, kind="ExternalInput").ap()
    w_f2 = dt("w_f2", [DFF, D], F32, kind="ExternalInput").ap()
    wbd_d = dt("wbd", [128, 8, 128], F32, kind="ExternalInput").ap()
    pv_d = dt("pv", [128, NV], F32, kind="ExternalInput").ap()
    nfin_d = dt("nfin", [128, D], F32, kind="ExternalInput").ap()
    cb_d = dt("cb", [128, 3, 128], BF16, kind="ExternalInput").ap()
    mk_d = dt("mk", [128, 2, 8, 512], BF16, kind="ExternalInput").ap()
    out_d = dt("out", [NSLOT * 512, D], F32, kind="ExternalOutput").ap()
    KT = dt("KT", [512, SEQ], BF16, kind=kind_s).ap()
    Vd = dt("Vd", [SEQ, 512], BF16, kind=kind_s).ap()
    QT = dt("QT", [NSLOT, 2, 512, 512], BF16, kind=kind_s).ap()
    YL = dt("YL", [NSLOT, 2, 512, 512], BF16, kind=kind_s).ap()
    YA = dt("YA", [512, NSLOT * 512], BF16, kind=kind_s).ap()
    X1 = dt("X1", [NSLOT * 512, D], F32, kind=kind_s).ap()
    W1s = dt("W1s", [128, 8, 2 * DFF], BF16, kind="Internal").ap()
    W2s = dt("W2s", [128, NFC, D], BF16, kind="Internal").ap()

    P = Prog()
    def finish(es):
        P.emit(nc, es)
        nc._prog_stats = (P.stats, P.counts)
        return nc

    with ExitStack() as es:
        arena_h = es.enter_context(nc.sbuf_tensor("arena", [128, 51968], F32))
        AR = Arena(arena_h)
        psum_h = es.enter_context(nc.psum_tensor("ps", [128, 4096], F32))
        psum16_h = psum_h.bitcast(BF16)
        banks = [psum_h[:, i * 512:(i + 1) * 512] for i in range(8)]
        banks16 = [psum16_h[:, i * 1024:(i + 1) * 1024] for i in range(8)]

        pv = AR.alloc([NV], F32)
        cb = AR.alloc([3, 128], BF16)
        ident, negtri, ones = cb[:, 0, :], cb[:, 1, :], cb[:, 2, :]
        cneg = AR.alloc([512], F32)
        cpos = AR.alloc([512], F32)
        cs = AR.alloc([4], F32)
        epsb = AR.alloc([4], F32)
        tmp4 = AR.alloc([4], F32)
        junks = [AR.alloc([1024], BF16) for _ in range(2)]
        jcnt = [0]

        def sq_accum(in_ap, acc_ap, rkeys, wkeys):
            jb = jcnt[0] % 2
            jcnt[0] += 1
            P.add("act", ACT(junks[jb], in_ap, AF.Square, accum_out=acc_ap), r=rkeys, w=list(wkeys) + [("junk", jb)])
        P.add("sp", DMA(pv, pv_d[:, :]), w=["pv"], dma="pv")
        P.add("sp", DMA(cb, cb_d[:, :, :]), w=["cb"], dma="cb")
        P.add("pool", MS(cneg, -0.5), w=["cneg"])
        P.add("pool", MS(epsb, EPS), w=["epsb"])
        P.add("pool", MS(cpos, 0.5), w=["cpos"])
        P.add("act", ACT(tmp4, pv[:, 44:48], AF.Exp, scale=-1.0), r=["pv"], w=["tmp4"])
        P.add("act", ACT(tmp4, tmp4, AF.Ln, bias=1.0), r=["tmp4"], w=["tmp4"])
        P.add("dve", TS(cs, tmp4, -8.0, None, ALU.mult), r=["tmp4"], w=["cs"])
        base_top = AR.top

        def norm_tile(xt_ap, xkey, ss_ap, sskey, rs_ap, rskey, hb_ap, hbkey):
            sq_accum(xt_ap, ss_ap, [xkey], [sskey])
            P.add("pool", TS(rs_ap, ss_ap, 1.0 / D, EPS, ALU.mult, ALU.add), r=[sskey], w=[rskey])
            P.add("pool", TT(rs_ap, rs_ap, cneg[:, 0:1], ALU.pow), r=[rskey, "cneg"], w=[rskey])
            P.add("dve", TS(hb_ap, xt_ap, rs_ap, None, ALU.mult), r=[xkey, rskey], w=[hbkey])

        def transpose_group(hb, hbkeys, hT, hTkey, gcol0, ntt):
            w_ = ntt * 128
            for cp in range(4):
                bk = cp % 2
                for ci in range(2):
                    c = cp * 2 + ci
                    for tt in range(ntt):
                        o0 = ci * w_ + tt * 128
                        P.add("pe", TR(banks16[bk][:, o0:o0 + 128], hb[:, tt, c * 128:(c + 1) * 128], ident), r=[hbkeys[tt], "cb"], w=[("pb", bk)])
                for ci in range(2):
                    c = cp * 2 + ci
                    src = banks16[bk][:, ci * w_:(ci + 1) * w_]
                    g = pv[:, gcol0 + c:gcol0 + c + 1]
                    P.add("act", ACT(hT[:, c, :], src, AF.Identity, scale=g), r=[("pb", bk), "pv"], w=[(hTkey, c)])

        Win = AR.alloc([8, 2560], BF16)
        Wbd = AR.alloc([8, 128], BF16)
        for c in range(8):
            pass
        for blk in (3, 2, 4, 1, 0):
            for c in range(8):
                P.add("pool", DMA(Win[:, c, blk * 512:(blk + 1) * 512], w_in[c * 128:(c + 1) * 128, blk * 512:(blk + 1) * 512], max_dma_last_dim=4096),
                      w=[("win", blk)], dma=("win", blk))
        P.add("pool", DMA(Wbd, wbd_d[:, :, :]), w=["wbd"], dma="wbd")
        NXT = 4
        xt = [AR.alloc([1024], F32) for _ in range(NXT)]
        ssb = AR.alloc([2, 4], F32)
        rsb = AR.alloc([2, 4], F32)
        hb = [AR.alloc([4, 1024], BF16) for _ in range(2)]
        hT = [AR.alloc([8, 512], BF16) for _ in range(2)]
        kt_o = [AR.alloc([4, 512], BF16) for _ in range(2)]
        qt_o = [AR.alloc([4, 512], BF16) for _ in range(2)]
        v_o = [AR.alloc([4, 512], BF16) for _ in range(2)]
        yn_o = [AR.alloc([4, 512], BF16) for _ in range(2)]
        xl = [AR.alloc([4, 516], F32) for _ in range(2)]
        gl = [AR.alloc([4, 512], BF16) for _ in range(2)]
        yl = AR.alloc([4, 512], F32)
        hprev = AR.alloc([4], F32)
        rl = AR.alloc([512], F32)
        L_xc = [AR.alloc([512], F32) for _ in range(2)]
        L_xcb = [AR.alloc([512], BF16) for _ in range(2)]
        L_r = [AR.alloc([512], F32) for _ in range(2)]
        L_i = [AR.alloc([512], F32) for _ in range(2)]
        L_a = [AR.alloc([512], F32) for _ in range(2)]
        L_t = [AR.alloc([512], F32) for _ in range(2)]
        L_bt = [AR.alloc([512], F32) for _ in range(2)]
        L_hl = [AR.alloc([512], F32) for _ in range(2)]
        L_sq = [AR.alloc([512], BF16) for _ in range(4)]
        for g2_ in range(2):
            P.add("pool", MS(xl[g2_], 0.0), w=[("xlm", g2_, cc) for cc in range(4)] + [("xlh", g2_, cc) for cc in range(4)])
        P.add("pool", MS(hprev, 0.0), w=[("hprev", cc) for cc in range(4)])

        st1 = {"xcnt": 0, "pjc": 0}

        def hTkeys(g2):
            return [(("hT", g2), c) for c in range(8)]

        def proj_fm(G, col):
            g2 = G % 2
            hTk = hTkeys(g2)
            bk = 2 + st1["pjc"] % 2
            st1["pjc"] += 1
            for c in range(8):
                P.add("pe", MM(banks[bk], Win[:, c, col:col + 128], hT[g2][:, c, :], c == 0, c == 7), r=[("win", col // 512), hTk[c]], w=[("pb", bk)])
            return bk

        def front_norm(G):
            g2 = G % 2
            hbk = [("hb", g2, tt) for tt in range(4)]
            for tt in range(4):
                s = st1["xcnt"] % NXT
                st1["xcnt"] += 1
                row0 = (G * 4 + tt) * 128
                P.add("sp", DMA(xt[s], xs[row0:row0 + 128, :]), w=[("xt", s)], dma=("xt", s))
                norm_tile(xt[s], ("xt", s), ssb[:, g2, tt:tt + 1], ("ss", g2, tt), rsb[:, g2, tt:tt + 1], ("rs", g2, tt), hb[g2][:, tt, :], hbk[tt])
            transpose_group(hb[g2], hbk, hT[g2], ("hT", g2), 0, 4)

        def proj_K(G):
            g2 = G % 2
            for i in range(4):
                bk = proj_fm(G, 1536 + i * 128)
                P.add("act", ACT(kt_o[g2][:, i, :], banks[bk], AF.Copy), r=[("pb", bk)], w=[("kt_o", g2, i)])
            P.add("sp", DMA(KT[:, G * 512:(G + 1) * 512].rearrange("(i p) t -> p i t", p=128), kt_o[g2]),
                  r=[("kt_o", g2, i) for i in range(4)], w=[("KT", G)], dma=("st_kt", g2))

        def proj_Q(G):
            g2 = G % 2
            for i in range(4):
                bk = proj_fm(G, 1024 + i * 128)
                P.add("act", ACT(qt_o[g2][:, i, :], banks[bk], AF.Copy, scale=0.125), r=[("pb", bk)], w=[("qt_o", g2, i)])
            P.add("sp", DMA(QT[G // 2, G % 2].rearrange("(i p) t -> p i t", p=128), qt_o[g2]),
                  r=[("qt_o", g2, i) for i in range(4)], w=[("QT", G)], dma=("st_qt", g2))

        def proj_V(G):
            g2 = G % 2
            hTk = hTkeys(g2)
            for tt in range(4):
                bk = 2 + st1["pjc"] % 2
                st1["pjc"] += 1
                for c in range(8):
                    P.add("pe", MM(banks[bk], hT[g2][:, c, tt * 128:(tt + 1) * 128], Win[:, c, 2048:2560], c == 0, c == 7), r=[("win", 4), hTk[c]], w=[("pb", bk)])
                P.add("dve", CP(v_o[g2][:, tt, :], banks[bk]), r=[("pb", bk)], w=[("v_o", g2, tt)])
            P.add("sp", DMA(Vd[G * 512:(G + 1) * 512, :].rearrange("(tt p) n -> p tt n", p=128), v_o[g2]),
                  r=[("v_o", g2, tt) for tt in range(4)], w=[("Vd", G)], dma=("st_v", g2))

        def proj_gate(G):
            g2 = G % 2
            for cc in range(4):
                bk = proj_fm(G, 512 + cc * 128)
                P.add("act", ACT(gl[g2][:, cc, :], banks[bk], AF.Gelu_apprx_tanh), r=[("pb", bk)], w=[("gl", g2, cc)])

        def proj_x(G):
            g2 = G % 2
            for cc in range(4):
                if G > 0:
                    P.add("pool", CP(xl[g2][:, cc, 0:3], xl[1 - g2][:, cc, 512:515]), r=[("xlm", 1 - g2, cc)], w=[("xlh", g2, cc)])
                bk = proj_fm(G, cc * 128)
                P.add("dve", CP(xl[g2][:, cc, 3:515], banks[bk]), r=[("pb", bk)], w=[("xlm", g2, cc)])

        def lru_s1(G, cc):
            g2 = G % 2
            lb = cc % 2
            xlg = xl[g2]
            xc, xcb = L_xc[lb], L_xcb[lb]
            kx = [("xlm", g2, cc), ("xlh", g2, cc), "pv"]
            cw = lambda tap: pv[:, 20 + cc * 4 + tap:20 + cc * 4 + tap + 1]
            P.add("pool", TS(xc, xlg[:, cc, 3:515], cw(3), pv[:, 16 + cc:17 + cc], ALU.mult, ALU.add), r=kx, w=[("xc", lb)])
            for tap in (2, 1, 0):
                P.add("dve", STT(xc, xlg[:, cc, tap:tap + 512], cw(tap), xc, ALU.mult, ALU.add), r=kx + [("xc", lb)], w=[("xc", lb)])
            P.add("act", ACT(xcb, xc, AF.Copy), r=[("xc", lb)], w=[("xcb", lb)])
            gr, gi = (5, 6) if lb == 0 else (4, 7)
            P.add("pe", MM(banks[gr], Wbd[:, cc, :], xcb, True, True), r=["wbd", ("xcb", lb)], w=[("pb", gr)])
            P.add("pe", MM(banks[gi], Wbd[:, 4 + cc, :], xcb, True, True), r=["wbd", ("xcb", lb)], w=[("pb", gi)])
            r_, i_ = L_r[lb], L_i[lb]
            P.add("act", ACT(r_, banks[gr], AF.Sigmoid, bias=pv[:, 36 + cc:37 + cc]), r=[("pb", gr), "pv"], w=[("r", lb)])
            P.add("act", ACT(i_, banks[gi], AF.Sigmoid, bias=pv[:, 40 + cc:41 + cc]), r=[("pb", gi), "pv"], w=[("i", lb)])

        def lru_s23(G, cc):
            g2 = G % 2
            lb = cc % 2
            xc, r_, i_, a_, t_, bt, hl, sq = L_xc[lb], L_r[lb], L_i[lb], L_a[lb], L_t[lb], L_bt[lb], L_hl[lb], L_sq[cc]
            P.add("act", ACT(a_, r_, AF.Exp, scale=cs[:, cc:cc + 1]), r=[("r", lb), "cs"], w=[("a", lb)])
            P.add("pool", TT(t_, a_, a_, ALU.mult), r=[("a", lb)], w=[("t", lb)])
            P.add("act", ACT(t_, t_, AF.Ln, scale=-1.0, bias=1.0), r=[("t", lb)], w=[("t", lb)])
            P.add("act", ACT(t_, t_, AF.Exp, scale=0.5), r=[("t", lb)], w=[("t", lb)])
            P.add("pool", TT(bt, i_, xc, ALU.mult), r=[("i", lb), ("xc", lb)], w=[("bt", lb)])
            P.add("pool", TT(bt, bt, t_, ALU.mult), r=[("bt", lb), ("t", lb)], w=[("bt", lb)])
            P.add("dve", SCAN(hl, a_, bt, hprev[:, cc:cc + 1]), r=[("a", lb), ("bt", lb), ("hprev", cc)], w=[("hl", lb)])
            P.add("dve", CP(hprev[:, cc:cc + 1], hl[:, 511:512]), r=[("hl", lb)], w=[("hprev", cc)])
            P.add("pool", TT(yl[:, cc, :], gl[g2][:, cc, :], hl, ALU.mult), r=[("gl", g2, cc), ("hl", lb)], w=[("yl", cc)])
            P.add("pool", TT(sq, yl[:, cc, :], yl[:, cc, :], ALU.mult), r=[("yl", cc)], w=[("sq", cc)])

        def lru_final(G):
            g2 = G % 2
            for cc in range(4):
                P.add("pe", MM(banks[7], ones, L_sq[cc], cc == 0, cc == 3), r=["cb", ("sq", cc)], w=[("pb", 7)])
            P.add("act", ACT(rl, banks[7], AF.Ln, scale=1.0 / 512, bias=epsb[:, 0:1]), r=[("pb", 7), "epsb"], w=["rl"])
            P.add("act", ACT(rl, rl, AF.Exp, scale=-0.5), r=["rl"], w=["rl"])
            for cc in range(4):
                P.add("dve", STT(yn_o[g2][:, cc, :], yl[:, cc, :], pv[:, 48 + cc:49 + cc], rl, ALU.mult, ALU.mult),
                      r=[("yl", cc), "rl", "pv"], w=[("yn_o", g2, cc)])
            P.add("sp", DMA(YL[G // 2, G % 2].rearrange("(i p) t -> p i t", p=128), yn_o[g2]),
                  r=[("yn_o", g2, cc) for cc in range(4)], w=[("YL", G)], dma=("st_yl", g2))

        def capture(fn_):
            P.cap = []
            fn_()
            lst = P.cap
            P.cap = None
            return lst

        def front_all(G):
            proj_K(G)
            proj_Q(G)
            proj_V(G)
            proj_gate(G)
            proj_x(G)

        def merge(lists):
            idx = [0] * len(lists)
            while True:
                best = None
                for k, l in enumerate(lists):
                    if idx[k] < len(l):
                        fr = idx[k] / len(l)
                        if best is None or fr < best[0]:
                            best = (fr, k)
                if best is None:
                    break
                k = best[1]
                e_ = lists[k][idx[k]]
                P.add(e_[0], e_[1], r=e_[2], w=e_[3], dma=e_[4])
                idx[k] += 1

        def lru_stream(G, ccs, fin):
            for cc in ccs:
                lru_s1(G, cc)
                lru_s23(G, cc)
            if fin:
                lru_final(G)

        front_norm(0)
        for G in range(n_groups + 1):
            lists = []
            if G < n_groups:
                lists.append(capture(lambda: front_all(G)))
            if G + 1 < n_groups:
                lists.append(capture(lambda: front_norm(G + 1)))
            if G >= 1:
                lists.append(capture(lambda: lru_stream(G - 1, (0, 2), False)))
                lists.append(capture(lambda: lru_stream(G - 1, (1, 3), False)))
            merge(lists)
            if G >= 1:
                lru_final(G - 1)

        P.barrier()
        if stop_after == "p1":
            return finish(es)

        AR.top = base_top
        Wout = AR.alloc([8, 1024], BF16)
        p2_top = AR.top
        MK = AR.alloc([2, 8, 512], BF16)
        KA = [[AR.alloc([SEQ], BF16) for _ in range(2)] for _ in range(2)]
        VP = [AR.alloc([64, 128], BF16) for _ in range(2)]
        QS = [[AR.alloc([2, 512], BF16) for _ in range(4)] for _ in range(2)]
        QC = [[AR.alloc([2, 512], BF16) for _ in range(2)] for _ in range(2)]
        EB = [AR.alloc([1024], F32) for _ in range(2)]
        SPB = [AR.alloc([1024], BF16) for _ in range(3)]
        AB = [AR.alloc([1024], BF16) for _ in range(3)]
        CF = [AR.alloc([1024], F32) for _ in range(2)]
        YO = [AR.alloc([512], BF16) for _ in range(2)]
        P.add("sp", DMA(MK, mk_d[:, :, :, :]), w=["mk"], dma="mk")
        stg = []
        for c in range(8):
            stg.append((DMA(Wout[:, c, :], w_out[c * 128:(c + 1) * 128, :], max_dma_last_dim=4096), "wout", "wout"))
        for c in range(8):
            for hf in range(2):
                stg.append((DMA(W1s[:, c, hf * DFF:(hf + 1) * DFF], w_f1[c * 128:(c + 1) * 128, hf * DFF:(hf + 1) * DFF], max_dma_last_dim=4096), "Wstg", "stg"))
        for fc in range(NFC):
            stg.append((DMA(W2s[:, fc, :], w_f2[fc * 128:(fc + 1) * 128, :], max_dma_last_dim=4096), "Wstg", "stg"))
        for pb in range(2):
            for s_ in range(2):
                P.add("pool", MS(KA[pb][s_][0:32, :], 0.0), w=[("ka", pb, s_)])
                P.add("pool", MS(KA[pb][s_][0:1, :], -1.0), w=[("ka", pb, s_)])
        for qs in range(2):
            for k in range(4):
                P.add("pool", MS(QS[qs][k][0:32, :, :], 0.0), w=[("q", qs, k)])
            for k in range(2):
                P.add("pool", MS(QC[qs][k][0:32, :, :], 0.0), w=[("qc", qs, k)])

        pjs = [(p, j) for p in range(4) for j in range(NSLOT)]
        if stop_after == "p2small":
            pjs = [(0, 0), (0, 1), (1, 0)]
        blocks = []
        for pj, (p, j) in enumerate(pjs):
            nblk = 8 * j + 8
            for n in range(nblk):
                kb = 8 * j + 7 - n
                blocks.append(dict(p=p, j=j, n=n, kb=kb, first=(n == 0), last=(n == nblk - 1), band=(kb >= 8 * j), bi=kb - 8 * j, pj=pj))
        L = len(blocks)
        allKT = [("KT", G) for G in range(NG)]
        allV = [("Vd", G) for G in range(NG)]
        allQT = [("QT", G) for G in range(NG)]
        loaded_pairs = set()

        def load_pair(p):
            if p in loaded_pairs or p >= 4:
                return
            loaded_pairs.add(p)
            pb = p % 2
            for s_ in range(2):
                h = 2 * p + s_
                P.add("sp", DMA(KA[pb][s_][32:96, :], KT[h * 64:(h + 1) * 64, :]), r=allKT, w=[("ka", pb, s_)], dma=("ld_ka", pb, s_))
            for q8 in range(8):
                P.add("sp", DMA(VP[pb][:, q8 * 8:(q8 + 1) * 8, :], Vd[q8 * 1024:(q8 + 1) * 1024, p * 128:(p + 1) * 128].rearrange("(blk k) d -> k blk d", k=128)),
                      r=allV, w=[("vp", pb)], dma=("ld_vp", pb))

        def load_q(pj_):
            p, j = pjs[pj_]
            qs = pj_ % 2
            for k in range(2):
                P.add("sp", DMA(QC[qs][k][32:96, :, :], QT[j, k, p * 128:(p + 1) * 128, :].rearrange("(s d) t -> d s t", s=2)),
                      r=allQT, w=[("qc", qs, k)], dma=("ld_q", qs, k))
            sc = 56 + 2 * (j % 2)
            qz = QS[qs][0][0:96, :, :]
            P.add("dve", TS(qz, QC[qs][0][0:96, :, :], pv[0:96, sc:sc + 1], None, ALU.mult), r=[("qc", qs, 0), "pv"], w=[("q", qs, 0)])
            P.add("dve", STT(qz, QC[qs][1][0:96, :, :], pv[0:96, sc + 1:sc + 2], qz, ALU.mult, ALU.add), r=[("qc", qs, 1), ("q", qs, 0), "pv"], w=[("q", qs, 0)])
            for k in (1, 2, 3):
                P.add("pool", CP(QS[qs][k][0:96, :, :], qz), r=[("q", qs, 0)], w=[("q", qs, k)])
            if pj_ >= 1:
                for _ in range(2):
                    if stg:
                        f_, wk_, dk_ = stg.pop(0)
                        P.add("pool", f_, w=[wk_], dma=dk_)

        def stageA_pe(i, b):
            p, j, pj = b["p"], b["j"], b["pj"]
            if i == 0:
                load_pair(p)
                load_q(0)
            if b["n"] == 2:
                if pj + 1 < len(pjs):
                    load_q(pj + 1)
                    if j >= 1 or pjs[pj + 1][0] != p:
                        load_pair(pjs[pj + 1][0] if pjs[pj + 1][0] != p else p + 1)
            pb = p % 2
            qs = pj % 2
            kb = b["kb"]
            for s_ in range(2):
                P.add("pe", MM(banks[s_], KA[pb][s_][0:96, kb * 128:(kb + 1) * 128], QS[qs][0][0:96, s_, :], True, not b["band"]),
                      r=[("ka", pb, s_), ("q", qs, 0)], w=[("pb", s_)])
                if b["band"]:
                    P.add("pe", MM(banks[s_], ident, MK[:, j % 2, b["bi"], :], False, True), r=["cb", "mk"], w=[("pb", s_)])

        def stageA_act(i, b):
            P.add("act", ACT(EB[i % 2], psum_h[:, 0:1024], AF.Exp), r=[("pb", 0), ("pb", 1)], w=[("eb", i % 2)])
            P.add("act", ACT(SPB[i % 3], EB[i % 2], AF.Ln, bias=1.0), r=[("eb", i % 2)], w=[("spb", i % 3)])

        def stageB(i, b):
            p, j, n, pj = b["p"], b["j"], b["n"], b["pj"]
            pb = p % 2
            qs = pj % 2
            kb = b["kb"]
            qa = 1 + n % 3
            qn = 1 + (n + 1) % 3
            cf = CF[pj % 2]
            if not b["last"]:
                for s_ in range(2):
                    gb = 2 + s_
                    P.add("pe", MM(banks[gb], ones, SPB[i % 3][:, s_ * 512:(s_ + 1) * 512], True, True), r=["cb", ("spb", i % 3)], w=[("pb", gb)])
                for s_ in range(2):
                    gb = 2 + s_
                    cfs = cf[0:1, s_ * 512:(s_ + 1) * 512]
                    if n == 0:
                        P.add("dve", CP(cfs, banks[gb][0:1, :]), r=[("pb", gb)], w=[("cf", pj % 2, s_)])
                    else:
                        P.add("dve", TT(cfs, cfs, banks[gb][0:1, :], ALU.add), r=[("pb", gb), ("cf", pj % 2, s_)], w=[("cf", pj % 2, s_)])
                    P.add("dve", CP(QS[qs][qn][0:1, s_, :], cfs), r=[("cf", pj % 2, s_)], w=[("q", qs, qn)])
            for s_ in range(2):
                bb = 4 + s_
                P.add("pe", MM(banks[bb], KA[pb][s_][0:96, kb * 128:(kb + 1) * 128], QS[qs][qa][0:96, s_, :], True, False),
                      r=[("ka", pb, s_), ("q", qs, qa)], w=[("pb", bb)])
                P.add("pe", MM(banks[bb], negtri, SPB[i % 3][:, s_ * 512:(s_ + 1) * 512], False, not b["band"]), r=["cb", ("spb", i % 3)], w=[("pb", bb)])
                if b["band"]:
                    P.add("pe", MM(banks[bb], ident, MK[:, j % 2, b["bi"], :], False, True), r=["cb", "mk"], w=[("pb", bb)])
            P.add("act", ACT(AB[i % 3], psum_h[:, 4 * 512:6 * 512], AF.Exp), r=[("pb", 4), ("pb", 5)], w=[("ab", i % 3)])

        def stageC(i, b):
            p, j, pj = b["p"], b["j"], b["pj"]
            pb = p % 2
            kb = b["kb"]
            for s_ in range(2):
                P.add("pe", MM(banks[6 + s_], VP[pb][:, kb, :], AB[i % 3][:, s_ * 512:(s_ + 1) * 512], b["first"], b["last"]),
                      r=[("vp", pb), ("ab", i % 3)], w=[("pb", 6 + s_)])
            if b["last"]:
                yo = pj % 2
                for s_ in range(2):
                    P.add("dve", CP(YO[yo][s_ * 64:(s_ + 1) * 64, :], banks[6 + s_][s_ * 64:(s_ + 1) * 64, :]), r=[("pb", 6 + s_)], w=[("yo", yo)])
                P.add("sp", DMA(YA[p * 128:(p + 1) * 128, j * 512:(j + 1) * 512], YO[yo]),
                      r=[("yo", yo)], w=[("YA", 2 * p, j), ("YA", 2 * p + 1, j)], dma=("st_ya", yo))

        stageA_pe(0, blocks[0])
        for it in range(L + 2):
            if it == L:
                while stg:
                    f_, wk_, dk_ = stg.pop(0)
                    P.add("pool", f_, w=[wk_], dma=dk_)
            if it < L:
                stageA_act(it, blocks[it])
            if 1 <= it <= L:
                stageB(it - 1, blocks[it - 1])
            if it >= 2:
                stageC(it - 2, blocks[it - 2])
            if it + 1 < L:
                stageA_pe(it + 1, blocks[it + 1])

        P.barrier()
        if stop_after in ("p2", "p2small"):
            return finish(es)

        AR.top = p2_top
        W1 = AR.alloc([8, 2 * DFF], BF16)
        W2 = AR.alloc([NFC, 1024], BF16)
        wloads = []
        for c in range(8):
            wloads.append((DMA(W1[:, c, :], W1s[:, c, :]), "w1", ["Wstg"]))
        for f4 in range(0, NFC, 4):
            f5 = min(NFC, f4 + 4)
            wloads.append((DMA(W2[:, f4:f5, :], W2s[:, f4:f5, :]), "w2", ["Wstg"]))
        p3_top = AR.top
        ylb = AR.alloc([4, 512], BF16)
        ylc = [AR.alloc([4, 512], BF16) for _ in range(2)]
        yab = AR.alloc([4, 512], BF16)
        yan = AR.alloc([4, 512], BF16)
        sq2 = [AR.alloc([512], BF16) for _ in range(2)]
        rl2 = AR.alloc([512], F32)
        xt2 = [AR.alloc([1024], F32) for _ in range(3)]
        xc2 = 0
        allYL = [("YL", G) for G in range(NG)]
        for j in range(NSLOT):
            for k in range(2):
                P.add("sp", DMA(ylc[k], YL[j, k].rearrange("(i p) t -> p i t", p=128)), r=allYL, w=[("ylc", k)], dma=("ld_ylc", k))
            sc = 56 + 2 * (j % 2)
            P.add("dve", TS(ylb, ylc[0], pv[:, sc:sc + 1], None, ALU.mult), r=[("ylc", 0), "pv"], w=["ylb"])
            P.add("dve", STT(ylb, ylc[1], pv[:, sc + 1:sc + 2], ylb, ALU.mult, ALU.add), r=[("ylc", 1), "ylb", "pv"], w=["ylb"])
            P.add("sp", DMA(yab, YA[:, j * 512:(j + 1) * 512].rearrange("(i p) t -> p i t", p=128)),
                  r=[("YA", h, j) for h in range(8)], w=["yab", ("p25slot", j)], dma="ld_yab")
            nw = 3 if j < 4 else len(wloads)
            for fn_, key_, rk_ in wloads[:nw]:
                P.add("sp", fn_, r=[("p25slot", j)] + rk_, w=[key_], dma=key_)
            wloads = wloads[nw:]
            for p in range(4):
                P.add("pool", TT(sq2[p % 2], yab[:, p, :], yab[:, p, :], ALU.mult), r=["yab"], w=[("sq2", p % 2)])
                P.add("pe", MM(banks[7], ones, sq2[p % 2], p == 0, p == 3), r=["cb", ("sq2", p % 2)], w=[("pb", 7)])
            P.add("act", ACT(rl2, banks[7], AF.Ln, scale=1.0 / 512, bias=epsb[:, 0:1]), r=[("pb", 7), "epsb"], w=["rl2"])
            P.add("act", ACT(rl2, rl2, AF.Exp, scale=-0.5), r=["rl2"], w=["rl2"])
            for p in range(4):
                P.add("dve", STT(yan[:, p, :], yab[:, p, :], pv[:, 52 + p:53 + p], rl2, ALU.mult, ALU.mult), r=["yab", "rl2", "pv"], w=[("yan", p)])
            for tt in range(4):
                s = xc2 % 3
                xc2 += 1
                row0 = j * 512 + tt * 128
                P.add("sp", DMA(xt2[s], xq[row0:row0 + 128, :]), w=[("xt2", s)], dma=("xt2", s))
                for hf in range(2):
                    bk = (tt * 2 + hf) % 4
                    for k in range(8):
                        src = ylb[:, k, tt * 128:(tt + 1) * 128] if k < 4 else yan[:, k - 4, tt * 128:(tt + 1) * 128]
                        rk = "ylb" if k < 4 else ("yan", k - 4)
                        P.add("pe", MM(banks[bk], src, Wout[:, k, hf * 512:(hf + 1) * 512], k == 0, k == 7), r=["wout", rk], w=[("pb", bk)])
                    P.add("dve", TT(xt2[s][:, hf * 512:(hf + 1) * 512], xt2[s][:, hf * 512:(hf + 1) * 512], banks[bk], ALU.add),
                          r=[("pb", bk), ("xt2", s)], w=[("xt2", s)])
                P.add("sp", DMA(X1[row0:row0 + 128, :], xt2[s]), r=[("xt2", s)], w=[("X1", j * 4 + tt)], dma=("st_x1", s))

        P.barrier()
        if stop_after == "p25":
            return finish(es)

        AR.top = base_top
        x1t_b = AR.alloc([NT3, 1024], F32)
        hb3_b = AR.alloc([NT3, 1024], BF16)
        hT3_b = AR.alloc([8, TG], BF16)
        assert AR.top <= p2_top
        AR.top = p3_top
        nfin = AR.alloc([1024], F32)
        x1t = [AR.alloc([NT3, 1024], F32), x1t_b]
        hb3 = [AR.alloc([NT3, 1024], BF16), hb3_b]
        hT3 = [AR.alloc([8, TG], BF16), hT3_b]
        aT = AR.alloc([NFC, TG], BF16)
        sgb = [AR.alloc([TG], F32) for _ in range(2)]
        ot = [AR.alloc([1024], F32) for _ in range(2)]
        ss3 = AR.alloc([2, NT3], F32)
        rs3 = AR.alloc([2, NT3], F32)
        ss4 = AR.alloc([2, NT3], F32)
        rs4 = AR.alloc([2, NT3], F32)
        P.add("sp", DMA(nfin, nfin_d[:, :]), w=["nfin"], dma="nfin")
        st3 = {"oc": 0}
        n_it = NSLOT * 512 // TG

        def p3_pre(it3):
            j2 = it3 % 2
            hbk = [("hb3", j2, tt) for tt in range(NT3)]
            for tt in range(NT3):
                tile_id = it3 * NT3 + tt
                row0 = tile_id * 128
                P.add("sp", DMA(x1t[j2][:, tt, :], X1[row0:row0 + 128, :]), r=[("X1", tile_id)], w=[("x1t", j2, tt)], dma=("ld_x1", j2, tt))
                norm_tile(x1t[j2][:, tt, :], ("x1t", j2, tt), ss3[:, j2, tt:tt + 1], ("ss3", j2, tt), rs3[:, j2, tt:tt + 1], ("rs3", j2, tt), hb3[j2][:, tt, :], hbk[tt])
            transpose_group(hb3[j2], hbk, hT3[j2], ("hT3", j2), 8, NT3)

        def p3_main(it3):
            j2 = it3 % 2
            hTk = [(("hT3", j2), c) for c in range(8)]
            for fc in range(NFC):
                gb = 2 + fc % 2
                ub = 4 + fc % 2
                for c in range(8):
                    P.add("pe", MM(banks[gb][:, 0:TG], W1[:, c, fc * 128:(fc + 1) * 128], hT3[j2][:, c, :], c == 0, c == 7), r=["w1", hTk[c]], w=[("pb", gb)])
                for c in range(8):
                    P.add("pe", MM(banks[ub][:, 0:TG], W1[:, c, DFF + fc * 128:DFF + (fc + 1) * 128], hT3[j2][:, c, :], c == 0, c == 7), r=["w1", hTk[c]], w=[("pb", ub)])
                P.add("act", ACT(sgb[fc % 2], banks[gb][:, 0:TG], AF.Silu), r=[("pb", gb)], w=[("sgb", fc % 2)])
                P.add("dve", TT(aT[:, fc, :], sgb[fc % 2], banks[ub][:, 0:TG], ALU.mult), r=[("sgb", fc % 2), ("pb", ub)], w=[("aT", fc)])
            for tt in range(NT3):
                xk = ("x1t", j2, tt)
                for hf in range(2):
                    ob = 6 + (tt * 2 + hf) % 2
                    for fc in range(NFC):
                        P.add("pe", MM(banks[ob], aT[:, fc, tt * 128:(tt + 1) * 128], W2[:, fc, hf * 512:(hf + 1) * 512], fc == 0, fc == NFC - 1), r=["w2", ("aT", fc)], w=[("pb", ob)])
                    P.add("dve", TT(x1t[j2][:, tt, hf * 512:(hf + 1) * 512], x1t[j2][:, tt, hf * 512:(hf + 1) * 512], banks[ob], ALU.add),
                          r=[("pb", ob), xk], w=[xk])
                o = st3["oc"] % 2
                st3["oc"] += 1
                sq_accum(x1t[j2][:, tt, :], ss4[:, j2, tt:tt + 1], [xk], [("ss4", j2, tt)])
                P.add("pool", TS(rs4[:, j2, tt:tt + 1], ss4[:, j2, tt:tt + 1], 1.0 / D, EPS, ALU.mult, ALU.add), r=[("ss4", j2, tt)], w=[("rs4", j2, tt)])
                P.add("pool", TT(rs4[:, j2, tt:tt + 1], rs4[:, j2, tt:tt + 1], cneg[:, 0:1], ALU.pow), r=[("rs4", j2, tt), "cneg"], w=[("rs4", j2, tt)])
                P.add("dve", STT(ot[o], x1t[j2][:, tt, :], rs4[:, j2, tt:tt + 1], nfin, ALU.mult, ALU.mult), r=[xk, ("rs4", j2, tt), "nfin"], w=[("ot", o)])
                row0 = (it3 * NT3 + tt) * 128
                P.add("sp", DMA(out_d[row0:row0 + 128, :], ot[o]), r=[("ot", o)], w=[("out", it3, tt)], dma=("st_out", o))

        p3_pre(0)
        for it3 in range(n_it):
            lists = [capture(lambda: p3_main(it3))]
            if it3 + 1 < n_it:
                lists.append(capture(lambda: p3_pre(it3 + 1)))
            merge(lists)

        P.barrier()
        return finish(es)


def _slot_groups(par):
    return [2 * j + (par ^ (j & 1)) for j in range(NSLOT)]


def _consts():
    bf = ml_dtypes.bfloat16
    cb = np.zeros((128, 3, 128), np.float32)
    cb[:, 0, :] = np.eye(128)
    jj = np.arange(128)[:, None]
    s_ = np.arange(128)[None, :]
    cb[:, 1, :] = np.where(jj >= s_, -1.0, 0.0)
    cb[:, 2, :] = 1.0
    s = np.arange(128)[:, None, None]
    i = np.arange(8)[None, :, None]
    t = np.arange(512)[None, None, :]
    kpos = 128 * i + s
    m_min = np.where(kpos < t, 0.0, NEG)
    m_max = np.where(kpos < 512 + t, 0.0, NEG)
    return cb.astype(bf), m_min.astype(bf), m_max.astype(bf)


def _make_in_maps(inputs):
    f = lambda a: np.ascontiguousarray(np.asarray(a, dtype=np.float32))
    x = f(inputs["x"])
    w_in = f(inputs["w_in"][0])
    w_out = f(inputs["w_out"][0])
    w_f1 = f(inputs["w_ffn_in"][0])
    w_f2 = f(inputs["w_ffn_out"][0])
    pv = np.zeros((128, NV), np.float32)
    pv[:, 0:8] = f(inputs["norm_mix"][0]).reshape(8, 128).T
    pv[:, 8:16] = f(inputs["norm_ffn"][0]).reshape(8, 128).T
    pv[:, 16:20] = f(inputs["conv_b"][0]).reshape(4, 128).T
    cw = f(inputs["conv_w"][0])
    for cc in range(4):
        pv[:, 20 + cc * 4:24 + cc * 4] = cw[:, cc * 128:(cc + 1) * 128].T
    pv[:, 36:40] = f(inputs["b_rg"][0]).reshape(4, 128).T
    pv[:, 40:44] = f(inputs["b_ig"][0]).reshape(4, 128).T
    pv[:, 44:48] = f(inputs["lru_lambda"][0]).reshape(4, 128).T
    pv[:, 48:52] = f(inputs["norm_lru_out"][0]).reshape(4, 128).T
    pv[:, 52:56] = f(inputs["norm_att_out"][0]).reshape(4, 128).T
    wbd = np.zeros((128, 8, 128), np.float32)
    wrg = f(inputs["w_rg"][0])
    wig = f(inputs["w_ig"][0])
    for cc in range(4):
        for hb_ in range(2):
            blk = 2 * cc + hb_
            wbd[hb_ * 64:(hb_ + 1) * 64, cc, hb_ * 64:(hb_ + 1) * 64] = wrg[blk]
            wbd[hb_ * 64:(hb_ + 1) * 64, 4 + cc, hb_ * 64:(hb_ + 1) * 64] = wig[blk]
    nfin = np.ascontiguousarray(np.broadcast_to(f(inputs["norm_final"])[None, :], (128, D)))
    cb, m_min, m_max = _consts()
    maps = []
    for c in range(N_CORES):
        b, par = c // 2, c % 2
        groups = _slot_groups(par)
        xq = np.concatenate([x[b, g * 512:(g + 1) * 512] for g in groups], axis=0)
        mk = np.stack([m_min, m_max] if par == 0 else [m_max, m_min], axis=1)
        pvc = pv.copy()
        pvc[:, 56:60] = np.array([1 - par, par, par, 1 - par], np.float32)[None, :]
        maps.append({"xs": np.ascontiguousarray(x[b]), "xq": np.ascontiguousarray(xq), "w_in": w_in, "w_out": w_out,
                     "w_f1": w_f1, "w_f2": w_f2, "wbd": wbd, "pv": pvc, "nfin": nfin, "cb": cb,
                     "mk": np.ascontiguousarray(mk)})
    return maps


_NC_CACHE = {}


def kernel(**inputs):
    maps = _make_in_maps(inputs)
    if "nc" not in _NC_CACHE:
        _NC_CACHE["nc"] = build_program()
    nc = _NC_CACHE["nc"]
    res = run_bass_kernel_spmd(nc, maps, core_ids=list(range(N_CORES)))
    out = np.zeros((4, SEQ, D), np.float32)
    for c in range(N_CORES):
        b, par = c // 2, c % 2
        o = np.asarray(res.results[c]["out"], dtype=np.float32)
        for j, g in enumerate(_slot_groups(par)):
            out[b, g * 512:(g + 1) * 512] = o[j * 512:(j + 1) * 512]
    return out
```

```python
import numpy as np
import ml_dtypes
from contextlib import ExitStack
import concourse.bass as bass
import concourse.mybir as mybir
from concourse.bass_utils import run_bass_kernel_spmd

F32 = mybir.dt.float32
BF16 = mybir.dt.bfloat16
AF = mybir.ActivationFunctionType
ALU = mybir.AluOpType

D = 1024
SEQ = 8192
NG = 16
NSLOT = 8
DFF = 2816
NFC = 22
EPS = 1e-6
NEG = -30000.0
NV = 64
N_CORES = 8
TG = 256
NT3 = TG // 128


class _Op:
    __slots__ = ("eng", "fn", "deps", "key", "sig", "signal")


class Prog:
    def __init__(self):
        self.ops = []
        self.lastw = {}
        self.rd = {}
        self.last_eng = {}
        self.last_dma = {}

    def add(self, eng, fn, r=(), w=(), dma=None):
        if getattr(self, "cap", None) is not None:
            self.cap.append((eng, fn, tuple(r), tuple(w), dma))
            return -1
        i = len(self.ops)
        xr = [k for k in r if isinstance(k, tuple) and k[0] == "pb"]
        if xr:
            w = list(w) + xr
        deps = set()
        for k in r:
            j = self.lastw.get(k)
            if j is not None:
                deps.add(j)
        for k in w:
            j = self.lastw.get(k)
            if j is not None:
                deps.add(j)
            rk = self.rd.get(k)
            if rk:
                deps.update(rk.values())
        for k in r:
            rk = self.rd.setdefault(k, {})
            rk[("dma", i) if dma is not None else eng] = i
        for k in w:
            self.lastw[k] = i
            self.rd[k] = {}
        op = _Op()
        op.eng = eng
        op.fn = fn
        op.deps = deps
        op.key = dma
        op.sig = None
        op.signal = dma is not None
        self.ops.append(op)
        if dma is not None:
            self.last_dma[dma] = i
        else:
            self.last_eng[eng] = i
        return i

    def barrier(self, engines=("pe", "act", "dve", "pool", "sp")):
        deps = set(self.last_eng.values()) | set(self.last_dma.values())
        for e in engines:
            op = _Op()
            op.eng = e
            op.fn = None
            op.deps = set(deps)
            op.key = None
            op.sig = None
            op.signal = False
            self.ops.append(op)

    def emit(self, nc, es, pre_sp=None):
        ops = self.ops
        for op in ops:
            latest = {}
            for d in op.deps:
                D_ = ops[d]
                if D_.key is None and D_.fn is not None:
                    if D_.eng == "pe" and op.eng == "pe" and op.key is None:
                        continue
                    if d > latest.get(D_.eng, -1):
                        latest[D_.eng] = d
            for d in latest.values():
                ops[d].signal = True
        cnt = {}
        for op in ops:
            if op.fn is None:
                continue
            if op.key is not None:
                k = ("dma", op.key)
                cnt[k] = cnt.get(k, 0) + 16
                op.sig = cnt[k]
            elif op.signal:
                cnt[op.eng] = cnt.get(op.eng, 0) + 1
                op.sig = cnt[op.eng]
        self.counts = cnt
        sems = {}
        for n_, k in enumerate(cnt.keys()):
            sems[k] = es.enter_context(nc.semaphore("s%d" % n_))
        block = es.enter_context(nc.Block())
        stats = {}

        def run(engname, e):
            waited = {}
            nw = 0
            ni = 0
            for op in ops:
                if op.eng != engname:
                    continue
                need = {}
                for d in op.deps:
                    D_ = ops[d]
                    if D_.fn is None:
                        continue
                    if D_.key is None:
                        if D_.eng == "pe" and engname == "pe" and op.key is None:
                            continue
                        k = D_.eng
                        if D_.sig is None:
                            continue
                    else:
                        k = ("dma", D_.key)
                    if D_.sig > need.get(k, 0):
                        need[k] = D_.sig
                for k, v in need.items():
                    if waited.get(k, 0) >= v:
                        continue
                    e.wait_ge(sems[k], v)
                    waited[k] = v
                    nw += 1
                if op.fn is not None:
                    ins = op.fn(e)
                    ni += 1
                    if op.key is not None:
                        ins.then_inc(sems[("dma", op.key)], 16)
                    elif op.signal:
                        ins.then_inc(sems[op.eng], 1)
            stats[engname] = (ni, nw)

        @block.tensor
        def _(t):
            run("pe", t)

        @block.scalar
        def _(a):
            run("act", a)

        @block.vector
        def _(v):
            run("dve", v)

        @block.gpsimd
        def _(g):
            run("pool", g)

        @block.sync
        def _(s):
            if pre_sp is not None:
                pre_sp(s)
            run("sp", s)

        self.stats = stats


class Arena:
    def __init__(self, h32):
        self.h32 = h32
        self.h16 = h32.bitcast(BF16)
        self.top = 0
        self.cap = h32.shape[1] * 4

    def alloc(self, free, dtype):
        n = 1
        for f in free:
            n *= f
        sz = 4 if dtype == F32 else 2
        off = self.top
        nb = (n * sz + 63) // 64 * 64
        self.top += nb
        assert self.top <= self.cap, ("SBUF arena overflow", self.top, self.cap)
        if dtype == F32:
            ap = self.h32[:, off // 4: off // 4 + n]
        else:
            ap = self.h16[:, off // 2: off // 2 + n]
        if len(free) == 2:
            ap = ap.rearrange("p (a b) -> p a b", a=free[0])
        elif len(free) == 3:
            ap = ap.rearrange("p (a b c) -> p a b c", a=free[0], b=free[1])
        return ap


def ACT(out, in_, func, **kw):
    return lambda e: e.activation(out=out, in_=in_, func=func, **kw)


def MM(out, lhsT, rhs, start, stop):
    return lambda e: e.matmul(out, lhsT=lhsT, rhs=rhs, start=start, stop=stop)


def TR(out, in_, ident):
    return lambda e: e.transpose(out=out, in_=in_, identity=ident)


def TT(out, in0, in1, op):
    return lambda e: e.tensor_tensor(out=out, in0=in0, in1=in1, op=op)


def TS(out, in0, s1, s2, op0, op1=None):
    if op1 is None:
        return lambda e: e.tensor_scalar(out=out, in0=in0, scalar1=s1, scalar2=None, op0=op0)
    return lambda e: e.tensor_scalar(out=out, in0=in0, scalar1=s1, scalar2=s2, op0=op0, op1=op1)


def STT(out, in0, scalar, in1, op0, op1):
    return lambda e: e.scalar_tensor_tensor(out=out, in0=in0, scalar=scalar, in1=in1, op0=op0, op1=op1)


def CP(out, in_):
    return lambda e: e.tensor_copy(out=out, in_=in_)


def MS(ap, v):
    return lambda e: e.memset(ap, v)


def SCAN(out, d0, d1, init):
    return lambda e: e.tensor_tensor_scan(out=out, data0=d0, data1=d1, initial=init, op0=ALU.mult, op1=ALU.add)


def DMA(out, in_, **kw):
    return lambda e: e.dma_start(out=out, in_=in_, **kw)


def build_program(stop_after=None, debug=False, n_groups=NG):
    nc = bass.Bass("TRN2", target_bir_lowering=False)
    kind_s = "ExternalOutput" if debug else "Internal"
    dt = nc.dram_tensor
    xs = dt("xs", [SEQ, D], F32, kind="ExternalInput").ap()
    xq = dt("xq", [NSLOT * 512, D], F32, kind="ExternalInput").ap()
    w_in = dt("w_in", [D, 2560], F32, kind="ExternalInput").ap()
    w_out = dt("w_out", [D, D], F32, kind="ExternalInput").ap()
    w_f1 = dt("w_f1", [D, 2 * DFF], F32, kind="ExternalInput").ap()
    w_f2 = dt("w_f2", [DFF, D], F32, kind="ExternalInput").ap()
    wbd_d = dt("wbd", [128, 8, 128], F32, kind="ExternalInput").ap()
    pv_d = dt("pv", [128, NV], F32, kind="ExternalInput").ap()
    nfin_d = dt("nfin", [128, D], F32, kind="ExternalInput").ap()
    cb_d = dt("cb", [128, 3, 128], BF16, kind="ExternalInput").ap()
    mk_d = dt("mk", [128, 2, 8, 512], BF16, kind="ExternalInput").ap()
    out_d = dt("out", [NSLOT * 512, D], F32, kind="ExternalOutput").ap()
    KT = dt("KT", [512, SEQ], BF16, kind=kind_s).ap()
    Vd = dt("Vd", [SEQ, 512], BF16, kind=kind_s).ap()
    QT = dt("QT", [NSLOT, 2, 512, 512], BF16, kind=kind_s).ap()
    YL = dt("YL", [NSLOT, 2, 512, 512], BF16, kind=kind_s).ap()
    YA = dt("YA", [512, NSLOT * 512], BF16, kind=kind_s).ap()
    X1 = dt("X1", [NSLOT * 512, D], F32, kind=kind_s).ap()
    W1s = dt("W1s", [128, 8, 2 * DFF], BF16, kind="Internal").ap()
    W2s = dt("W2s", [128, NFC, D], BF16, kind="Internal").ap()

    P = Prog()
    def finish(es):
        P.emit(nc, es)
        nc._prog_stats = (P.stats, P.counts)
        return nc

    with ExitStack() as es:
        arena_h = es.enter_context(nc.sbuf_tensor("arena", [128, 51968], F32))
        AR = Arena(arena_h)
        psum_h = es.enter_context(nc.psum_tensor("ps", [128, 4096], F32))
        psum16_h = psum_h.bitcast(BF16)
        banks = [psum_h[:, i * 512:(i + 1) * 512] for i in range(8)]
        banks16 = [psum16_h[:, i * 1024:(i + 1) * 1024] for i in range(8)]

        pv = AR.alloc([NV], F32)
        cb = AR.alloc([3, 128], BF16)
        ident, negtri, ones = cb[:, 0, :], cb[:, 1, :], cb[:, 2, :]
        cneg = AR.alloc([512], F32)
        cpos = AR.alloc([512], F32)
        cs = AR.alloc([4], F32)
        epsb = AR.alloc([4], F32)
        tmp4 = AR.alloc([4], F32)
        junks = [AR.alloc([1024], BF16) for _ in range(2)]
        jcnt = [0]

        def sq_accum(in_ap, acc_ap, rkeys, wkeys):
            jb = jcnt[0] % 2
            jcnt[0] += 1
            P.add("act", ACT(junks[jb], in_ap, AF.Square, accum_out=acc_ap), r=rkeys, w=list(wkeys) + [("junk", jb)])
        P.add("sp", DMA(pv, pv_d[:, :]), w=["pv"], dma="pv")
        P.add("sp", DMA(cb, cb_d[:, :, :]), w=["cb"], dma="cb")
        P.add("pool", MS(cneg, -0.5), w=["cneg"])
        P.add("pool", MS(epsb, EPS), w=["epsb"])
        P.add("pool", MS(cpos, 0.5), w=["cpos"])
        P.add("act", ACT(tmp4, pv[:, 44:48], AF.Exp, scale=-1.0), r=["pv"], w=["tmp4"])
        P.add("act", ACT(tmp4, tmp4, AF.Ln, bias=1.0), r=["tmp4"], w=["tmp4"])
        P.add("dve", TS(cs, tmp4, -8.0, None, ALU.mult), r=["tmp4"], w=["cs"])
        base_top = AR.top

        def norm_tile(xt_ap, xkey, ss_ap, sskey, rs_ap, rskey, hb_ap, hbkey):
            sq_accum(xt_ap, ss_ap, [xkey], [sskey])
            P.add("pool", TS(rs_ap, ss_ap, 1.0 / D, EPS, ALU.mult, ALU.add), r=[sskey], w=[rskey])
            P.add("pool", TT(rs_ap, rs_ap, cneg[:, 0:1], ALU.pow), r=[rskey, "cneg"], w=[rskey])
            P.add("dve", TS(hb_ap, xt_ap, rs_ap, None, ALU.mult), r=[xkey, rskey], w=[hbkey])

        def transpose_group(hb, hbkeys, hT, hTkey, gcol0, ntt):
            w_ = ntt * 128
            for cp in range(4):
                bk = cp % 2
                for ci in range(2):
                    c = cp * 2 + ci
                    for tt in range(ntt):
                        o0 = ci * w_ + tt * 128
                        P.add("pe", TR(banks16[bk][:, o0:o0 + 128], hb[:, tt, c * 128:(c + 1) * 128], ident), r=[hbkeys[tt], "cb"], w=[("pb", bk)])
                for ci in range(2):
                    c = cp * 2 + ci
                    src = banks16[bk][:, ci * w_:(ci + 1) * w_]
                    g = pv[:, gcol0 + c:gcol0 + c + 1]
                    P.add("act", ACT(hT[:, c, :], src, AF.Identity, scale=g), r=[("pb", bk), "pv"], w=[(hTkey, c)])

        Win = AR.alloc([8, 2560], BF16)
        Wbd = AR.alloc([8, 128], BF16)
        for c in range(8):
            P.add("pool", DMA(Win[:, c, :], w_in[c * 128:(c + 1) * 128, :], max_dma_last_dim=4096), w=["win"], dma="win")
        P.add("pool", DMA(Wbd, wbd_d[:, :, :]), w=["wbd"], dma="wbd")
        NXT = 4
        xt = [AR.alloc([1024], F32) for _ in range(NXT)]
        ssb = AR.alloc([2, 4], F32)
        rsb = AR.alloc([2, 4], F32)
        hb = [AR.alloc([4, 1024], BF16) for _ in range(2)]
        hT = [AR.alloc([8, 512], BF16) for _ in range(2)]
        kt_o = [AR.alloc([4, 512], BF16) for _ in range(2)]
        qt_o = [AR.alloc([4, 512], BF16) for _ in range(2)]
        v_o = [AR.alloc([4, 512], BF16) for _ in range(2)]
        yn_o = [AR.alloc([4, 512], BF16) for _ in range(2)]
        xl = [AR.alloc([4, 516], F32) for _ in range(2)]
        gl = [AR.alloc([4, 512], BF16) for _ in range(2)]
        yl = AR.alloc([4, 512], F32)
        hprev = AR.alloc([4], F32)
        rl = AR.alloc([512], F32)
        L_xc = [AR.alloc([512], F32) for _ in range(2)]
        L_xcb = [AR.alloc([512], BF16) for _ in range(2)]
        L_r = [AR.alloc([512], F32) for _ in range(2)]
        L_i = [AR.alloc([512], F32) for _ in range(2)]
        L_a = [AR.alloc([512], F32) for _ in range(2)]
        L_t = [AR.alloc([512], F32) for _ in range(2)]
        L_bt = [AR.alloc([512], F32) for _ in range(2)]
        L_hl = [AR.alloc([512], F32) for _ in range(2)]
        L_sq = [AR.alloc([512], BF16) for _ in range(4)]
        for g2_ in range(2):
            P.add("pool", MS(xl[g2_], 0.0), w=[("xlm", g2_, cc) for cc in range(4)] + [("xlh", g2_, cc) for cc in range(4)])
        P.add("pool", MS(hprev, 0.0), w=[("hprev", cc) for cc in range(4)])

        st1 = {"xcnt": 0, "pjc": 0}

        def hTkeys(g2):
            return [(("hT", g2), c) for c in range(8)]

        def proj_fm(G, col):
            g2 = G % 2
            hTk = hTkeys(g2)
            bk = 2 + st1["pjc"] % 2
            st1["pjc"] += 1
            for c in range(8):
                P.add("pe", MM(banks[bk], Win[:, c, col:col + 128], hT[g2][:, c, :], c == 0, c == 7), r=["win", hTk[c]], w=[("pb", bk)])
            return bk

        def front_norm(G):
            g2 = G % 2
            hbk = [("hb", g2, tt) for tt in range(4)]
            for tt in range(4):
                s = st1["xcnt"] % NXT
                st1["xcnt"] += 1
                row0 = (G * 4 + tt) * 128
                P.add("sp", DMA(xt[s], xs[row0:row0 + 128, :]), w=[("xt", s)], dma=("xt", s))
                norm_tile(xt[s], ("xt", s), ssb[:, g2, tt:tt + 1], ("ss", g2, tt), rsb[:, g2, tt:tt + 1], ("rs", g2, tt), hb[g2][:, tt, :], hbk[tt])
            transpose_group(hb[g2], hbk, hT[g2], ("hT", g2), 0, 4)

        def proj_K(G):
            g2 = G % 2
            for i in range(4):
                bk = proj_fm(G, 1536 + i * 128)
                P.add("act", ACT(kt_o[g2][:, i, :], banks[bk], AF.Copy), r=[("pb", bk)], w=[("kt_o", g2, i)])
            P.add("sp", DMA(KT[:, G * 512:(G + 1) * 512].rearrange("(i p) t -> p i t", p=128), kt_o[g2]),
                  r=[("kt_o", g2, i) for i in range(4)], w=[("KT", G)], dma=("st_kt", g2))

        def proj_Q(G):
            g2 = G % 2
            for i in range(4):
                bk = proj_fm(G, 1024 + i * 128)
                P.add("act", ACT(qt_o[g2][:, i, :], banks[bk], AF.Copy, scale=0.125), r=[("pb", bk)], w=[("qt_o", g2, i)])
            P.add("sp", DMA(QT[G // 2, G % 2].rearrange("(i p) t -> p i t", p=128), qt_o[g2]),
                  r=[("qt_o", g2, i) for i in range(4)], w=[("QT", G)], dma=("st_qt", g2))

        def proj_V(G):
            g2 = G % 2
            hTk = hTkeys(g2)
            for tt in range(4):
                bk = 2 + st1["pjc"] % 2
                st1["pjc"] += 1
                for c in range(8):
                    P.add("pe", MM(banks[bk], hT[g2][:, c, tt * 128:(tt + 1) * 128], Win[:, c, 2048:2560], c == 0, c == 7), r=["win", hTk[c]], w=[("pb", bk)])
                P.add("dve", CP(v_o[g2][:, tt, :], banks[bk]), r=[("pb", bk)], w=[("v_o", g2, tt)])
            P.add("sp", DMA(Vd[G * 512:(G + 1) * 512, :].rearrange("(tt p) n -> p tt n", p=128), v_o[g2]),
                  r=[("v_o", g2, tt) for tt in range(4)], w=[("Vd", G)], dma=("st_v", g2))

        def proj_gate(G):
            g2 = G % 2
            for cc in range(4):
                bk = proj_fm(G, 512 + cc * 128)
                P.add("act", ACT(gl[g2][:, cc, :], banks[bk], AF.Gelu_apprx_tanh), r=[("pb", bk)], w=[("gl", g2, cc)])

        def proj_x(G):
            g2 = G % 2
            for cc in range(4):
                if G > 0:
                    P.add("pool", CP(xl[g2][:, cc, 0:3], xl[1 - g2][:, cc, 512:515]), r=[("xlm", 1 - g2, cc)], w=[("xlh", g2, cc)])
                bk = proj_fm(G, cc * 128)
                P.add("dve", CP(xl[g2][:, cc, 3:515], banks[bk]), r=[("pb", bk)], w=[("xlm", g2, cc)])

        def lru_s1(G, cc):
            g2 = G % 2
            lb = cc % 2
            xlg = xl[g2]
            xc, xcb = L_xc[lb], L_xcb[lb]
            kx = [("xlm", g2, cc), ("xlh", g2, cc), "pv"]
            cw = lambda tap: pv[:, 20 + cc * 4 + tap:20 + cc * 4 + tap + 1]
            P.add("pool", TS(xc, xlg[:, cc, 3:515], cw(3), pv[:, 16 + cc:17 + cc], ALU.mult, ALU.add), r=kx, w=[("xc", lb)])
            for tap in (2, 1, 0):
                P.add("dve", STT(xc, xlg[:, cc, tap:tap + 512], cw(tap), xc, ALU.mult, ALU.add), r=kx + [("xc", lb)], w=[("xc", lb)])
            P.add("act", ACT(xcb, xc, AF.Copy), r=[("xc", lb)], w=[("xcb", lb)])
            gr, gi = (5, 6) if lb == 0 else (4, 7)
            P.add("pe", MM(banks[gr], Wbd[:, cc, :], xcb, True, True), r=["wbd", ("xcb", lb)], w=[("pb", gr)])
            P.add("pe", MM(banks[gi], Wbd[:, 4 + cc, :], xcb, True, True), r=["wbd", ("xcb", lb)], w=[("pb", gi)])
            r_, i_ = L_r[lb], L_i[lb]
            P.add("act", ACT(r_, banks[gr], AF.Sigmoid, bias=pv[:, 36 + cc:37 + cc]), r=[("pb", gr), "pv"], w=[("r", lb)])
            P.add("act", ACT(i_, banks[gi], AF.Sigmoid, bias=pv[:, 40 + cc:41 + cc]), r=[("pb", gi), "pv"], w=[("i", lb)])

        def lru_s23(G, cc):
            g2 = G % 2
            lb = cc % 2
            xc, r_, i_, a_, t_, bt, hl, sq = L_xc[lb], L_r[lb], L_i[lb], L_a[lb], L_t[lb], L_bt[lb], L_hl[lb], L_sq[cc]
            P.add("act", ACT(a_, r_, AF.Exp, scale=cs[:, cc:cc + 1]), r=[("r", lb), "cs"], w=[("a", lb)])
            P.add("pool", TT(t_, a_, a_, ALU.mult), r=[("a", lb)], w=[("t", lb)])
            P.add("act", ACT(t_, t_, AF.Ln, scale=-1.0, bias=1.0), r=[("t", lb)], w=[("t", lb)])
            P.add("act", ACT(t_, t_, AF.Exp, scale=0.5), r=[("t", lb)], w=[("t", lb)])
            P.add("pool", TT(bt, i_, xc, ALU.mult), r=[("i", lb), ("xc", lb)], w=[("bt", lb)])
            P.add("pool", TT(bt, bt, t_, ALU.mult), r=[("bt", lb), ("t", lb)], w=[("bt", lb)])
            P.add("dve", SCAN(hl, a_, bt, hprev[:, cc:cc + 1]), r=[("a", lb), ("bt", lb), ("hprev", cc)], w=[("hl", lb)])
            P.add("dve", CP(hprev[:, cc:cc + 1], hl[:, 511:512]), r=[("hl", lb)], w=[("hprev", cc)])
            P.add("pool", TT(yl[:, cc, :], gl[g2][:, cc, :], hl, ALU.mult), r=[("gl", g2, cc), ("hl", lb)], w=[("yl", cc)])
            P.add("pool", TT(sq, yl[:, cc, :], yl[:, cc, :], ALU.mult), r=[("yl", cc)], w=[("sq", cc)])

        def lru_final(G):
            g2 = G % 2
            for cc in range(4):
                P.add("pe", MM(banks[7], ones, L_sq[cc], cc == 0, cc == 3), r=["cb", ("sq", cc)], w=[("pb", 7)])
            P.add("act", ACT(rl, banks[7], AF.Ln, scale=1.0 / 512, bias=epsb[:, 0:1]), r=[("pb", 7), "epsb"], w=["rl"])
            P.add("act", ACT(rl, rl, AF.Exp, scale=-0.5), r=["rl"], w=["rl"])
            for cc in range(4):
                P.add("dve", STT(yn_o[g2][:, cc, :], yl[:, cc, :], pv[:, 48 + cc:49 + cc], rl, ALU.mult, ALU.mult),
                      r=[("yl", cc), "rl", "pv"], w=[("yn_o", g2, cc)])
            P.add("sp", DMA(YL[G // 2, G % 2].rearrange("(i p) t -> p i t", p=128), yn_o[g2]),
                  r=[("yn_o", g2, cc) for cc in range(4)], w=[("YL", G)], dma=("st_yl", g2))

        def capture(fn_):
            P.cap = []
            fn_()
            lst = P.cap
            P.cap = None
            return lst

        def front_all(G):
            proj_K(G)
            proj_Q(G)
            proj_V(G)
            proj_gate(G)
            proj_x(G)

        def merge(lists):
            idx = [0] * len(lists)
            while True:
                best = None
                for k, l in enumerate(lists):
                    if idx[k] < len(l):
                        fr = idx[k] / len(l)
                        if best is None or fr < best[0]:
                            best = (fr, k)
                if best is None:
                    break
                k = best[1]
                e_ = lists[k][idx[k]]
                P.add(e_[0], e_[1], r=e_[2], w=e_[3], dma=e_[4])
                idx[k] += 1

        def lru_stream(G, ccs, fin):
            for cc in ccs:
                lru_s1(G, cc)
                lru_s23(G, cc)
            if fin:
                lru_final(G)

        front_norm(0)
        for G in range(n_groups + 1):
            lists = []
            if G < n_groups:
                lists.append(capture(lambda: front_all(G)))
            if G + 1 < n_groups:
                lists.append(capture(lambda: front_norm(G + 1)))
            if G >= 1:
                lists.append(capture(lambda: lru_stream(G - 1, (0, 2), False)))
                lists.append(capture(lambda: lru_stream(G - 1, (1, 3), False)))
            merge(lists)
            if G >= 1:
                lru_final(G - 1)

        P.barrier()
        if stop_after == "p1":
            return finish(es)

        AR.top = base_top
        Wout = AR.alloc([8, 1024], BF16)
        p2_top = AR.top
        MK = AR.alloc([2, 8, 512], BF16)
        KA = [[AR.alloc([SEQ], BF16) for _ in range(2)] for _ in range(2)]
        VP = [AR.alloc([64, 128], BF16) for _ in range(2)]
        QS = [[AR.alloc([2, 512], BF16) for _ in range(4)] for _ in range(2)]
        QC = [[AR.alloc([2, 512], BF16) for _ in range(2)] for _ in range(2)]
        EB = [AR.alloc([1024], F32) for _ in range(2)]
        SPB = [AR.alloc([1024], BF16) for _ in range(3)]
        AB = [AR.alloc([1024], BF16) for _ in range(3)]
        CF = [AR.alloc([1024], F32) for _ in range(2)]
        YO = [AR.alloc([512], BF16) for _ in range(2)]
        P.add("sp", DMA(MK, mk_d[:, :, :, :]), w=["mk"], dma="mk")
        stg = []
        for c in range(8):
            stg.append((DMA(Wout[:, c, :], w_out[c * 128:(c + 1) * 128, :], max_dma_last_dim=4096), "wout", "wout"))
        for c in range(8):
            for hf in range(2):
                stg.append((DMA(W1s[:, c, hf * DFF:(hf + 1) * DFF], w_f1[c * 128:(c + 1) * 128, hf * DFF:(hf + 1) * DFF], max_dma_last_dim=4096), "Wstg", "stg"))
        for fc in range(NFC):
            stg.append((DMA(W2s[:, fc, :], w_f2[fc * 128:(fc + 1) * 128, :], max_dma_last_dim=4096), "Wstg", "stg"))
        for pb in range(2):
            for s_ in range(2):
                P.add("pool", MS(KA[pb][s_][0:32, :], 0.0), w=[("ka", pb, s_)])
                P.add("pool", MS(KA[pb][s_][0:1, :], -1.0), w=[("ka", pb, s_)])
        for qs in range(2):
            for k in range(4):
                P.add("pool", MS(QS[qs][k][0:32, :, :], 0.0), w=[("q", qs, k)])
            for k in range(2):
                P.add("pool", MS(QC[qs][k][0:32, :, :], 0.0), w=[("qc", qs, k)])

        pjs = [(p, j) for p in range(4) for j in range(NSLOT)]
        if stop_after == "p2small":
            pjs = [(0, 0), (0, 1), (1, 0)]
        blocks = []
        for pj, (p, j) in enumerate(pjs):
            nblk = 8 * j + 8
            for n in range(nblk):
                kb = 8 * j + 7 - n
                blocks.append(dict(p=p, j=j, n=n, kb=kb, first=(n == 0), last=(n == nblk - 1), band=(kb >= 8 * j), bi=kb - 8 * j, pj=pj))
        L = len(blocks)
        allKT = [("KT", G) for G in range(NG)]
        allV = [("Vd", G) for G in range(NG)]
        allQT = [("QT", G) for G in range(NG)]
        loaded_pairs = set()

        def load_pair(p):
            if p in loaded_pairs or p >= 4:
                return
            loaded_pairs.add(p)
            pb = p % 2
            for s_ in range(2):
                h = 2 * p + s_
                P.add("sp", DMA(KA[pb][s_][32:96, :], KT[h * 64:(h + 1) * 64, :]), r=allKT, w=[("ka", pb, s_)], dma=("ld_ka", pb, s_))
            for q8 in range(8):
                P.add("sp", DMA(VP[pb][:, q8 * 8:(q8 + 1) * 8, :], Vd[q8 * 1024:(q8 + 1) * 1024, p * 128:(p + 1) * 128].rearrange("(blk k) d -> k blk d", k=128)),
                      r=allV, w=[("vp", pb)], dma=("ld_vp", pb))

        def load_q(pj_):
            p, j = pjs[pj_]
            qs = pj_ % 2
            for k in range(2):
                P.add("sp", DMA(QC[qs][k][32:96, :, :], QT[j, k, p * 128:(p + 1) * 128, :].rearrange("(s d) t -> d s t", s=2)),
                      r=allQT, w=[("qc", qs, k)], dma=("ld_q", qs, k))
            sc = 56 + 2 * (j % 2)
            qz = QS[qs][0][0:96, :, :]
            P.add("dve", TS(qz, QC[qs][0][0:96, :, :], pv[0:96, sc:sc + 1], None, ALU.mult), r=[("qc", qs, 0), "pv"], w=[("q", qs, 0)])
            P.add("dve", STT(qz, QC[qs][1][0:96, :, :], pv[0:96, sc + 1:sc + 2], qz, ALU.mult, ALU.add), r=[("qc", qs, 1), ("q", qs, 0), "pv"], w=[("q", qs, 0)])
            for k in (1, 2, 3):
                P.add("pool", CP(QS[qs][k][0:96, :, :], qz), r=[("q", qs, 0)], w=[("q", qs, k)])
            if pj_ >= 1:
                for _ in range(2):
                    if stg:
                        f_, wk_, dk_ = stg.pop(0)
                        P.add("pool", f_, w=[wk_], dma=dk_)

        def stageA_pe(i, b):
            p, j, pj = b["p"], b["j"], b["pj"]
            if i == 0:
                load_pair(p)
                load_q(0)
            if b["n"] == 2:
                if pj + 1 < len(pjs):
                    load_q(pj + 1)
                    if j >= 1 or pjs[pj + 1][0] != p:
                        load_pair(pjs[pj + 1][0] if pjs[pj + 1][0] != p else p + 1)
            pb = p % 2
            qs = pj % 2
            kb = b["kb"]
            for s_ in range(2):
                P.add("pe", MM(banks[s_], KA[pb][s_][0:96, kb * 128:(kb + 1) * 128], QS[qs][0][0:96, s_, :], True, not b["band"]),
                      r=[("ka", pb, s_), ("q", qs, 0)], w=[("pb", s_)])
                if b["band"]:
                    P.add("pe", MM(banks[s_], ident, MK[:, j % 2, b["bi"], :], False, True), r=["cb", "mk"], w=[("pb", s_)])

        def stageA_act(i, b):
            P.add("act", ACT(EB[i % 2], psum_h[:, 0:1024], AF.Exp), r=[("pb", 0), ("pb", 1)], w=[("eb", i % 2)])
            P.add("act", ACT(SPB[i % 3], EB[i % 2], AF.Ln, bias=1.0), r=[("eb", i % 2)], w=[("spb", i % 3)])

        def stageB(i, b):
            p, j, n, pj = b["p"], b["j"], b["n"], b["pj"]
            pb = p % 2
            qs = pj % 2
            kb = b["kb"]
            qa = 1 + n % 3
            qn = 1 + (n + 1) % 3
            cf = CF[pj % 2]
            if not b["last"]:
                for s_ in range(2):
                    gb = 2 + s_
                    P.add("pe", MM(banks[gb], ones, SPB[i % 3][:, s_ * 512:(s_ + 1) * 512], True, True), r=["cb", ("spb", i % 3)], w=[("pb", gb)])
                for s_ in range(2):
                    gb = 2 + s_
                    cfs = cf[0:1, s_ * 512:(s_ + 1) * 512]
                    if n == 0:
                        P.add("dve", CP(cfs, banks[gb][0:1, :]), r=[("pb", gb)], w=[("cf", pj % 2, s_)])
                    else:
                        P.add("dve", TT(cfs, cfs, banks[gb][0:1, :], ALU.add), r=[("pb", gb), ("cf", pj % 2, s_)], w=[("cf", pj % 2, s_)])
                    P.add("dve", CP(QS[qs][qn][0:1, s_, :], cfs), r=[("cf", pj % 2, s_)], w=[("q", qs, qn)])
            for s_ in range(2):
                bb = 4 + s_
                P.add("pe", MM(banks[bb], KA[pb][s_][0:96, kb * 128:(kb + 1) * 128], QS[qs][qa][0:96, s_, :], True, False),
                      r=[("ka", pb, s_), ("q", qs, qa)], w=[("pb", bb)])
                P.add("pe", MM(banks[bb], negtri, SPB[i % 3][:, s_ * 512:(s_ + 1) * 512], False, not b["band"]), r=["cb", ("spb", i % 3)], w=[("pb", bb)])
                if b["band"]:
                    P.add("pe", MM(banks[bb], ident, MK[:, j % 2, b["bi"], :], False, True), r=["cb", "mk"], w=[("pb", bb)])
            P.add("act", ACT(AB[i % 3], psum_h[:, 4 * 512:6 * 512], AF.Exp), r=[("pb", 4), ("pb", 5)], w=[("ab", i % 3)])

        def stageC(i, b):
            p, j, pj = b["p"], b["j"], b["pj"]
            pb = p % 2
            kb = b["kb"]
            for s_ in range(2):
                P.add("pe", MM(banks[6 + s_], VP[pb][:, kb, :], AB[i % 3][:, s_ * 512:(s_ + 1) * 512], b["first"], b["last"]),
                      r=[("vp", pb), ("ab", i % 3)], w=[("pb", 6 + s_)])
            if b["last"]:
                yo = pj % 2
                for s_ in range(2):
                    P.add("dve", CP(YO[yo][s_ * 64:(s_ + 1) * 64, :], banks[6 + s_][s_ * 64:(s_ + 1) * 64, :]), r=[("pb", 6 + s_)], w=[("yo", yo)])
                P.add("sp", DMA(YA[p * 128:(p + 1) * 128, j * 512:(j + 1) * 512], YO[yo]),
                      r=[("yo", yo)], w=[("YA", 2 * p, j), ("YA", 2 * p + 1, j)], dma=("st_ya", yo))

        stageA_pe(0, blocks[0])
        for it in range(L + 2):
            if it == L:
                while stg:
                    f_, wk_, dk_ = stg.pop(0)
                    P.add("pool", f_, w=[wk_], dma=dk_)
            if it < L:
                stageA_act(it, blocks[it])
            if 1 <= it <= L:
                stageB(it - 1, blocks[it - 1])
            if it >= 2:
                stageC(it - 2, blocks[it - 2])
            if it + 1 < L:
                stageA_pe(it + 1, blocks[it + 1])

        P.barrier()
        if stop_after in ("p2", "p2small"):
            return finish(es)

        AR.top = p2_top
        W1 = AR.alloc([8, 2 * DFF], BF16)
        W2 = AR.alloc([NFC, 1024], BF16)
        wloads = []
        for c in range(8):
            wloads.append((DMA(W1[:, c, :], W1s[:, c, :]), "w1", ["Wstg"]))
        for f4 in range(0, NFC, 4):
            f5 = min(NFC, f4 + 4)
            wloads.append((DMA(W2[:, f4:f5, :], W2s[:, f4:f5, :]), "w2", ["Wstg"]))
        p3_top = AR.top
        ylb = AR.alloc([4, 512], BF16)
        ylc = [AR.alloc([4, 512], BF16) for _ in range(2)]
        yab = AR.alloc([4, 512], BF16)
        yan = AR.alloc([4, 512], BF16)
        sq2 = [AR.alloc([512], BF16) for _ in range(2)]
        rl2 = AR.alloc([512], F32)
        xt2 = [AR.alloc([1024], F32) for _ in range(3)]
        xc2 = 0
        allYL = [("YL", G) for G in range(NG)]
        for j in range(NSLOT):
            for k in range(2):
                P.add("sp", DMA(ylc[k], YL[j, k].rearrange("(i p) t -> p i t", p=128)), r=allYL, w=[("ylc", k)], dma=("ld_ylc", k))
            sc = 56 + 2 * (j % 2)
            P.add("dve", TS(ylb, ylc[0], pv[:, sc:sc + 1], None, ALU.mult), r=[("ylc", 0), "pv"], w=["ylb"])
            P.add("dve", STT(ylb, ylc[1], pv[:, sc + 1:sc + 2], ylb, ALU.mult, ALU.add), r=[("ylc", 1), "ylb", "pv"], w=["ylb"])
            P.add("sp", DMA(yab, YA[:, j * 512:(j + 1) * 512].rearrange("(i p) t -> p i t", p=128)),
                  r=[("YA", h, j) for h in range(8)], w=["yab", ("p25slot", j)], dma="ld_yab")
            nw = 3 if j < 4 else len(wloads)
            for fn_, key_, rk_ in wloads[:nw]:
                P.add("sp", fn_, r=[("p25slot", j)] + rk_, w=[key_], dma=key_)
            wloads = wloads[nw:]
            for p in range(4):
                P.add("pool", TT(sq2[p % 2], yab[:, p, :], yab[:, p, :], ALU.mult), r=["yab"], w=[("sq2", p % 2)])
                P.add("pe", MM(banks[7], ones, sq2[p % 2], p == 0, p == 3), r=["cb", ("sq2", p % 2)], w=[("pb", 7)])
            P.add("act", ACT(rl2, banks[7], AF.Ln, scale=1.0 / 512, bias=epsb[:, 0:1]), r=[("pb", 7), "epsb"], w=["rl2"])
            P.add("act", ACT(rl2, rl2, AF.Exp, scale=-0.5), r=["rl2"], w=["rl2"])
            for p in range(4):
                P.add("dve", STT(yan[:, p, :], yab[:, p, :], pv[:, 52 + p:53 + p], rl2, ALU.mult, ALU.mult), r=["yab", "rl2", "pv"], w=[("yan", p)])
            for tt in range(4):
                s = xc2 % 3
                xc2 += 1
                row0 = j * 512 + tt * 128
                P.add("sp", DMA(xt2[s], xq[row0:row0 + 128, :]), w=[("xt2", s)], dma=("xt2", s))
                for hf in range(2):
                    bk = (tt * 2 + hf) % 4
                    for k in range(8):
                        src = ylb[:, k, tt * 128:(tt + 1) * 128] if k < 4 else yan[:, k - 4, tt * 128:(tt + 1) * 128]
                        rk = "ylb" if k < 4 else ("yan", k - 4)
                        P.add("pe", MM(banks[bk], src, Wout[:, k, hf * 512:(hf + 1) * 512], k == 0, k == 7), r=["wout", rk], w=[("pb", bk)])
                    P.add("dve", TT(xt2[s][:, hf * 512:(hf + 1) * 512], xt2[s][:, hf * 512:(hf + 1) * 512], banks[bk], ALU.add),
                          r=[("pb", bk), ("xt2", s)], w=[("xt2", s)])
                P.add("sp", DMA(X1[row0:row0 + 128, :], xt2[s]), r=[("xt2", s)], w=[("X1", j * 4 + tt)], dma=("st_x1", s))

        P.barrier()
        if stop_after == "p25":
            return finish(es)

        AR.top = base_top
        x1t_b = AR.alloc([NT3, 1024], F32)
        hb3_b = AR.alloc([NT3, 1024], BF16)
        hT3_b = AR.alloc([8, TG], BF16)
        assert AR.top <= p2_top
        AR.top = p3_top
        nfin = AR.alloc([1024], F32)
        x1t = [AR.alloc([NT3, 1024], F32), x1t_b]
        hb3 = [AR.alloc([NT3, 1024], BF16), hb3_b]
        hT3 = [AR.alloc([8, TG], BF16), hT3_b]
        aT = AR.alloc([NFC, TG], BF16)
        sgb = [AR.alloc([TG], F32) for _ in range(2)]
        ot = [AR.alloc([1024], F32) for _ in range(2)]
        ss3 = AR.alloc([2, NT3], F32)
        rs3 = AR.alloc([2, NT3], F32)
        ss4 = AR.alloc([2, NT3], F32)
        rs4 = AR.alloc([2, NT3], F32)
        P.add("sp", DMA(nfin, nfin_d[:, :]), w=["nfin"], dma="nfin")
        st3 = {"oc": 0}
        n_it = NSLOT * 512 // TG

        def p3_pre(it3):
            j2 = it3 % 2
            hbk = [("hb3", j2, tt) for tt in range(NT3)]
            for tt in range(NT3):
                tile_id = it3 * NT3 + tt
                row0 = tile_id * 128
                P.add("sp", DMA(x1t[j2][:, tt, :], X1[row0:row0 + 128, :]), r=[("X1", tile_id)], w=[("x1t", j2, tt)], dma=("ld_x1", j2, tt))
                norm_tile(x1t[j2][:, tt, :], ("x1t", j2, tt), ss3[:, j2, tt:tt + 1], ("ss3", j2, tt), rs3[:, j2, tt:tt + 1], ("rs3", j2, tt), hb3[j2][:, tt, :], hbk[tt])
            transpose_group(hb3[j2], hbk, hT3[j2], ("hT3", j2), 8, NT3)

        def p3_main(it3):
            j2 = it3 % 2
            hTk = [(("hT3", j2), c) for c in range(8)]
            for fc in range(NFC):
                gb = 2 + fc % 2
                ub = 4 + fc % 2
                for c in range(8):
                    P.add("pe", MM(banks[gb][:, 0:TG], W1[:, c, fc * 128:(fc + 1) * 128], hT3[j2][:, c, :], c == 0, c == 7), r=["w1", hTk[c]], w=[("pb", gb)])
                for c in range(8):
                    P.add("pe", MM(banks[ub][:, 0:TG], W1[:, c, DFF + fc * 128:DFF + (fc + 1) * 128], hT3[j2][:, c, :], c == 0, c == 7), r=["w1", hTk[c]], w=[("pb", ub)])
                P.add("act", ACT(sgb[fc % 2], banks[gb][:, 0:TG], AF.Silu), r=[("pb", gb)], w=[("sgb", fc % 2)])
                P.add("dve", TT(aT[:, fc, :], sgb[fc % 2], banks[ub][:, 0:TG], ALU.mult), r=[("sgb", fc % 2), ("pb", ub)], w=[("aT", fc)])
            for tt in range(NT3):
                xk = ("x1t", j2, tt)
                for hf in range(2):
                    ob = 6 + (tt * 2 + hf) % 2
                    for fc in range(NFC):
                        P.add("pe", MM(banks[ob], aT[:, fc, tt * 128:(tt + 1) * 128], W2[:, fc, hf * 512:(hf + 1) * 512], fc == 0, fc == NFC - 1), r=["w2", ("aT", fc)], w=[("pb", ob)])
                    P.add("dve", TT(x1t[j2][:, tt, hf * 512:(hf + 1) * 512], x1t[j2][:, tt, hf * 512:(hf + 1) * 512], banks[ob], ALU.add),
                          r=[("pb", ob), xk], w=[xk])
                o = st3["oc"] % 2
                st3["oc"] += 1
                sq_accum(x1t[j2][:, tt, :], ss4[:, j2, tt:tt + 1], [xk], [("ss4", j2, tt)])
                P.add("pool", TS(rs4[:, j2, tt:tt + 1], ss4[:, j2, tt:tt + 1], 1.0 / D, EPS, ALU.mult, ALU.add), r=[("ss4", j2, tt)], w=[("rs4", j2, tt)])
                P.add("pool", TT(rs4[:, j2, tt:tt + 1], rs4[:, j2, tt:tt + 1], cneg[:, 0:1], ALU.pow), r=[("rs4", j2, tt), "cneg"], w=[("rs4", j2, tt)])
                P.add("dve", STT(ot[o], x1t[j2][:, tt, :], rs4[:, j2, tt:tt + 1], nfin, ALU.mult, ALU.mult), r=[xk, ("rs4", j2, tt), "nfin"], w=[("ot", o)])
                row0 = (it3 * NT3 + tt) * 128
                P.add("sp", DMA(out_d[row0:row0 + 128, :], ot[o]), r=[("ot", o)], w=[("out", it3, tt)], dma=("st_out", o))

        p3_pre(0)
        for it3 in range(n_it):
            lists = [capture(lambda: p3_main(it3))]
            if it3 + 1 < n_it:
                lists.append(capture(lambda: p3_pre(it3 + 1)))
            merge(lists)

        P.barrier()
        return finish(es)


def _slot_groups(par):
    return [2 * j + (par ^ (j & 1)) for j in range(NSLOT)]


def _consts():
    bf = ml_dtypes.bfloat16
    cb = np.zeros((128, 3, 128), np.float32)
    cb[:, 0, :] = np.eye(128)
    jj = np.arange(128)[:, None]
    s_ = np.arange(128)[None, :]
    cb[:, 1, :] = np.where(jj >= s_, -1.0, 0.0)
    cb[:, 2, :] = 1.0
    s = np.arange(128)[:, None, None]
    i = np.arange(8)[None, :, None]
    t = np.arange(512)[None, None, :]
    kpos = 128 * i + s
    m_min = np.where(kpos < t, 0.0, NEG)
    m_max = np.where(kpos < 512 + t, 0.0, NEG)
    return cb.astype(bf), m_min.astype(bf), m_max.astype(bf)


def _make_in_maps(inputs):
    f = lambda a: np.ascontiguousarray(np.asarray(a, dtype=np.float32))
    x = f(inputs["x"])
    w_in = f(inputs["w_in"][0])
    w_out = f(inputs["w_out"][0])
    w_f1 = f(inputs["w_ffn_in"][0])
    w_f2 = f(inputs["w_ffn_out"][0])
    pv = np.zeros((128, NV), np.float32)
    pv[:, 0:8] = f(inputs["norm_mix"][0]).reshape(8, 128).T
    pv[:, 8:16] = f(inputs["norm_ffn"][0]).reshape(8, 128).T
    pv[:, 16:20] = f(inputs["conv_b"][0]).reshape(4, 128).T
    cw = f(inputs["conv_w"][0])
    for cc in range(4):
        pv[:, 20 + cc * 4:24 + cc * 4] = cw[:, cc * 128:(cc + 1) * 128].T
    pv[:, 36:40] = f(inputs["b_rg"][0]).reshape(4, 128).T
    pv[:, 40:44] = f(inputs["b_ig"][0]).reshape(4, 128).T
    pv[:, 44:48] = f(inputs["lru_lambda"][0]).reshape(4, 128).T
    pv[:, 48:52] = f(inputs["norm_lru_out"][0]).reshape(4, 128).T
    pv[:, 52:56] = f(inputs["norm_att_out"][0]).reshape(4, 128).T
    wbd = np.zeros((128, 8, 128), np.float32)
    wrg = f(inputs["w_rg"][0])
    wig = f(inputs["w_ig"][0])
    for cc in range(4):
        for hb_ in range(2):
            blk = 2 * cc + hb_
            wbd[hb_ * 64:(hb_ + 1) * 64, cc, hb_ * 64:(hb_ + 1) * 64] = wrg[blk]
            wbd[hb_ * 64:(hb_ + 1) * 64, 4 + cc, hb_ * 64:(hb_ + 1) * 64] = wig[blk]
    nfin = np.ascontiguousarray(np.broadcast_to(f(inputs["norm_final"])[None, :], (128, D)))
    cb, m_min, m_max = _consts()
    maps = []
    for c in range(N_CORES):
        b, par = c // 2, c % 2
        groups = _slot_groups(par)
        xq = np.concatenate([x[b, g * 512:(g + 1) * 512] for g in groups], axis=0)
        mk = np.stack([m_min, m_max] if par == 0 else [m_max, m_min], axis=1)
        pvc = pv.copy()
        pvc[:, 56:60] = np.array([1 - par, par, par, 1 - par], np.float32)[None, :]
        maps.append({"xs": np.ascontiguousarray(x[b]), "xq": np.ascontiguousarray(xq), "w_in": w_in, "w_out": w_out,
                     "w_f1": w_f1, "w_f2": w_f2, "wbd": wbd, "pv": pvc, "nfin": nfin, "cb": cb,
                     "mk": np.ascontiguousarray(mk)})
    return maps


_NC_CACHE = {}


def kernel(**inputs):
    maps = _make_in_maps(inputs)
    if "nc" not in _NC_CACHE:
        _NC_CACHE["nc"] = build_program()
    nc = _NC_CACHE["nc"]
    res = run_bass_kernel_spmd(nc, maps, core_ids=list(range(N_CORES)))
    out = np.zeros((4, SEQ, D), np.float32)
    for c in range(N_CORES):
        b, par = c // 2, c % 2
        o = np.asarray(res.results[c]["out"], dtype=np.float32)
        for j, g in enumerate(_slot_groups(par)):
            out[b, g * 512:(g + 1) * 512] = o[j * 512:(j + 1) * 512]
    return out
```

```python
import numpy as np
import ml_dtypes
from contextlib import ExitStack
import concourse.bass as bass
import concourse.mybir as mybir
from concourse.bass_utils import run_bass_kernel_spmd

F32 = mybir.dt.float32
BF16 = mybir.dt.bfloat16
AF = mybir.ActivationFunctionType
ALU = mybir.AluOpType

D = 1024
SEQ = 8192
NG = 16
NSLOT = 8
DFF = 2816
NFC = 22
EPS = 1e-6
NEG = -30000.0
NV = 64
N_CORES = 8
TG = 256
NT3 = TG // 128


class _Op:
    __slots__ = ("eng", "fn", "deps", "key", "sig", "signal")


class Prog:
    def __init__(self):
        self.ops = []
        self.lastw = {}
        self.rd = {}
        self.last_eng = {}
        self.last_dma = {}

    def add(self, eng, fn, r=(), w=(), dma=None):
        if getattr(self, "cap", None) is not None:
            self.cap.append((eng, fn, tuple(r), tuple(w), dma))
            return -1
        i = len(self.ops)
        xr = [k for k in r if isinstance(k, tuple) and k[0] == "pb"]
        if xr:
            w = list(w) + xr
        deps = set()
        for k in r:
            j = self.lastw.get(k)
            if j is not None:
                deps.add(j)
        for k in w:
            j = self.lastw.get(k)
            if j is not None:
                deps.add(j)
            rk = self.rd.get(k)
            if rk:
                deps.update(rk.values())
        for k in r:
            rk = self.rd.setdefault(k, {})
            rk[("dma", i) if dma is not None else eng] = i
        for k in w:
            self.lastw[k] = i
            self.rd[k] = {}
        op = _Op()
        op.eng = eng
        op.fn = fn
        op.deps = deps
        op.key = dma
        op.sig = None
        op.signal = dma is not None
        self.ops.append(op)
        if dma is not None:
            self.last_dma[dma] = i
        else:
            self.last_eng[eng] = i
        return i

    def barrier(self, engines=("pe", "act", "dve", "pool", "sp")):
        deps = set(self.last_eng.values()) | set(self.last_dma.values())
        for e in engines:
            op = _Op()
            op.eng = e
            op.fn = None
            op.deps = set(deps)
            op.key = None
            op.sig = None
            op.signal = False
            self.ops.append(op)

    def emit(self, nc, es, pre_sp=None):
        ops = self.ops
        for op in ops:
            latest = {}
            for d in op.deps:
                D_ = ops[d]
                if D_.key is None and D_.fn is not None:
                    if D_.eng == "pe" and op.eng == "pe" and op.key is None:
                        continue
                    if d > latest.get(D_.eng, -1):
                        latest[D_.eng] = d
            for d in latest.values():
                ops[d].signal = True
        cnt = {}
        for op in ops:
            if op.fn is None:
                continue
            if op.key is not None:
                k = ("dma", op.key)
                cnt[k] = cnt.get(k, 0) + 16
                op.sig = cnt[k]
            elif op.signal:
                cnt[op.eng] = cnt.get(op.eng, 0) + 1
                op.sig = cnt[op.eng]
        self.counts = cnt
        sems = {}
        for n_, k in enumerate(cnt.keys()):
            sems[k] = es.enter_context(nc.semaphore("s%d" % n_))
        block = es.enter_context(nc.Block())
        stats = {}

        def run(engname, e):
            waited = {}
            nw = 0
            ni = 0
            for op in ops:
                if op.eng != engname:
                    continue
                need = {}
                for d in op.deps:
                    D_ = ops[d]
                    if D_.fn is None:
                        continue
                    if D_.key is None:
                        if D_.eng == "pe" and engname == "pe" and op.key is None:
                            continue
                        k = D_.eng
                        if D_.sig is None:
                            continue
                    else:
                        k = ("dma", D_.key)
                    if D_.sig > need.get(k, 0):
                        need[k] = D_.sig
                for k, v in need.items():
                    if waited.get(k, 0) >= v:
                        continue
                    e.wait_ge(sems[k], v)
                    waited[k] = v
                    nw += 1
                if op.fn is not None:
                    ins = op.fn(e)
                    ni += 1
                    if op.key is not None:
                        ins.then_inc(sems[("dma", op.key)], 16)
                    elif op.signal:
                        ins.then_inc(sems[op.eng], 1)
            stats[engname] = (ni, nw)

        @block.tensor
        def _(t):
            run("pe", t)

        @block.scalar
        def _(a):
            run("act", a)

        @block.vector
        def _(v):
            run("dve", v)

        @block.gpsimd
        def _(g):
            run("pool", g)

        @block.sync
        def _(s):
            if pre_sp is not None:
                pre_sp(s)
            run("sp", s)

        self.stats = stats


class Arena:
    def __init__(self, h32):
        self.h32 = h32
        self.h16 = h32.bitcast(BF16)
        self.top = 0
        self.cap = h32.shape[1] * 4

    def alloc(self, free, dtype):
        n = 1
        for f in free:
            n *= f
        sz = 4 if dtype == F32 else 2
        off = self.top
        nb = (n * sz + 63) // 64 * 64
        self.top += nb
        assert self.top <= self.cap, ("SBUF arena overflow", self.top, self.cap)
        if dtype == F32:
            ap = self.h32[:, off // 4: off // 4 + n]
        else:
            ap = self.h16[:, off // 2: off // 2 + n]
        if len(free) == 2:
            ap = ap.rearrange("p (a b) -> p a b", a=free[0])
        elif len(free) == 3:
            ap = ap.rearrange("p (a b c) -> p a b c", a=free[0], b=free[1])
        return ap


def ACT(out, in_, func, **kw):
    return lambda e: e.activation(out=out, in_=in_, func=func, **kw)


def MM(out, lhsT, rhs, start, stop):
    return lambda e: e.matmul(out, lhsT=lhsT, rhs=rhs, start=start, stop=stop)


def TR(out, in_, ident):
    return lambda e: e.transpose(out=out, in_=in_, identity=ident)


def TT(out, in0, in1, op):
    return lambda e: e.tensor_tensor(out=out, in0=in0, in1=in1, op=op)


def TS(out, in0, s1, s2, op0, op1=None):
    if op1 is None:
        return lambda e: e.tensor_scalar(out=out, in0=in0, scalar1=s1, scalar2=None, op0=op0)
    return lambda e: e.tensor_scalar(out=out, in0=in0, scalar1=s1, scalar2=s2, op0=op0, op1=op1)


def STT(out, in0, scalar, in1, op0, op1):
    return lambda e: e.scalar_tensor_tensor(out=out, in0=in0, scalar=scalar, in1=in1, op0=op0, op1=op1)


def CP(out, in_):
    return lambda e: e.tensor_copy(out=out, in_=in_)


def MS(ap, v):
    return lambda e: e.memset(ap, v)


def SCAN(out, d0, d1, init):
    return lambda e: e.tensor_tensor_scan(out=out, data0=d0, data1=d1, initial=init, op0=ALU.mult, op1=ALU.add)


def DMA(out, in_, **kw):
    return lambda e: e.dma_start(out=out, in_=in_, **kw)


def build_program(stop_after=None, debug=False, n_groups=NG):
    nc = bass.Bass("TRN2", target_bir_lowering=False)
    kind_s = "ExternalOutput" if debug else "Internal"
    dt = nc.dram_tensor
    xs = dt("xs", [SEQ, D], F32, kind="ExternalInput").ap()
    xq = dt("xq", [NSLOT * 512, D], F32, kind="ExternalInput").ap()
    w_in = dt("w_in", [D, 2560], F32, kind="ExternalInput").ap()
    w_out = dt("w_out", [D, D], F32, kind="ExternalInput").ap()
    w_f1 = dt("w_f1", [D, 2 * DFF], F32, kind="ExternalInput").ap()
    w_f2 = dt("w_f2", [DFF, D], F32, kind="ExternalInput").ap()
    wbd_d = dt("wbd", [128, 8, 128], F32, kind="ExternalInput").ap()
    pv_d = dt("pv", [128, NV], F32, kind="ExternalInput").ap()
    nfin_d = dt("nfin", [128, D], F32, kind="ExternalInput").ap()
    cb_d = dt("cb", [128, 3, 128], BF16, kind="ExternalInput").ap()
    mk_d = dt("mk", [128, 2, 8, 512], BF16, kind="ExternalInput").ap()
    out_d = dt("out", [NSLOT * 512, D], F32, kind="ExternalOutput").ap()
    KT = dt("KT", [512, SEQ], BF16, kind=kind_s).ap()
    Vd = dt("Vd", [SEQ, 512], BF16, kind=kind_s).ap()
    QT = dt("QT", [NSLOT, 2, 512, 512], BF16, kind=kind_s).ap()
    YL = dt("YL", [NSLOT, 2, 512, 512], BF16, kind=kind_s).ap()
    YA = dt("YA", [512, NSLOT * 512], BF16, kind=kind_s).ap()
    X1 = dt("X1", [NSLOT * 512, D], F32, kind=kind_s).ap()
    W1s = dt("W1s", [128, 8, 2 * DFF], BF16, kind="Internal").ap()
    W2s = dt("W2s", [128, NFC, D], BF16, kind="Internal").ap()

    P = Prog()
    def finish(es):
        P.emit(nc, es)
        nc._prog_stats = (P.stats, P.counts)
        return nc

    with ExitStack() as es:
        arena_h = es.enter_context(nc.sbuf_tensor("arena", [128, 51968], F32))
        AR = Arena(arena_h)
        psum_h = es.enter_context(nc.psum_tensor("ps", [128, 4096], F32))
        psum16_h = psum_h.bitcast(BF16)
        banks = [psum_h[:, i * 512:(i + 1) * 512] for i in range(8)]
        banks16 = [psum16_h[:, i * 1024:(i + 1) * 1024] for i in range(8)]

        pv = AR.alloc([NV], F32)
        cb = AR.alloc([3, 128], BF16)
        ident, negtri, ones = cb[:, 0, :], cb[:, 1, :], cb[:, 2, :]
        cneg = AR.alloc([512], F32)
        cpos = AR.alloc([512], F32)
        cs = AR.alloc([4], F32)
        epsb = AR.alloc([4], F32)
        tmp4 = AR.alloc([4], F32)
        junks = [AR.alloc([1024], BF16) for _ in range(2)]
        jcnt = [0]

        def sq_accum(in_ap, acc_ap, rkeys, wkeys):
            jb = jcnt[0] % 2
            jcnt[0] += 1
            P.add("act", ACT(junks[jb], in_ap, AF.Square, accum_out=acc_ap), r=rkeys, w=list(wkeys) + [("junk", jb)])
        P.add("sp", DMA(pv, pv_d[:, :]), w=["pv"], dma="pv")
        P.add("sp", DMA(cb, cb_d[:, :, :]), w=["cb"], dma="cb")
        P.add("pool", MS(cneg, -0.5), w=["cneg"])
        P.add("pool", MS(epsb, EPS), w=["epsb"])
        P.add("pool", MS(cpos, 0.5), w=["cpos"])
        P.add("act", ACT(tmp4, pv[:, 44:48], AF.Exp, scale=-1.0), r=["pv"], w=["tmp4"])
        P.add("act", ACT(tmp4, tmp4, AF.Ln, bias=1.0), r=["tmp4"], w=["tmp4"])
        P.add("dve", TS(cs, tmp4, -8.0, None, ALU.mult), r=["tmp4"], w=["cs"])
        base_top = AR.top

        def norm_tile(xt_ap, xkey, ss_ap, sskey, rs_ap, rskey, hb_ap, hbkey):
            sq_accum(xt_ap, ss_ap, [xkey], [sskey])
            P.add("pool", TS(rs_ap, ss_ap, 1.0 / D, EPS, ALU.mult, ALU.add), r=[sskey], w=[rskey])
            P.add("pool", TT(rs_ap, rs_ap, cneg[:, 0:1], ALU.pow), r=[rskey, "cneg"], w=[rskey])
            P.add("dve", TS(hb_ap, xt_ap, rs_ap, None, ALU.mult), r=[xkey, rskey], w=[hbkey])

        def transpose_group(hb, hbkeys, hT, hTkey, gcol0, ntt):
            w_ = ntt * 128
            for cp in range(4):
                bk = cp % 2
                for ci in range(2):
                    c = cp * 2 + ci
                    for tt in range(ntt):
                        o0 = ci * w_ + tt * 128
                        P.add("pe", TR(banks16[bk][:, o0:o0 + 128], hb[:, tt, c * 128:(c + 1) * 128], ident), r=[hbkeys[tt], "cb"], w=[("pb", bk)])
                for ci in range(2):
                    c = cp * 2 + ci
                    src = banks16[bk][:, ci * w_:(ci + 1) * w_]
                    g = pv[:, gcol0 + c:gcol0 + c + 1]
                    P.add("act", ACT(hT[:, c, :], src, AF.Identity, scale=g), r=[("pb", bk), "pv"], w=[(hTkey, c)])

        Win = AR.alloc([8, 2560], BF16)
        Wbd = AR.alloc([8, 128], BF16)
        for c in range(8):
            P.add("pool", DMA(Win[:, c, :], w_in[c * 128:(c + 1) * 128, :], max_dma_last_dim=4096), w=["win"], dma="win")
        P.add("pool", DMA(Wbd, wbd_d[:, :, :]), w=["wbd"], dma="wbd")
        NXT = 4
        xt = [AR.alloc([1024], F32) for _ in range(NXT)]
        ssb = AR.alloc([2, 4], F32)
        rsb = AR.alloc([2, 4], F32)
        hb = [AR.alloc([4, 1024], BF16) for _ in range(2)]
        hT = [AR.alloc([8, 512], BF16) for _ in range(2)]
        kt_o = [AR.alloc([4, 512], BF16) for _ in range(2)]
        qt_o = [AR.alloc([4, 512], BF16) for _ in range(2)]
        v_o = [AR.alloc([4, 512], BF16) for _ in range(2)]
        yn_o = [AR.alloc([4, 512], BF16) for _ in range(2)]
        xl = [AR.alloc([4, 516], F32) for _ in range(2)]
        gl = [AR.alloc([4, 512], BF16) for _ in range(2)]
        yl = AR.alloc([4, 512], F32)
        hprev = AR.alloc([4], F32)
        rl = AR.alloc([512], F32)
        L_xc = [AR.alloc([512], F32) for _ in range(2)]
        L_xcb = [AR.alloc([512], BF16) for _ in range(2)]
        L_r = [AR.alloc([512], F32) for _ in range(2)]
        L_i = [AR.alloc([512], F32) for _ in range(2)]
        L_a = [AR.alloc([512], F32) for _ in range(2)]
        L_t = [AR.alloc([512], F32) for _ in range(2)]
        L_bt = [AR.alloc([512], F32) for _ in range(2)]
        L_hl = [AR.alloc([512], F32) for _ in range(2)]
        L_sq = [AR.alloc([512], BF16) for _ in range(4)]
        for g2_ in range(2):
            P.add("pool", MS(xl[g2_], 0.0), w=[("xlm", g2_, cc) for cc in range(4)] + [("xlh", g2_, cc) for cc in range(4)])
        P.add("pool", MS(hprev, 0.0), w=[("hprev", cc) for cc in range(4)])

        st1 = {"xcnt": 0, "pjc": 0}

        def hTkeys(g2):
            return [(("hT", g2), c) for c in range(8)]

        def proj_fm(G, col):
            g2 = G % 2
            hTk = hTkeys(g2)
            bk = 2 + st1["pjc"] % 2
            st1["pjc"] += 1
            for c in range(8):
                P.add("pe", MM(banks[bk], Win[:, c, col:col + 128], hT[g2][:, c, :], c == 0, c == 7), r=["win", hTk[c]], w=[("pb", bk)])
            return bk

        def front_norm(G):
            g2 = G % 2
            hbk = [("hb", g2, tt) for tt in range(4)]
            for tt in range(4):
                s = st1["xcnt"] % NXT
                st1["xcnt"] += 1
                row0 = (G * 4 + tt) * 128
                P.add("sp", DMA(xt[s], xs[row0:row0 + 128, :]), w=[("xt", s)], dma=("xt", s))
                norm_tile(xt[s], ("xt", s), ssb[:, g2, tt:tt + 1], ("ss", g2, tt), rsb[:, g2, tt:tt + 1], ("rs", g2, tt), hb[g2][:, tt, :], hbk[tt])
            transpose_group(hb[g2], hbk, hT[g2], ("hT", g2), 0, 4)

        def proj_K(G):
            g2 = G % 2
            for i in range(4):
                bk = proj_fm(G, 1536 + i * 128)
                P.add("act", ACT(kt_o[g2][:, i, :], banks[bk], AF.Copy), r=[("pb", bk)], w=[("kt_o", g2, i)])
            P.add("sp", DMA(KT[:, G * 512:(G + 1) * 512].rearrange("(i p) t -> p i t", p=128), kt_o[g2]),
                  r=[("kt_o", g2, i) for i in range(4)], w=[("KT", G)], dma=("st_kt", g2))

        def proj_Q(G):
            g2 = G % 2
            for i in range(4):
                bk = proj_fm(G, 1024 + i * 128)
                P.add("act", ACT(qt_o[g2][:, i, :], banks[bk], AF.Copy, scale=0.125), r=[("pb", bk)], w=[("qt_o", g2, i)])
            P.add("sp", DMA(QT[G // 2, G % 2].rearrange("(i p) t -> p i t", p=128), qt_o[g2]),
                  r=[("qt_o", g2, i) for i in range(4)], w=[("QT", G)], dma=("st_qt", g2))

        def proj_V(G):
            g2 = G % 2
            hTk = hTkeys(g2)
            for tt in range(4):
                bk = 2 + st1["pjc"] % 2
                st1["pjc"] += 1
                for c in range(8):
                    P.add("pe", MM(banks[bk], hT[g2][:, c, tt * 128:(tt + 1) * 128], Win[:, c, 2048:2560], c == 0, c == 7), r=["win", hTk[c]], w=[("pb", bk)])
                P.add("dve", CP(v_o[g2][:, tt, :], banks[bk]), r=[("pb", bk)], w=[("v_o", g2, tt)])
            P.add("sp", DMA(Vd[G * 512:(G + 1) * 512, :].rearrange("(tt p) n -> p tt n", p=128), v_o[g2]),
                  r=[("v_o", g2, tt) for tt in range(4)], w=[("Vd", G)], dma=("st_v", g2))

        def proj_gate(G):
            g2 = G % 2
            for cc in range(4):
                bk = proj_fm(G, 512 + cc * 128)
                P.add("act", ACT(gl[g2][:, cc, :], banks[bk], AF.Gelu_apprx_tanh), r=[("pb", bk)], w=[("gl", g2, cc)])

        def proj_x(G):
            g2 = G % 2
            for cc in range(4):
                if G > 0:
                    P.add("pool", CP(xl[g2][:, cc, 0:3], xl[1 - g2][:, cc, 512:515]), r=[("xlm", 1 - g2, cc)], w=[("xlh", g2, cc)])
                bk = proj_fm(G, cc * 128)
                P.add("dve", CP(xl[g2][:, cc, 3:515], banks[bk]), r=[("pb", bk)], w=[("xlm", g2, cc)])

        def lru_s1(G, cc):
            g2 = G % 2
            lb = cc % 2
            xlg = xl[g2]
            xc, xcb = L_xc[lb], L_xcb[lb]
            kx = [("xlm", g2, cc), ("xlh", g2, cc), "pv"]
            cw = lambda tap: pv[:, 20 + cc * 4 + tap:20 + cc * 4 + tap + 1]
            P.add("pool", TS(xc, xlg[:, cc, 3:515], cw(3), pv[:, 16 + cc:17 + cc], ALU.mult, ALU.add), r=kx, w=[("xc", lb)])
            for tap in (2, 1, 0):
                P.add("dve", STT(xc, xlg[:, cc, tap:tap + 512], cw(tap), xc, ALU.mult, ALU.add), r=kx + [("xc", lb)], w=[("xc", lb)])
            P.add("act", ACT(xcb, xc, AF.Copy), r=[("xc", lb)], w=[("xcb", lb)])
            gr, gi = (5, 6) if lb == 0 else (4, 7)
            P.add("pe", MM(banks[gr], Wbd[:, cc, :], xcb, True, True), r=["wbd", ("xcb", lb)], w=[("pb", gr)])
            P.add("pe", MM(banks[gi], Wbd[:, 4 + cc, :], xcb, True, True), r=["wbd", ("xcb", lb)], w=[("pb", gi)])
            r_, i_ = L_r[lb], L_i[lb]
            P.add("act", ACT(r_, banks[gr], AF.Sigmoid, bias=pv[:, 36 + cc:37 + cc]), r=[("pb", gr), "pv"], w=[("r", lb)])
            P.add("act", ACT(i_, banks[gi], AF.Sigmoid, bias=pv[:, 40 + cc:41 + cc]), r=[("pb", gi), "pv"], w=[("i", lb)])

        def lru_s23(G, cc):
            g2 = G % 2
            lb = cc % 2
            xc, r_, i_, a_, t_, bt, hl, sq = L_xc[lb], L_r[lb], L_i[lb], L_a[lb], L_t[lb], L_bt[lb], L_hl[lb], L_sq[cc]
            P.add("act", ACT(a_, r_, AF.Exp, scale=cs[:, cc:cc + 1]), r=[("r", lb), "cs"], w=[("a", lb)])
            P.add("pool", TT(t_, a_, a_, ALU.mult), r=[("a", lb)], w=[("t", lb)])
            P.add("act", ACT(t_, t_, AF.Ln, scale=-1.0, bias=1.0), r=[("t", lb)], w=[("t", lb)])
            P.add("act", ACT(t_, t_, AF.Exp, scale=0.5), r=[("t", lb)], w=[("t", lb)])
            P.add("pool", TT(bt, i_, xc, ALU.mult), r=[("i", lb), ("xc", lb)], w=[("bt", lb)])
            P.add("pool", TT(bt, bt, t_, ALU.mult), r=[("bt", lb), ("t", lb)], w=[("bt", lb)])
            P.add("dve", SCAN(hl, a_, bt, hprev[:, cc:cc + 1]), r=[("a", lb), ("bt", lb), ("hprev", cc)], w=[("hl", lb)])
            P.add("dve", CP(hprev[:, cc:cc + 1], hl[:, 511:512]), r=[("hl", lb)], w=[("hprev", cc)])
            P.add("pool", TT(yl[:, cc, :], gl[g2][:, cc, :], hl, ALU.mult), r=[("gl", g2, cc), ("hl", lb)], w=[("yl", cc)])
            P.add("pool", TT(sq, yl[:, cc, :], yl[:, cc, :], ALU.mult), r=[("yl", cc)], w=[("sq", cc)])

        def lru_final(G):
            g2 = G % 2
            for cc in range(4):
                P.add("pe", MM(banks[7], ones, L_sq[cc], cc == 0, cc == 3), r=["cb", ("sq", cc)], w=[("pb", 7)])
            P.add("act", ACT(rl, banks[7], AF.Ln, scale=1.0 / 512, bias=epsb[:, 0:1]), r=[("pb", 7), "epsb"], w=["rl"])
            P.add("act", ACT(rl, rl, AF.Exp, scale=-0.5), r=["rl"], w=["rl"])
            for cc in range(4):
                P.add("dve", STT(yn_o[g2][:, cc, :], yl[:, cc, :], pv[:, 48 + cc:49 + cc], rl, ALU.mult, ALU.mult),
                      r=[("yl", cc), "rl", "pv"], w=[("yn_o", g2, cc)])
            P.add("sp", DMA(YL[G // 2, G % 2].rearrange("(i p) t -> p i t", p=128), yn_o[g2]),
                  r=[("yn_o", g2, cc) for cc in range(4)], w=[("YL", G)], dma=("st_yl", g2))

        def capture(fn_):
            P.cap = []
            fn_()
            lst = P.cap
            P.cap = None
            return lst

        def front_all(G):
            proj_K(G)
            proj_Q(G)
            proj_V(G)
            proj_gate(G)
            proj_x(G)

        def merge(lists, pace=None):
            idx = [0] * len(lists)
            while True:
                best = None
                for k, l in enumerate(lists):
                    if idx[k] < len(l):
                        fr = idx[k] / len(l) * (pace[k] if pace else 1.0)
                        if best is None or fr < best[0]:
                            best = (fr, k)
                if best is None:
                    break
                k = best[1]
                e_ = lists[k][idx[k]]
                P.add(e_[0], e_[1], r=e_[2], w=e_[3], dma=e_[4])
                idx[k] += 1

        def lru_stream(G, ccs, fin):
            for cc in ccs:
                lru_s1(G, cc)
                lru_s23(G, cc)
            if fin:
                lru_final(G)

        front_norm(0)
        for G in range(n_groups + 1):
            lists = []
            pace = []
            if G < n_groups:
                lists.append(capture(lambda: front_all(G)))
                pace.append(1.0)
            if G + 1 < n_groups:
                lists.append(capture(lambda: front_norm(G + 1)))
                pace.append(0.7)
            if G >= 1:
                lists.append(capture(lambda: lru_stream(G - 1, (0, 2), False)))
                lists.append(capture(lambda: lru_stream(G - 1, (1, 3), False)))
                pace += [0.7, 0.7]
            merge(lists, pace)
            if G >= 1:
                lru_final(G - 1)

        P.barrier()
        if stop_after == "p1":
            return finish(es)

        AR.top = base_top
        Wout = AR.alloc([8, 1024], BF16)
        p2_top = AR.top
        MK = AR.alloc([2, 8, 512], BF16)
        KA = [[AR.alloc([SEQ], BF16) for _ in range(2)] for _ in range(2)]
        VP = [AR.alloc([64, 128], BF16) for _ in range(2)]
        QS = [[AR.alloc([2, 512], BF16) for _ in range(4)] for _ in range(2)]
        QC = [[AR.alloc([2, 512], BF16) for _ in range(2)] for _ in range(2)]
        EB = [AR.alloc([1024], F32) for _ in range(2)]
        SPB = [AR.alloc([1024], BF16) for _ in range(3)]
        AB = [AR.alloc([1024], BF16) for _ in range(3)]
        CF = [AR.alloc([1024], F32) for _ in range(2)]
        YO = [AR.alloc([512], BF16) for _ in range(2)]
        P.add("sp", DMA(MK, mk_d[:, :, :, :]), w=["mk"], dma="mk")
        stg = []
        for c in range(8):
            stg.append((DMA(Wout[:, c, :], w_out[c * 128:(c + 1) * 128, :], max_dma_last_dim=4096), "wout", "wout"))
        for c in range(8):
            for hf in range(2):
                stg.append((DMA(W1s[:, c, hf * DFF:(hf + 1) * DFF], w_f1[c * 128:(c + 1) * 128, hf * DFF:(hf + 1) * DFF], max_dma_last_dim=4096), "Wstg", "stg"))
        for fc in range(NFC):
            stg.append((DMA(W2s[:, fc, :], w_f2[fc * 128:(fc + 1) * 128, :], max_dma_last_dim=4096), "Wstg", "stg"))
        for pb in range(2):
            for s_ in range(2):
                P.add("pool", MS(KA[pb][s_][0:32, :], 0.0), w=[("ka", pb, s_)])
                P.add("pool", MS(KA[pb][s_][0:1, :], -1.0), w=[("ka", pb, s_)])
        for qs in range(2):
            for k in range(4):
                P.add("pool", MS(QS[qs][k][0:32, :, :], 0.0), w=[("q", qs, k)])
            for k in range(2):
                P.add("pool", MS(QC[qs][k][0:32, :, :], 0.0), w=[("qc", qs, k)])

        pjs = [(p, j) for p in range(4) for j in range(NSLOT)]
        if stop_after == "p2small":
            pjs = [(0, 0), (0, 1), (1, 0)]
        blocks = []
        for pj, (p, j) in enumerate(pjs):
            nblk = 8 * j + 8
            for n in range(nblk):
                kb = 8 * j + 7 - n
                blocks.append(dict(p=p, j=j, n=n, kb=kb, first=(n == 0), last=(n == nblk - 1), band=(kb >= 8 * j), bi=kb - 8 * j, pj=pj))
        L = len(blocks)
        allKT = [("KT", G) for G in range(NG)]
        allV = [("Vd", G) for G in range(NG)]
        allQT = [("QT", G) for G in range(NG)]
        loaded_pairs = set()

        def load_pair(p):
            if p in loaded_pairs or p >= 4:
                return
            loaded_pairs.add(p)
            pb = p % 2
            for s_ in range(2):
                h = 2 * p + s_
                P.add("sp", DMA(KA[pb][s_][32:96, :], KT[h * 64:(h + 1) * 64, :]), r=allKT, w=[("ka", pb, s_)], dma=("ld_ka", pb, s_))
            for q8 in range(8):
                P.add("sp", DMA(VP[pb][:, q8 * 8:(q8 + 1) * 8, :], Vd[q8 * 1024:(q8 + 1) * 1024, p * 128:(p + 1) * 128].rearrange("(blk k) d -> k blk d", k=128)),
                      r=allV, w=[("vp", pb)], dma=("ld_vp", pb))

        def load_q(pj_):
            p, j = pjs[pj_]
            qs = pj_ % 2
            for k in range(2):
                P.add("sp", DMA(QC[qs][k][32:96, :, :], QT[j, k, p * 128:(p + 1) * 128, :].rearrange("(s d) t -> d s t", s=2)),
                      r=allQT, w=[("qc", qs, k)], dma=("ld_q", qs, k))
            sc = 56 + 2 * (j % 2)
            qz = QS[qs][0][0:96, :, :]
            P.add("dve", TS(qz, QC[qs][0][0:96, :, :], pv[0:96, sc:sc + 1], None, ALU.mult), r=[("qc", qs, 0), "pv"], w=[("q", qs, 0)])
            P.add("dve", STT(qz, QC[qs][1][0:96, :, :], pv[0:96, sc + 1:sc + 2], qz, ALU.mult, ALU.add), r=[("qc", qs, 1), ("q", qs, 0), "pv"], w=[("q", qs, 0)])
            for k in (1, 2, 3):
                P.add("pool", CP(QS[qs][k][0:96, :, :], qz), r=[("q", qs, 0)], w=[("q", qs, k)])
            if pj_ >= 1:
                for _ in range(2):
                    if stg:
                        f_, wk_, dk_ = stg.pop(0)
                        P.add("pool", f_, w=[wk_], dma=dk_)

        def stageA_pe(i, b):
            p, j, pj = b["p"], b["j"], b["pj"]
            if i == 0:
                load_pair(p)
                load_q(0)
            if b["n"] == 2:
                if pj + 1 < len(pjs):
                    load_q(pj + 1)
                    if j >= 1 or pjs[pj + 1][0] != p:
                        load_pair(pjs[pj + 1][0] if pjs[pj + 1][0] != p else p + 1)
            pb = p % 2
            qs = pj % 2
            kb = b["kb"]
            for s_ in range(2):
                P.add("pe", MM(banks[s_], KA[pb][s_][0:96, kb * 128:(kb + 1) * 128], QS[qs][0][0:96, s_, :], True, not b["band"]),
                      r=[("ka", pb, s_), ("q", qs, 0)], w=[("pb", s_)])
                if b["band"]:
                    P.add("pe", MM(banks[s_], ident, MK[:, j % 2, b["bi"], :], False, True), r=["cb", "mk"], w=[("pb", s_)])

        def stageA_act(i, b):
            P.add("act", ACT(EB[i % 2], psum_h[:, 0:1024], AF.Exp), r=[("pb", 0), ("pb", 1)], w=[("eb", i % 2)])
            P.add("act", ACT(SPB[i % 3], EB[i % 2], AF.Ln, bias=1.0), r=[("eb", i % 2)], w=[("spb", i % 3)])

        def stageB(i, b):
            p, j, n, pj = b["p"], b["j"], b["n"], b["pj"]
            pb = p % 2
            qs = pj % 2
            kb = b["kb"]
            qa = 1 + n % 3
            qn = 1 + (n + 1) % 3
            cf = CF[pj % 2]
            if not b["last"]:
                for s_ in range(2):
                    gb = 2 + s_
                    P.add("pe", MM(banks[gb], ones, SPB[i % 3][:, s_ * 512:(s_ + 1) * 512], True, True), r=["cb", ("spb", i % 3)], w=[("pb", gb)])
                for s_ in range(2):
                    gb = 2 + s_
                    cfs = cf[0:1, s_ * 512:(s_ + 1) * 512]
                    if n == 0:
                        P.add("dve", CP(cfs, banks[gb][0:1, :]), r=[("pb", gb)], w=[("cf", pj % 2, s_)])
                    else:
                        P.add("dve", TT(cfs, cfs, banks[gb][0:1, :], ALU.add), r=[("pb", gb), ("cf", pj % 2, s_)], w=[("cf", pj % 2, s_)])
                    P.add("dve", CP(QS[qs][qn][0:1, s_, :], cfs), r=[("cf", pj % 2, s_)], w=[("q", qs, qn)])
            for s_ in range(2):
                bb = 4 + s_
                P.add("pe", MM(banks[bb], KA[pb][s_][0:96, kb * 128:(kb + 1) * 128], QS[qs][qa][0:96, s_, :], True, False),
                      r=[("ka", pb, s_), ("q", qs, qa)], w=[("pb", bb)])
                P.add("pe", MM(banks[bb], negtri, SPB[i % 3][:, s_ * 512:(s_ + 1) * 512], False, not b["band"]), r=["cb", ("spb", i % 3)], w=[("pb", bb)])
                if b["band"]:
                    P.add("pe", MM(banks[bb], ident, MK[:, j % 2, b["bi"], :], False, True), r=["cb", "mk"], w=[("pb", bb)])
            P.add("act", ACT(AB[i % 3], psum_h[:, 4 * 512:6 * 512], AF.Exp), r=[("pb", 4), ("pb", 5)], w=[("ab", i % 3)])

        def stageC(i, b):
            p, j, pj = b["p"], b["j"], b["pj"]
            pb = p % 2
            kb = b["kb"]
            for s_ in range(2):
                P.add("pe", MM(banks[6 + s_], VP[pb][:, kb, :], AB[i % 3][:, s_ * 512:(s_ + 1) * 512], b["first"], b["last"]),
                      r=[("vp", pb), ("ab", i % 3)], w=[("pb", 6 + s_)])
            if b["last"]:
                yo = pj % 2
                for s_ in range(2):
                    P.add("dve", CP(YO[yo][s_ * 64:(s_ + 1) * 64, :], banks[6 + s_][s_ * 64:(s_ + 1) * 64, :]), r=[("pb", 6 + s_)], w=[("yo", yo)])
                P.add("sp", DMA(YA[p * 128:(p + 1) * 128, j * 512:(j + 1) * 512], YO[yo]),
                      r=[("yo", yo)], w=[("YA", 2 * p, j), ("YA", 2 * p + 1, j)], dma=("st_ya", yo))

        stageA_pe(0, blocks[0])
        for it in range(L + 2):
            if it == L:
                while stg:
                    f_, wk_, dk_ = stg.pop(0)
                    P.add("pool", f_, w=[wk_], dma=dk_)
            if it < L:
                stageA_act(it, blocks[it])
            if 1 <= it <= L:
                stageB(it - 1, blocks[it - 1])
            if it >= 2:
                stageC(it - 2, blocks[it - 2])
            if it + 1 < L:
                stageA_pe(it + 1, blocks[it + 1])

        P.barrier()
        if stop_after in ("p2", "p2small"):
            return finish(es)

        AR.top = p2_top
        W1 = AR.alloc([8, 2 * DFF], BF16)
        W2 = AR.alloc([NFC, 1024], BF16)
        wloads = []
        for c in range(8):
            wloads.append((DMA(W1[:, c, :], W1s[:, c, :]), "w1", ["Wstg"]))
        for f4 in range(0, NFC, 4):
            f5 = min(NFC, f4 + 4)
            wloads.append((DMA(W2[:, f4:f5, :], W2s[:, f4:f5, :]), "w2", ["Wstg"]))
        p3_top = AR.top
        ylb = AR.alloc([4, 512], BF16)
        ylc = [AR.alloc([4, 512], BF16) for _ in range(2)]
        yab = AR.alloc([4, 512], BF16)
        yan = AR.alloc([4, 512], BF16)
        sq2 = [AR.alloc([512], BF16) for _ in range(2)]
        rl2 = AR.alloc([512], F32)
        xt2 = [AR.alloc([1024], F32) for _ in range(3)]
        xc2 = 0
        allYL = [("YL", G) for G in range(NG)]
        for j in range(NSLOT):
            for k in range(2):
                P.add("sp", DMA(ylc[k], YL[j, k].rearrange("(i p) t -> p i t", p=128)), r=allYL, w=[("ylc", k)], dma=("ld_ylc", k))
            sc = 56 + 2 * (j % 2)
            P.add("dve", TS(ylb, ylc[0], pv[:, sc:sc + 1], None, ALU.mult), r=[("ylc", 0), "pv"], w=["ylb"])
            P.add("dve", STT(ylb, ylc[1], pv[:, sc + 1:sc + 2], ylb, ALU.mult, ALU.add), r=[("ylc", 1), "ylb", "pv"], w=["ylb"])
            P.add("sp", DMA(yab, YA[:, j * 512:(j + 1) * 512].rearrange("(i p) t -> p i t", p=128)),
                  r=[("YA", h, j) for h in range(8)], w=["yab", ("p25slot", j)], dma="ld_yab")
            nw = 3 if j < 4 else len(wloads)
            for fn_, key_, rk_ in wloads[:nw]:
                P.add("sp", fn_, r=[("p25slot", j)] + rk_, w=[key_], dma=key_)
            wloads = wloads[nw:]
            for p in range(4):
                P.add("pool", TT(sq2[p % 2], yab[:, p, :], yab[:, p, :], ALU.mult), r=["yab"], w=[("sq2", p % 2)])
                P.add("pe", MM(banks[7], ones, sq2[p % 2], p == 0, p == 3), r=["cb", ("sq2", p % 2)], w=[("pb", 7)])
            P.add("act", ACT(rl2, banks[7], AF.Ln, scale=1.0 / 512, bias=epsb[:, 0:1]), r=[("pb", 7), "epsb"], w=["rl2"])
            P.add("act", ACT(rl2, rl2, AF.Exp, scale=-0.5), r=["rl2"], w=["rl2"])
            for p in range(4):
                P.add("dve", STT(yan[:, p, :], yab[:, p, :], pv[:, 52 + p:53 + p], rl2, ALU.mult, ALU.mult), r=["yab", "rl2", "pv"], w=[("yan", p)])
            for tt in range(4):
                s = xc2 % 3
                xc2 += 1
                row0 = j * 512 + tt * 128
                P.add("sp", DMA(xt2[s], xq[row0:row0 + 128, :]), w=[("xt2", s)], dma=("xt2", s))
                for hf in range(2):
                    bk = (tt * 2 + hf) % 4
                    for k in range(8):
                        src = ylb[:, k, tt * 128:(tt + 1) * 128] if k < 4 else yan[:, k - 4, tt * 128:(tt + 1) * 128]
                        rk = "ylb" if k < 4 else ("yan", k - 4)
                        P.add("pe", MM(banks[bk], src, Wout[:, k, hf * 512:(hf + 1) * 512], k == 0, k == 7), r=["wout", rk], w=[("pb", bk)])
                    P.add("dve", TT(xt2[s][:, hf * 512:(hf + 1) * 512], xt2[s][:, hf * 512:(hf + 1) * 512], banks[bk], ALU.add),
                          r=[("pb", bk), ("xt2", s)], w=[("xt2", s)])
                P.add("sp", DMA(X1[row0:row0 + 128, :], xt2[s]), r=[("xt2", s)], w=[("X1", j * 4 + tt)], dma=("st_x1", s))

        P.barrier()
        if stop_after == "p25":
            return finish(es)

        AR.top = base_top
        x1t_b = AR.alloc([NT3, 1024], F32)
        hb3_b = AR.alloc([NT3, 1024], BF16)
        hT3_b = AR.alloc([8, TG], BF16)
        assert AR.top <= p2_top
        AR.top = p3_top
        nfin = AR.alloc([1024], F32)
        x1t = [AR.alloc([NT3, 1024], F32), x1t_b]
        hb3 = [AR.alloc([NT3, 1024], BF16), hb3_b]
        hT3 = [AR.alloc([8, TG], BF16), hT3_b]
        aT = AR.alloc([NFC, TG], BF16)
        sgb = [AR.alloc([TG], F32) for _ in range(2)]
        ot = [AR.alloc([1024], F32) for _ in range(2)]
        ss3 = AR.alloc([2, NT3], F32)
        rs3 = AR.alloc([2, NT3], F32)
        ss4 = AR.alloc([2, NT3], F32)
        rs4 = AR.alloc([2, NT3], F32)
        P.add("sp", DMA(nfin, nfin_d[:, :]), w=["nfin"], dma="nfin")
        st3 = {"oc": 0}
        n_it = NSLOT * 512 // TG

        def p3_pre(it3):
            j2 = it3 % 2
            hbk = [("hb3", j2, tt) for tt in range(NT3)]
            for tt in range(NT3):
                tile_id = it3 * NT3 + tt
                row0 = tile_id * 128
                P.add("sp", DMA(x1t[j2][:, tt, :], X1[row0:row0 + 128, :]), r=[("X1", tile_id)], w=[("x1t", j2, tt)], dma=("ld_x1", j2, tt))
                norm_tile(x1t[j2][:, tt, :], ("x1t", j2, tt), ss3[:, j2, tt:tt + 1], ("ss3", j2, tt), rs3[:, j2, tt:tt + 1], ("rs3", j2, tt), hb3[j2][:, tt, :], hbk[tt])
            transpose_group(hb3[j2], hbk, hT3[j2], ("hT3", j2), 8, NT3)

        def p3_main(it3):
            j2 = it3 % 2
            hTk = [(("hT3", j2), c) for c in range(8)]
            for fc in range(NFC):
                gb = 2 + fc % 2
                ub = 4 + fc % 2
                for c in range(8):
                    P.add("pe", MM(banks[gb][:, 0:TG], W1[:, c, fc * 128:(fc + 1) * 128], hT3[j2][:, c, :], c == 0, c == 7), r=["w1", hTk[c]], w=[("pb", gb)])
                for c in range(8):
                    P.add("pe", MM(banks[ub][:, 0:TG], W1[:, c, DFF + fc * 128:DFF + (fc + 1) * 128], hT3[j2][:, c, :], c == 0, c == 7), r=["w1", hTk[c]], w=[("pb", ub)])
                P.add("act", ACT(sgb[fc % 2], banks[gb][:, 0:TG], AF.Silu), r=[("pb", gb)], w=[("sgb", fc % 2)])
                P.add("dve", TT(aT[:, fc, :], sgb[fc % 2], banks[ub][:, 0:TG], ALU.mult), r=[("sgb", fc % 2), ("pb", ub)], w=[("aT", fc)])
            for tt in range(NT3):
                xk = ("x1t", j2, tt)
                for hf in range(2):
                    ob = 6 + (tt * 2 + hf) % 2
                    for fc in range(NFC):
                        P.add("pe", MM(banks[ob], aT[:, fc, tt * 128:(tt + 1) * 128], W2[:, fc, hf * 512:(hf + 1) * 512], fc == 0, fc == NFC - 1), r=["w2", ("aT", fc)], w=[("pb", ob)])
                    P.add("dve", TT(x1t[j2][:, tt, hf * 512:(hf + 1) * 512], x1t[j2][:, tt, hf * 512:(hf + 1) * 512], banks[ob], ALU.add),
                          r=[("pb", ob), xk], w=[xk])
                o = st3["oc"] % 2
                st3["oc"] += 1
                sq_accum(x1t[j2][:, tt, :], ss4[:, j2, tt:tt + 1], [xk], [("ss4", j2, tt)])
                P.add("pool", TS(rs4[:, j2, tt:tt + 1], ss4[:, j2, tt:tt + 1], 1.0 / D, EPS, ALU.mult, ALU.add), r=[("ss4", j2, tt)], w=[("rs4", j2, tt)])
                P.add("pool", TT(rs4[:, j2, tt:tt + 1], rs4[:, j2, tt:tt + 1], cneg[:, 0:1], ALU.pow), r=[("rs4", j2, tt), "cneg"], w=[("rs4", j2, tt)])
                P.add("dve", STT(ot[o], x1t[j2][:, tt, :], rs4[:, j2, tt:tt + 1], nfin, ALU.mult, ALU.mult), r=[xk, ("rs4", j2, tt), "nfin"], w=[("ot", o)])
                row0 = (it3 * NT3 + tt) * 128
                P.add("sp", DMA(out_d[row0:row0 + 128, :], ot[o]), r=[("ot", o)], w=[("out", it3, tt)], dma=("st_out", o))

        p3_pre(0)
        for it3 in range(n_it):
            lists = [capture(lambda: p3_main(it3))]
            if it3 + 1 < n_it:
                lists.append(capture(lambda: p3_pre(it3 + 1)))
            merge(lists)

        P.barrier()
        return finish(es)


def _slot_groups(par):
    return [2 * j + (par ^ (j & 1)) for j in range(NSLOT)]


def _consts():
    bf = ml_dtypes.bfloat16
    cb = np.zeros((128, 3, 128), np.float32)
    cb[:, 0, :] = np.eye(128)
    jj = np.arange(128)[:, None]
    s_ = np.arange(128)[None, :]
    cb[:, 1, :] = np.where(jj >= s_, -1.0, 0.0)
    cb[:, 2, :] = 1.0
    s = np.arange(128)[:, None, None]
    i = np.arange(8)[None, :, None]
    t = np.arange(512)[None, None, :]
    kpos = 128 * i + s
    m_min = np.where(kpos < t, 0.0, NEG)
    m_max = np.where(kpos < 512 + t, 0.0, NEG)
    return cb.astype(bf), m_min.astype(bf), m_max.astype(bf)


def _make_in_maps(inputs):
    f = lambda a: np.ascontiguousarray(np.asarray(a, dtype=np.float32))
    x = f(inputs["x"])
    w_in = f(inputs["w_in"][0])
    w_out = f(inputs["w_out"][0])
    w_f1 = f(inputs["w_ffn_in"][0])
    w_f2 = f(inputs["w_ffn_out"][0])
    pv = np.zeros((128, NV), np.float32)
    pv[:, 0:8] = f(inputs["norm_mix"][0]).reshape(8, 128).T
    pv[:, 8:16] = f(inputs["norm_ffn"][0]).reshape(8, 128).T
    pv[:, 16:20] = f(inputs["conv_b"][0]).reshape(4, 128).T
    cw = f(inputs["conv_w"][0])
    for cc in range(4):
        pv[:, 20 + cc * 4:24 + cc * 4] = cw[:, cc * 128:(cc + 1) * 128].T
    pv[:, 36:40] = f(inputs["b_rg"][0]).reshape(4, 128).T
    pv[:, 40:44] = f(inputs["b_ig"][0]).reshape(4, 128).T
    pv[:, 44:48] = f(inputs["lru_lambda"][0]).reshape(4, 128).T
    pv[:, 48:52] = f(inputs["norm_lru_out"][0]).reshape(4, 128).T
    pv[:, 52:56] = f(inputs["norm_att_out"][0]).reshape(4, 128).T
    wbd = np.zeros((128, 8, 128), np.float32)
    wrg = f(inputs["w_rg"][0])
    wig = f(inputs["w_ig"][0])
    for cc in range(4):
        for hb_ in range(2):
            blk = 2 * cc + hb_
            wbd[hb_ * 64:(hb_ + 1) * 64, cc, hb_ * 64:(hb_ + 1) * 64] = wrg[blk]
            wbd[hb_ * 64:(hb_ + 1) * 64, 4 + cc, hb_ * 64:(hb_ + 1) * 64] = wig[blk]
    nfin = np.ascontiguousarray(np.broadcast_to(f(inputs["norm_final"])[None, :], (128, D)))
    cb, m_min, m_max = _consts()
    maps = []
    for c in range(N_CORES):
        b, par = c // 2, c % 2
        groups = _slot_groups(par)
        xq = np.concatenate([x[b, g * 512:(g + 1) * 512] for g in groups], axis=0)
        mk = np.stack([m_min, m_max] if par == 0 else [m_max, m_min], axis=1)
        pvc = pv.copy()
        pvc[:, 56:60] = np.array([1 - par, par, par, 1 - par], np.float32)[None, :]
        maps.append({"xs": np.ascontiguousarray(x[b]), "xq": np.ascontiguousarray(xq), "w_in": w_in, "w_out": w_out,
                     "w_f1": w_f1, "w_f2": w_f2, "wbd": wbd, "pv": pvc, "nfin": nfin, "cb": cb,
                     "mk": np.ascontiguousarray(mk)})
    return maps


_NC_CACHE = {}


def kernel(**inputs):
    maps = _make_in_maps(inputs)
    if "nc" not in _NC_CACHE:
        _NC_CACHE["nc"] = build_program()
    nc = _NC_CACHE["nc"]
    res = run_bass_kernel_spmd(nc, maps, core_ids=list(range(N_CORES)))
    out = np.zeros((4, SEQ, D), np.float32)
    for c in range(N_CORES):
        b, par = c // 2, c % 2
        o = np.asarray(res.results[c]["out"], dtype=np.float32)
        for j, g in enumerate(_slot_groups(par)):
            out[b, g * 512:(g + 1) * 512] = o[j * 512:(j + 1) * 512]
    return out
```
